# Optimizing a Trainium2 kernel written in Bass

```python
import jax, jax.numpy as jnp
from jax import lax
import numpy as np

D_MODEL = 1024
BATCH = 8
SEQ = 4096
DEPTH = 4

HEAD_DIM = 64
N_HEADS_GROUP = 4
GROUP_WIDTH = N_HEADS_GROUP * HEAD_DIM
N_MIXERS = 4
D_MIX = N_MIXERS * GROUP_WIDTH
D_PLE = 256

RWKV_DECAY_RANK = 32
RWKV_ICL_RANK = 32
RWKV_GATE_RANK = 64
RWKV_GN_EPS = 64e-5
RWKV_WIDTHS = (GROUP_WIDTH, GROUP_WIDTH, GROUP_WIDTH, RWKV_DECAY_RANK, RWKV_ICL_RANK, RWKV_GATE_RANK)
D_IN_A = 3 * GROUP_WIDTH + RWKV_DECAY_RANK + RWKV_ICL_RANK + RWKV_GATE_RANK

GLA_KEY_DIM = 32
GLA_QK_WIDTH = N_HEADS_GROUP * GLA_KEY_DIM
GLA_GATE_RANK = 16
GLA_TAU = 16.0
GLA_CHUNK = 64
GLA_WIDTHS = (GLA_QK_WIDTH, GLA_QK_WIDTH, GROUP_WIDTH, GLA_GATE_RANK, GROUP_WIDTH)
D_IN_B = 2 * GLA_QK_WIDTH + 2 * GROUP_WIDTH + GLA_GATE_RANK

MLSTM_CONV = 4
MLSTM_CHUNK = 64
MLSTM_WIDTHS = (GROUP_WIDTH, GROUP_WIDTH, GROUP_WIDTH, GROUP_WIDTH, N_HEADS_GROUP, N_HEADS_GROUP)
D_IN_C = 4 * GROUP_WIDTH + 2 * N_HEADS_GROUP

MLA_Q_RANK = 256
MLA_KV_RANK = 128
MLA_NOPE = 64
MLA_ROPE = 32
MLA_V = HEAD_DIM
MLA_WIDTHS = (MLA_Q_RANK, MLA_KV_RANK, MLA_ROPE)
D_IN_D = MLA_Q_RANK + MLA_KV_RANK + MLA_ROPE
ROPE_THETA = 10000.0
ATTN_BLOCK = 128

D_IN = D_IN_A + D_IN_B + D_IN_C + D_IN_D

N_EXPERT_GROUPS = 4
EXPERTS_PER_GROUP = 8
N_EXPERTS = N_EXPERT_GROUPS * EXPERTS_PER_GROUP
TOP_K = 2
D_EXPERT = 512
MOE_BLOCK = 256

DN_ALPHA = (2.0 * DEPTH) ** 0.25
DN_BETA = (8.0 * DEPTH) ** -0.25
LN_EPS = 1e-5
NORM_EPS = 1e-6

kernel_name = 'hybrid_parallel_groups_hmoe_deepnorm'


def split_last(t, widths):
    return jnp.split(t, np.cumsum(widths)[:-1].tolist(), axis=-1)


def heads(t, d):
    return t.reshape(t.shape[:-1] + (t.shape[-1] // d, d))


def layer_norm(x, g, b, eps):
    xf = x.astype(jnp.float32)
    xc = xf - jnp.mean(xf, axis=-1, keepdims=True)
    y = xc * lax.rsqrt(jnp.mean(xc * xc, axis=-1, keepdims=True) + eps) * g
    if b is not None:
        y = y + b
    return y.astype(x.dtype)


def rms_norm(x, g, eps):
    xf = x.astype(jnp.float32)
    return (xf * lax.rsqrt(jnp.mean(xf * xf, axis=-1, keepdims=True) + eps) * g).astype(x.dtype)


def token_shift(x):
    return jnp.pad(x, ((0, 0), (1, 0), (0, 0)))[:, :-1]


def causal_depthwise_conv(x, w, b):
    width, ch = w.shape
    y = lax.conv_general_dilated(x, w[:, None, :].astype(x.dtype), window_strides=(1,),
                                 padding=((width - 1, 0),), dimension_numbers=('NWC', 'WIO', 'NWC'),
                                 feature_group_count=ch)
    return y + b


def rope_cos_sin(positions):
    inv_freq = ROPE_THETA ** (-jnp.arange(0, MLA_ROPE, 2, dtype=jnp.float32) / MLA_ROPE)
    ang = positions.astype(jnp.float32)[..., None] * inv_freq
    return jnp.cos(ang), jnp.sin(ang)


def apply_rope(x, cos, sin):
    x1, x2 = jnp.split(x, 2, axis=-1)
    return jnp.concatenate([x1 * cos - x2 * sin, x1 * sin + x2 * cos], axis=-1).astype(x.dtype)


def rwkv7_scan(r, w, k, v, a, b):
    bsz, _, nh, n = r.shape

    def step(state, inp):
        r_t, w_t, k_t, v_t, a_t, b_t = inp
        sa = jnp.einsum('bhvk,bhk->bhv', state, a_t)
        state = state * w_t[:, :, None, :] + sa[..., None] * b_t[:, :, None, :] + v_t[..., None] * k_t[:, :, None, :]
        return state, jnp.einsum('bhvk,bhk->bhv', state, r_t)

    seq = tuple(jnp.moveaxis(t, 1, 0) for t in (r, w, k, v, a, b))
    _, y = lax.scan(step, jnp.zeros((bsz, nh, n, n), jnp.float32), seq)
    return jnp.moveaxis(y, 0, 1)


def rwkv7_group(u, mu, w0, w_up, a0, a_up, g_up, k_k, k_a, r_k, gn_g, gn_b):
    bsz, s, _ = u.shape
    u = u + (token_shift(u) - u) * mu
    r, k, v, wd, ad, gd = split_last(u, RWKV_WIDTHS)
    log_neg_logw = -jax.nn.softplus(-(w0 + jnp.tanh(wd) @ w_up)) - 0.5
    decay = jnp.exp(-jnp.exp(log_neg_logw.astype(jnp.float32)))
    a = jax.nn.sigmoid(a0 + ad @ a_up)
    g = jax.nn.sigmoid(gd) @ g_up
    kk = heads(k * k_k, HEAD_DIM).astype(jnp.float32)
    kk = kk / jnp.maximum(jnp.sqrt(jnp.sum(kk * kk, axis=-1, keepdims=True)), 1e-12)
    k = k * (1.0 + (a - 1.0) * k_a)
    rh, kh, vh, ah = (heads(t, HEAD_DIM).astype(jnp.float32) for t in (r, k, v, a))
    y = rwkv7_scan(rh, heads(decay, HEAD_DIM), kh, vh, -kk, kk * ah)
    y = layer_norm(y, gn_g.reshape(N_HEADS_GROUP, HEAD_DIM), gn_b.reshape(N_HEADS_GROUP, HEAD_DIM), RWKV_GN_EPS)
    y = y + jnp.sum(rh * kh * r_k, axis=-1, keepdims=True) * vh
    return (y.reshape(bsz, s, GROUP_WIDTH) * g).astype(u.dtype)


def gla_chunked(q, k, v, log_a):
    bsz, s, nh, dk = q.shape
    dv = v.shape[-1]
    n = s // GLA_CHUNK

    def chunks(t):
        return jnp.moveaxis(t.astype(jnp.float32).reshape(bsz, n, GLA_CHUNK, nh, t.shape[-1]), 1, 0)

    causal = jnp.tril(jnp.ones((GLA_CHUNK, GLA_CHUNK), dtype=bool))[None, :, :, None, None]

    def step(state, inp):
        qc, kc, vc, lac = inp
        b = jnp.cumsum(lac, axis=1)
        rel = jnp.exp(jnp.where(causal, b[:, :, None] - b[:, None, :], -jnp.inf))
        scores = jnp.einsum('bihd,bjhd,bijhd->bhij', qc, kc, rel)
        o = jnp.einsum('bhij,bjhv->bihv', scores, vc) + jnp.einsum('bihd,bhdv->bihv', qc * jnp.exp(b), state)
        b_end = b[:, -1]
        state = state * jnp.exp(b_end)[..., None] + jnp.einsum('bjhd,bjhv->bhdv', kc * jnp.exp(b_end[:, None] - b), vc)
        return state, o

    _, o = lax.scan(step, jnp.zeros((bsz, nh, dk, dv), jnp.float32), (chunks(q), chunks(k), chunks(v), chunks(log_a)))
    return jnp.moveaxis(o, 0, 1).reshape(bsz, s, nh, dv)


def gla_group(u, alpha_up, alpha_b, norm_g):
    bsz, s, _ = u.shape
    q, k, v, ad, gate = split_last(u, GLA_WIDTHS)
    log_a = jax.nn.log_sigmoid((ad @ alpha_up + alpha_b).astype(jnp.float32)) / GLA_TAU
    o = gla_chunked(heads(q, GLA_KEY_DIM) * GLA_KEY_DIM ** -0.5, heads(k, GLA_KEY_DIM),
                    heads(v, HEAD_DIM), heads(log_a, GLA_KEY_DIM))
    o = rms_norm(o, norm_g.reshape(N_HEADS_GROUP, HEAD_DIM), NORM_EPS)
    return (o.reshape(bsz, s, GROUP_WIDTH) * jax.nn.silu(gate)).astype(u.dtype)


def mlstm_chunked(q, k, v, log_i, log_f):
    bsz, s, nh, d = q.shape
    c = MLSTM_CHUNK
    n = s // c

    def chunks(t):
        return jnp.moveaxis(t.reshape((bsz, n, c) + t.shape[2:]), 1, 0)

    causal = jnp.tril(jnp.ones((c, c), dtype=bool))

    def step(carry, inp):
        mem, nrm, m = carry
        qc, kc, vc, li, lf = inp
        li = jnp.swapaxes(li, 1, 2)
        f_cum = jnp.cumsum(jnp.swapaxes(lf, 1, 2), axis=-1)
        log_w = jnp.where(causal, f_cum[..., :, None] - f_cum[..., None, :] + li[..., None, :], -jnp.inf)
        log_carry = f_cum + m[..., None]
        m_row = jnp.maximum(jnp.max(log_w, axis=-1), log_carry)
        sc = jnp.einsum('bihd,bjhd->bhij', qc, kc) * jnp.exp(log_w - m_row[..., None])
        w_c = jnp.exp(log_carry - m_row)
        num = jnp.einsum('bhij,bjhd->bhid', sc, vc) + w_c[..., None] * jnp.einsum('bihd,bhde->bhie', qc, mem)
        den = jnp.sum(sc, axis=-1) + w_c * jnp.einsum('bihd,bhd->bhi', qc, nrm)
        h = num / jnp.maximum(jnp.abs(den), jnp.exp(-m_row))[..., None]
        f_end = f_cum[..., -1]
        log_kv = f_end[..., None] - f_cum + li
        m_new = jnp.maximum(f_end + m, jnp.max(log_kv, axis=-1))
        kw = jnp.exp(log_kv - m_new[..., None])
        carry_decay = jnp.exp(f_end + m - m_new)
        mem = carry_decay[..., None, None] * mem + jnp.einsum('bhj,bjhd,bjhe->bhde', kw, kc, vc)
        nrm = carry_decay[..., None] * nrm + jnp.einsum('bhj,bjhd->bhd', kw, kc)
        return (mem, nrm, m_new), jnp.swapaxes(h, 1, 2)

    init = (jnp.zeros((bsz, nh, d, d), jnp.float32), jnp.zeros((bsz, nh, d), jnp.float32),
            jnp.zeros((bsz, nh), jnp.float32))
    _, h = lax.scan(step, init, tuple(chunks(t) for t in (q, k, v, log_i, log_f)))
    return jnp.moveaxis(h, 0, 1).reshape(bsz, s, nh, d)


def mlstm_group(u, conv_w, conv_b, i_b, f_b, norm_g):
    bsz, s, _ = u.shape
    q, k, v, o, ig, fg = split_last(u, MLSTM_WIDTHS)
    qk = jax.nn.silu(causal_depthwise_conv(jnp.concatenate([q, k], axis=-1), conv_w, conv_b))
    q, k = jnp.split(qk, 2, axis=-1)
    log_i = (ig + i_b).astype(jnp.float32)
    log_f = jax.nn.log_sigmoid((fg + f_b).astype(jnp.float32))
    h = mlstm_chunked(heads(q, HEAD_DIM).astype(jnp.float32),
                      heads(k, HEAD_DIM).astype(jnp.float32) * HEAD_DIM ** -0.5,
                      heads(v, HEAD_DIM).astype(jnp.float32), log_i, log_f)
    h = layer_norm(h, norm_g.reshape(N_HEADS_GROUP, HEAD_DIM), None, LN_EPS)
    return (h.reshape(bsz, s, GROUP_WIDTH) * jax.nn.sigmoid(o)).astype(u.dtype)


def mla_group(u, cos, sin, q_norm_g, w_uq, kv_norm_g, w_ukv):
    bsz, s, _ = u.shape
    cq, ckv, kr = split_last(u, MLA_WIDTHS)
    q = heads(rms_norm(cq, q_norm_g, NORM_EPS) @ w_uq, MLA_NOPE + MLA_ROPE)
    kv = heads(rms_norm(ckv, kv_norm_g, NORM_EPS) @ w_ukv, MLA_NOPE + MLA_V)
    q_nope, q_rope = jnp.split(q, [MLA_NOPE], axis=-1)
    k_nope, v = jnp.split(kv, [MLA_NOPE], axis=-1)
    q = jnp.concatenate([q_nope, apply_rope(q_rope, cos[:, :, None], sin[:, :, None])], axis=-1)
    k_rope = apply_rope(kr, cos, sin)[:, :, None]
    k = jnp.concatenate([k_nope, jnp.broadcast_to(k_rope, (bsz, s, N_HEADS_GROUP, MLA_ROPE))], axis=-1)
    scale = (MLA_NOPE + MLA_ROPE) ** -0.5
    outs = []
    for start in range(0, s, ATTN_BLOCK):
        end = start + ATTN_BLOCK
        sc = jnp.einsum('bqhd,bkhd->bhqk', q[:, start:end], k[:, :end]).astype(jnp.float32) * scale
        mask = jnp.arange(end)[None, :] <= jnp.arange(start, end)[:, None]
        probs = jax.nn.softmax(jnp.where(mask, sc, -jnp.inf), axis=-1).astype(v.dtype)
        outs.append(jnp.einsum('bhqk,bkhd->bqhd', probs, v[:, :end]))
    return jnp.concatenate(outs, axis=1).reshape(bsz, s, GROUP_WIDTH)


def hier_moe(x, w_rg, b_rg, w_re, b_re, w_gate, w_up, w_down):
    bsz, s, d = x.shape
    t_tok = bsz * s
    n_assign = t_tok * TOP_K
    xt = x.reshape(t_tok, d)
    group_prob = jax.nn.softmax((xt @ w_rg + b_rg).astype(jnp.float32), axis=-1)
    group_p, group_idx = lax.top_k(group_prob, 1)
    e_logits = (xt @ w_re + b_re).astype(jnp.float32).reshape(t_tok, N_EXPERT_GROUPS, EXPERTS_PER_GROUP)
    e_logits = jnp.take_along_axis(e_logits, group_idx[:, :, None], axis=1)[:, 0]
    expert_p, local_idx = lax.top_k(jax.nn.softmax(e_logits, axis=-1), TOP_K)
    gate = group_p * expert_p / jnp.sum(expert_p, axis=-1, keepdims=True)
    expert_idx = group_idx * EXPERTS_PER_GROUP + local_idx
    flat_e = expert_idx.reshape(n_assign)
    flat_t = jnp.repeat(jnp.arange(t_tok, dtype=jnp.int32), TOP_K)
    flat_g = gate.reshape(n_assign)
    order = jnp.argsort(flat_e)
    sorted_e = flat_e[order]
    counts = jnp.bincount(flat_e, length=N_EXPERTS)
    starts = jnp.cumsum(counts) - counts
    padded = (counts + MOE_BLOCK - 1) // MOE_BLOCK * MOE_BLOCK
    pad_ends = jnp.cumsum(padded)
    dest = pad_ends[sorted_e] - padded[sorted_e] + jnp.arange(n_assign) - starts[sorted_e]
    n_rows = -(-(n_assign + N_EXPERTS * (MOE_BLOCK - 1)) // MOE_BLOCK) * MOE_BLOCK
    n_blocks = n_rows // MOE_BLOCK
    row_tok = jnp.full((n_rows,), t_tok, jnp.int32).at[dest].set(flat_t[order])
    row_gate = jnp.zeros((n_rows,), x.dtype).at[dest].set(flat_g[order].astype(x.dtype))
    block_expert = jnp.minimum(jnp.searchsorted(pad_ends, jnp.arange(n_blocks) * MOE_BLOCK, side='right'), N_EXPERTS - 1)
    x_rows = jnp.concatenate([xt, jnp.zeros((1, d), xt.dtype)], axis=0)[row_tok].reshape(n_blocks, MOE_BLOCK, d)

    def expert_block(args):
        xb, e = args
        return (jax.nn.silu(xb @ w_gate[e]) * (xb @ w_up[e])) @ w_down[e]

    y_rows = lax.map(expert_block, (x_rows, block_expert)).reshape(n_rows, d)
    y = jax.ops.segment_sum(y_rows * row_gate[:, None], row_tok, num_segments=t_tok + 1)[:t_tok]
    return y.reshape(bsz, s, d)


def setup_inputs(seed: int = 0) -> dict:
    key = jax.random.key(seed)
    keys = iter(jax.random.split(key, 64))
    L = DEPTH
    f32 = jnp.float32
    D = D_MODEL

    def nrm(shape, scale):
        return jax.random.normal(next(keys), shape, f32) * scale

    x = nrm((BATCH, SEQ, D), 1.0)
    p = nrm((DEPTH, BATCH, SEQ, D_PLE), 1.0)
    positions = (jax.random.randint(next(keys), (BATCH, 1), 0, 1024, dtype=jnp.int32)
                 + jnp.arange(SEQ, dtype=jnp.int32)[None, :])
    return {
        'x': x,
        'p': p,
        'positions': positions,
        'w_in': nrm((L, D, D_IN), D ** -0.5),
        'rwkv_mu': jax.random.uniform(next(keys), (L, D_IN_A), f32),
        'rwkv_w0': nrm((L, GROUP_WIDTH), 0.5),
        'rwkv_w_up': nrm((L, RWKV_DECAY_RANK, GROUP_WIDTH), 0.5 * RWKV_DECAY_RANK ** -0.5),
        'rwkv_a0': nrm((L, GROUP_WIDTH), 0.5),
        'rwkv_a_up': nrm((L, RWKV_ICL_RANK, GROUP_WIDTH), 0.5 * RWKV_ICL_RANK ** -0.5),
        'rwkv_g_up': nrm((L, RWKV_GATE_RANK, GROUP_WIDTH), RWKV_GATE_RANK ** -0.5),
        'rwkv_k_k': 0.85 + nrm((L, GROUP_WIDTH), 0.02),
        'rwkv_k_a': 1.0 + nrm((L, GROUP_WIDTH), 0.02),
        'rwkv_r_k': nrm((L, N_HEADS_GROUP, HEAD_DIM), 0.1),
        'rwkv_gn_g': 1.0 + nrm((L, GROUP_WIDTH), 0.02),
        'rwkv_gn_b': nrm((L, GROUP_WIDTH), 0.02),
        'gla_alpha_up': nrm((L, GLA_GATE_RANK, GLA_QK_WIDTH), GLA_GATE_RANK ** -0.5),
        'gla_alpha_b': nrm((L, GLA_QK_WIDTH), 0.5),
        'gla_norm_g': 1.0 + nrm((L, GROUP_WIDTH), 0.02),
        'mlstm_conv_w': nrm((L, MLSTM_CONV, 2 * GROUP_WIDTH), MLSTM_CONV ** -0.5),
        'mlstm_conv_b': nrm((L, 2 * GROUP_WIDTH), 0.02),
        'mlstm_i_b': nrm((L, N_HEADS_GROUP), 0.1),
        'mlstm_f_b': jnp.linspace(3.0, 6.0, N_HEADS_GROUP, dtype=f32) + nrm((L, N_HEADS_GROUP), 0.1),
        'mlstm_norm_g': 1.0 + nrm((L, GROUP_WIDTH), 0.02),
        'mla_q_norm_g': 1.0 + nrm((L, MLA_Q_RANK), 0.02),
        'mla_w_uq': nrm((L, MLA_Q_RANK, N_HEADS_GROUP * (MLA_NOPE + MLA_ROPE)), MLA_Q_RANK ** -0.5),
        'mla_kv_norm_g': 1.0 + nrm((L, MLA_KV_RANK), 0.02),
        'mla_w_ukv': nrm((L, MLA_KV_RANK, N_HEADS_GROUP * (MLA_NOPE + MLA_V)), MLA_KV_RANK ** -0.5),
        'w_out': nrm((L, D_MIX, D), DN_BETA * D_MIX ** -0.5),
        'ln1_g': 1.0 + nrm((L, D), 0.02),
        'ln1_b': nrm((L, D), 0.02),
        'moe_w_rg': nrm((L, D, N_EXPERT_GROUPS), D ** -0.5),
        'moe_b_rg': nrm((L, N_EXPERT_GROUPS), 0.01),
        'moe_w_re': nrm((L, D, N_EXPERTS), D ** -0.5),
        'moe_b_re': nrm((L, N_EXPERTS), 0.01),
        'moe_w_gate': nrm((L, N_EXPERTS, D, D_EXPERT), D ** -0.5),
        'moe_w_up': nrm((L, N_EXPERTS, D, D_EXPERT), DN_BETA * D ** -0.5),
        'moe_w_down': nrm((L, N_EXPERTS, D_EXPERT, D), DN_BETA * D_EXPERT ** -0.5),
        'ple_w_gate': nrm((L, D, D), D ** -0.5),
        'ple_b_gate': nrm((L, D), 0.02),
        'ple_w': nrm((L, D_PLE, D), DN_BETA * D_PLE ** -0.5),
        'ln2_g': 1.0 + nrm((L, D), 0.02),
        'ln2_b': nrm((L, D), 0.02),
    }


def reference(x, p, positions, w_in, rwkv_mu, rwkv_w0, rwkv_w_up, rwkv_a0, rwkv_a_up, rwkv_g_up,
              rwkv_k_k, rwkv_k_a, rwkv_r_k, rwkv_gn_g, rwkv_gn_b, gla_alpha_up, gla_alpha_b, gla_norm_g,
              mlstm_conv_w, mlstm_conv_b, mlstm_i_b, mlstm_f_b, mlstm_norm_g, mla_q_norm_g, mla_w_uq,
              mla_kv_norm_g, mla_w_ukv, w_out, ln1_g, ln1_b, moe_w_rg, moe_b_rg, moe_w_re, moe_b_re,
              moe_w_gate, moe_w_up, moe_w_down, ple_w_gate, ple_b_gate, ple_w, ln2_g, ln2_b):
    cos, sin = rope_cos_sin(positions)
    for i in range(DEPTH):
        u = x @ w_in[i]
        ua, ub, uc, ud = split_last(u, (D_IN_A, D_IN_B, D_IN_C, D_IN_D))
        ya = rwkv7_group(ua, rwkv_mu[i], rwkv_w0[i], rwkv_w_up[i], rwkv_a0[i], rwkv_a_up[i], rwkv_g_up[i],
                         rwkv_k_k[i], rwkv_k_a[i], rwkv_r_k[i], rwkv_gn_g[i], rwkv_gn_b[i])
        yb = gla_group(ub, gla_alpha_up[i], gla_alpha_b[i], gla_norm_g[i])
        yc = mlstm_group(uc, mlstm_conv_w[i], mlstm_conv_b[i], mlstm_i_b[i], mlstm_f_b[i], mlstm_norm_g[i])
        yd = mla_group(ud, cos, sin, mla_q_norm_g[i], mla_w_uq[i], mla_kv_norm_g[i], mla_w_ukv[i])
        mix = jnp.concatenate([ya, yb, yc, yd], axis=-1) @ w_out[i]
        x = layer_norm(DN_ALPHA * x + mix, ln1_g[i], ln1_b[i], LN_EPS)
        ffn = hier_moe(x, moe_w_rg[i], moe_b_rg[i], moe_w_re[i], moe_b_re[i],
                       moe_w_gate[i], moe_w_up[i], moe_w_down[i])
        ple = jax.nn.sigmoid(x @ ple_w_gate[i] + ple_b_gate[i]) * (p[i] @ ple_w[i])
        x = layer_norm(DN_ALPHA * x + ffn + ple, ln2_g[i], ln2_b[i], LN_EPS)
    return x
```

```python
import numpy as np
import concourse.bass as bass
import concourse.mybir as mybir
from contextlib import ExitStack
from concourse.bass_utils import run_bass_kernel_spmd

F32 = mybir.dt.float32
BF16 = mybir.dt.bfloat16
I32 = mybir.dt.int32
AF = mybir.ActivationFunctionType
ALU = mybir.AluOpType
AX = mybir.AxisListType
ENGS = ['pe', 'dve', 'act', 'pool', 'sp']


class Buf:
    def __init__(self, name, h, dsem=None, dkey=None):
        self.name = name
        self.h = h
        self.last_write = None
        self.reads = {}
        self.dsem = dsem
        self.dkey = dkey
        self.dcount = 0
        self.wtoks = {}
        self.is_dram = False

    def __getitem__(self, key):
        return self.h[key]


class KB:
    def __init__(self, nc, es):
        self.nc = nc
        self.es = es
        self.ins = {e: [] for e in ENGS}
        self.waited = {e: {} for e in ENGS}
        self.miles = {}
        self.psem = {}
        self.epoch = -1
        self.new_epoch()
        self.dsems = {}
        self.nbuf = 0
        self.psum_banks = []
        self.psum_rr = 0

    def new_epoch(self):
        self.epoch += 1
        self.ekey = {}
        for e in ENGS:
            k = "%s#%d" % (e, self.epoch)
            self.ekey[e] = k
            self.psem[k] = self.es.enter_context(self.nc.semaphore("ps_%s_%d" % (e, self.epoch)))
            self.miles[k] = set()

    def _dsem(self, name):
        s = self.es.enter_context(self.nc.semaphore("d_" + name))
        key = len(self.dsems)
        self.dsems[key] = s
        return s, key

    ARENA = 52500

    def _arena_init(self):
        self.big = self.es.enter_context(self.nc.sbuf_tensor("arena", [128, self.ARENA], F32))
        self.aptr = 0
        self.live = []
        self.inherit = {}
        self.dpool = []
        self.dpool_sw = []

    def sbuf(self, name, shape, dt, dma=False):
        if not hasattr(self, 'big'):
            self._arena_init()
        P, Fd = shape
        esz = 2 if dt == BF16 else 4
        ncol = (Fd * esz + 3) // 4
        ncol = (ncol + 7) // 8 * 8
        assert self.aptr + ncol <= self.ARENA, ("SBUF arena overflow", name, self.aptr, ncol)
        v = self.big[0:P, self.aptr:self.aptr + ncol]
        if dt != F32:
            v = v.bitcast(dt)
        v = v[:, 0:Fd]
        self.aptr += ncol
        b = Buf(name, v)
        if dma:
            b.sw = (dma == 'sw')
            pool = self.dpool_sw if b.sw else self.dpool
            if pool:
                b.ds = pool.pop()
            else:
                s, k = self._dsem(name)
                b.ds = [s, k, 0]
            b.dsem, b.dkey, b.dcount = b.ds
            b.dbase = b.dcount * 16
        b.reads = dict(self.inherit)
        self.live.append(b)
        return b

    def mark(self):
        if not hasattr(self, 'big'):
            self._arena_init()
        return (self.aptr, len(self.live))

    def release(self, mk):
        aptr, nl = mk
        for b in self.live[nl:]:
            toks = list(b.reads.values())
            if b.last_write is not None:
                toks.append(b.last_write)
            for t in toks:
                k = t[:2]
                if k not in self.inherit or self.inherit[k][2] < t[2]:
                    self.inherit[k] = t
            if b.dsem is not None:
                b.ds[2] = b.dcount
                (self.dpool_sw if b.sw else self.dpool).append(b.ds)
        del self.live[nl:]
        self.aptr = aptr

    def psum_init(self):
        for i in range(8):
            h = self.es.enter_context(self.nc.psum_tensor("psb%d" % i, [128, 512], F32))
            self.psum_banks.append(Buf("psb%d" % i, h))

    NROT = 8

    def psum(self):
        b = self.psum_banks[self.psum_rr % self.NROT]
        self.psum_rr += 1
        return b

    def dram(self, name, shape, dt, kind=None):
        if kind:
            h = self.nc.dram_tensor(name, list(shape), dt, kind=kind)
        else:
            h = self.nc.dram_tensor(name, list(shape), dt)
        b = Buf(name, h)
        b.is_dram = True
        return b

    def _deps(self, eng, reads, writes, skip_dkey=None, skip_base=0):
        toks = []
        for b in reads:
            if b.last_write is not None:
                toks.append(b.last_write)
            toks.extend(b.wtoks.values())
        for b in writes:
            if b.last_write is not None:
                toks.append(b.last_write)
            toks.extend(b.wtoks.values())
            toks.extend(b.reads.values())
        need = []
        for t in toks:
            key = t[:2]
            val = t[2]
            if t[0] == 'e' and eng == 'pe' and t[1].startswith('pe#'):
                continue
            if t[0] == 'd' and skip_dkey is not None and t[1] == skip_dkey and val > skip_base:
                continue
            if self.waited[eng].get(key, -1) >= val:
                continue
            self.waited[eng][key] = val
            need.append(t)
            if t[0] == 'e':
                self.miles[t[1]].add(val)
        return need

    def op(self, eng, fn, reads=(), writes=()):
        need = self._deps(eng, reads, writes)
        idx = len(self.ins[eng])
        key = self.ekey[eng]
        self.ins[eng].append((fn, need, None, key))
        tok = ('e', key, idx)
        for b in reads:
            b.reads[('e', key)] = tok
        for b in writes:
            b.last_write = tok
            b.reads = {}
        return tok

    def dma(self, eng, out_buf, out_ap, in_buf, in_ap):
        if out_buf.is_dram:
            sb = in_buf
            assert not in_buf.is_dram
        else:
            sb = out_buf
        assert sb.dsem is not None, sb.name
        assert (eng == 'pool') == bool(getattr(sb, 'sw', False)), ("dma queue/sem kind mismatch", sb.name, eng)
        need = self._deps(eng, [in_buf], [out_buf], skip_dkey=sb.dkey, skip_base=getattr(sb, 'dbase', 0))
        sb.dcount += 1
        tok = ('d', sb.dkey, 16 * sb.dcount)
        self.ins[eng].append((lambda e: e.dma_start(out=out_ap, in_=in_ap), need, sb.dsem, None))
        in_buf.reads[('d', sb.dkey)] = tok
        if out_buf.is_dram:
            out_buf.wtoks[sb.dkey] = tok
        else:
            out_buf.last_write = tok
        return tok

    def wait_all_writes(self, eng, bufs):
        need = self._deps(eng, bufs, [])
        self.ins[eng].append((None, need, None, None))

    def emit(self):
        nc = self.nc
        ranks = {k: {idx: r + 1 for r, idx in enumerate(sorted(v))} for k, v in self.miles.items()}
        kb = self

        def run(name, eng):
            for idx, (fn, need, dsem, key) in enumerate(kb.ins[name]):
                for t in need:
                    if t[0] == 'e':
                        eng.wait_ge(kb.psem[t[1]], ranks[t[1]][t[2]])
                    else:
                        eng.wait_ge(kb.dsems[t[1]], t[2])
                if fn is None:
                    continue
                ins = fn(eng)
                if dsem is not None:
                    ins.then_inc(dsem, 16)
                elif key is not None and idx in ranks[key]:
                    ins.then_inc(kb.psem[key], 1)

        with nc.Block() as block:
            @block.tensor
            def _(e):
                run('pe', e)

            @block.vector
            def _(e):
                run('dve', e)

            @block.scalar
            def _(e):
                run('act', e)

            @block.gpsimd
            def _(e):
                run('pool', e)

            @block.sync
            def _(e):
                run('sp', e)
        return {e: (len(self.ins[e]), max(len(v) for k, v in ranks.items() if k.startswith(e + '#'))) for e in ENGS}


S = 4096
D = 1024
DIN = 3128
L = 4
ALPHA = 8.0 ** 0.25

A0, B0, C0, D0 = 0, 896, 1680, 2712
UT0, UTN = 1024, 1688
UT_BK, UT_BV, UT_BG = 0, 128, 400
UT_CV, UT_CO, UT_CG = 1168, 1424, 1680
FM = [('A', 0, 896), ('Bq', 896, 128), ('Bk', 1024, 128), ('Bad', 1408, 16),
      ('Cqk', 1680, 512), ('Dcq', 2712, 256), ('Dckv', 2968, 128), ('Dkr', 3096, 32)]
FMOFF = {}
_r = 0
for _n, _c, _w in FM:
    FMOFF[_n] = _r
    _r += ((_w + 127) // 128) * 128
UFN = _r

CI_ID, CI_TRI, CI_TRIS, CI_ONES, CI_HM, CI_HMROW = 0, 128, 256, 384, 512, 520
CI_HM2 = CI_HMROW + 512
CI_FREQ = CI_HM2 + 2
CI_SGN = CI_HM2 + 3
CI_TRISL = CI_HM2 + 8
CI_BLK64 = CI_TRISL + 128
NCONST = CI_BLK64 + 128

RP = {}
_o = 0
for _n, _w in [('ln1_g', 1024), ('ln1_b', 1024), ('ln2_g', 1024), ('ln2_b', 1024), ('ple_b_gate', 1024),
               ('gla_norm_g', 256), ('mlstm_norm_g', 256), ('rwkv_gn_g', 256), ('rwkv_gn_b', 256),
               ('moe_b_rg', 4), ('moe_b_re', 32), ('mlstm_i_b', 4), ('mlstm_f_b', 4)]:
    RP[_n] = (_o, _w)
    _o += _w
NROW = (_o + 7) // 8 * 8
CP = {'rwkv_mu': (0, 7), 'rwkv_w0': (7, 2), 'rwkv_a0': (9, 2), 'rwkv_k_k': (11, 2), 'rwkv_k_a': (13, 2),
      'rwkv_r_k': (15, 2), 'conv_w': (17, 16), 'conv_b': (33, 4), 'mla_q_norm_g': (37, 2), 'mla_kv_norm_g': (39, 1)}
NCOL = 40


def make_consts():
    c = np.zeros((128, NCONST), np.float32)
    c[:, CI_ID:CI_ID + 128] = np.eye(128, dtype=np.float32)
    j = np.arange(128)[:, None]
    i = np.arange(128)[None, :]
    c[:, CI_TRI:CI_TRI + 128] = (j <= i)
    c[:, CI_TRIS:CI_TRIS + 128] = (j > i)
    c[:, CI_ONES:CI_ONES + 128] = 1.0
    c[:, CI_TRISL:CI_TRISL + 128] = (j < i)
    c[0:64, CI_BLK64:CI_BLK64 + 64] = 1.0
    c[64:128, CI_BLK64 + 64:CI_BLK64 + 128] = 1.0
    for h in range(4):
        c[h * 32:(h + 1) * 32, CI_HM + h] = 1.0
        c[:, CI_HMROW + h * 128 + h * 32:CI_HMROW + h * 128 + (h + 1) * 32] = 1.0
    c[0:64, CI_HM2] = 1.0
    inv = (10000.0 ** (-np.arange(0, 32, 2, dtype=np.float32) / 32)).astype(np.float32)
    c[64:96, CI_FREQ] = np.concatenate([inv, inv])
    c[64:80, CI_SGN] = -1.0
    c[80:96, CI_SGN] = 1.0
    c[64:128, CI_HM2 + 1] = 1.0
    return c


def pack_params(inp):
    rowp = np.zeros((L, NROW), np.float32)
    for n, (o, w) in RP.items():
        rowp[:, o:o + w] = np.asarray(inp[n]).reshape(L, w)
    colp = np.zeros((L, 128, NCOL), np.float32)

    def put(name, arr, ncols):
        o, w = CP[name]
        assert w == ncols
        colp[:, :, o:o + w] = np.asarray(arr).reshape(L, ncols, 128).transpose(0, 2, 1)
    put('rwkv_mu', inp['rwkv_mu'], 7)
    put('rwkv_w0', inp['rwkv_w0'], 2)
    put('rwkv_a0', inp['rwkv_a0'], 2)
    put('rwkv_k_k', inp['rwkv_k_k'], 2)
    put('rwkv_k_a', inp['rwkv_k_a'], 2)
    put('rwkv_r_k', np.asarray(inp['rwkv_r_k']).reshape(L, 256), 2)
    put('conv_w', np.asarray(inp['mlstm_conv_w']).reshape(L, 4 * 512), 16)
    put('conv_b', inp['mlstm_conv_b'], 4)
    put('mla_q_norm_g', inp['mla_q_norm_g'], 2)
    put('mla_kv_norm_g', inp['mla_kv_norm_g'], 1)
    return rowp, colp


IN_NAMES = ['w_in', 'gla_alpha_up', 'gla_alpha_b', 'mla_w_uq', 'mla_w_ukv', 'rwkv_w_up', 'rwkv_a_up', 'rwkv_g_up',
            'w_out', 'moe_w_rg', 'moe_w_re', 'moe_w_gate', 'moe_w_up', 'moe_w_down', 'ple_w_gate', 'ple_w']


def core_inputs(inp, b, shared=None):
    if shared is None:
        rowp, colp = pack_params(inp)
        shared = {'consts': make_consts(), 'rowp': rowp, 'colp': colp}
        for n in IN_NAMES:
            shared[n] = np.ascontiguousarray(inp[n], dtype=np.float32)
    m = dict(shared)
    m['x'] = np.ascontiguousarray(inp['x'][b])
    m['p'] = np.ascontiguousarray(inp['p'][:, b])
    m['positions'] = np.ascontiguousarray(inp['positions'][b:b + 1]).astype(np.int32)
    return m


class Ring:
    def __init__(self, kb, name, n, shape, dt, dma=False):
        self.bufs = [kb.sbuf("%s%d" % (name, i), shape, dt, dma=dma) for i in range(n)]
        self.i = 0

    def next(self):
        b = self.bufs[self.i % len(self.bufs)]
        self.i += 1
        return b


def bc3(ap, shape, axis):
    return ap.unsqueeze(axis).to_broadcast(list(shape))


def build(n_layers=L, stop=None, only=None):
    CUTC = 99
    CUTA = 99
    nc = bass.Bass("TRN2", target_bir_lowering=False)
    with ExitStack() as es:
        kb = KB(nc, es)
        kb.psum_init()
        x_in = kb.dram("x", [S, D], F32, "ExternalInput")
        w_in = kb.dram("w_in", [L, D, DIN], F32, "ExternalInput")
        consts = kb.dram("consts", [128, NCONST], F32, "ExternalInput")
        rowp = kb.dram("rowp", [L, NROW], F32, "ExternalInput")
        colp = kb.dram("colp", [L, 128, NCOL], F32, "ExternalInput")
        gla_aup = kb.dram("gla_alpha_up", [L, 16, 128], F32, "ExternalInput")
        gla_ab = kb.dram("gla_alpha_b", [L, 128], F32, "ExternalInput")
        pos_in = kb.dram("positions", [1, S], I32, "ExternalInput")
        rw_wup = kb.dram("rwkv_w_up", [L, 32, 256], F32, "ExternalInput")
        w_out = kb.dram("w_out", [L, D, D], F32, "ExternalInput")
        w_rg = kb.dram("moe_w_rg", [L, D, 4], F32, "ExternalInput")
        w_re = kb.dram("moe_w_re", [L, D, 32], F32, "ExternalInput")
        w_gate = kb.dram("moe_w_gate", [L, 32, D, 512], F32, "ExternalInput")
        w_up = kb.dram("moe_w_up", [L, 32, D, 512], F32, "ExternalInput")
        w_down = kb.dram("moe_w_down", [L, 32, 512, D], F32, "ExternalInput")
        ple_wg = kb.dram("ple_w_gate", [L, D, D], F32, "ExternalInput")
        ple_w = kb.dram("ple_w", [L, 256, D], F32, "ExternalInput")
        p_in = kb.dram("p", [L, S, 256], F32, "ExternalInput")
        X1 = kb.dram("X1", [S, D], F32, "ExternalOutput" if stop == 'WO' else None)
        X2 = kb.dram("X2", [S, D], F32)
        OUT = kb.dram("out", [S, D], F32, "ExternalOutput" if stop in (None, 'FF') else None)
        GD = kb.dram("GD", [S, 32], F32, "ExternalOutput" if stop == 'WO' else None)
        rw_aup = kb.dram("rwkv_a_up", [L, 32, 256], F32, "ExternalInput")
        rw_gup = kb.dram("rwkv_g_up", [L, 64, 256], F32, "ExternalInput")
        w_uq = kb.dram("mla_w_uq", [L, 256, 384], F32, "ExternalInput")
        w_ukv = kb.dram("mla_w_ukv", [L, 128, 512], F32, "ExternalInput")
        COS2 = kb.dram("COS2", [128, S], F32)
        SIN2 = kb.dram("SIN2", [128, S], F32)
        dbg = stop is not None
        UF = kb.dram("UF", [UFN, S], F32, "ExternalOutput" if stop == 'P1' else None)
        UT = kb.dram("UT", [S, UTN], F32, "ExternalOutput" if stop == 'P1' else None)
        Y = kb.dram("Y", [S, D], F32, "ExternalOutput" if stop == 'MIX' else None)

        def row_bc_ap(l, name, n0=0, n=None):
            o, w = RP[name]
            if n is None:
                n = w
            return bass.AP(rowp.h, l * NROW + o + n0, [[0, 128], [1, n]])

        def mm(ps, out_ap, lhsT, rhs, reads, start=True, stop=True):
            kb.op('pe', lambda e: e.matmul(out_ap, lhsT=lhsT, rhs=rhs, start=start, stop=stop), reads, [ps])

        def act(out_ap, in_ap, func, reads, writes, bias=0.0, scale=1.0):
            kb.op('act', lambda e: e.activation(out=out_ap, in_=in_ap, func=func, bias=bias, scale=scale), reads, writes)

        def tt(eng, out_ap, in0, in1, op, reads, writes):
            kb.op(eng, lambda e: e.tensor_tensor(out=out_ap, in0=in0, in1=in1, op=op), reads, writes)

        def ts(eng, out_ap, in0, s1, s2, op0, op1, reads, writes):
            kb.op(eng, lambda e: e.tensor_scalar(out=out_ap, in0=in0, scalar1=s1, scalar2=s2, op0=op0, op1=op1), reads, writes)

        def stt(eng, out_ap, in0, scalar, in1, op0, op1, reads, writes):
            eng = 'dve'
            kb.op(eng, lambda e: e.scalar_tensor_tensor(out=out_ap, in0=in0, scalar=scalar, in1=in1, op0=op0, op1=op1), reads, writes)

        def cp(eng, out_ap, in_ap, reads, writes):
            if eng == 'act':
                kb.op('act', lambda e: e.copy(out=out_ap, in_=in_ap), reads, writes)
            else:
                kb.op(eng, lambda e: e.tensor_copy(out=out_ap, in_=in_ap), reads, writes)

        def rsqrt(buf, out_ap, in_ap, scale, eps):
            act(out_ap, in_ap, AF.Sqrt, [buf], [buf], bias=eps, scale=scale)
            kb.op('dve', lambda e: e.reciprocal(out=out_ap, in_=out_ap), [buf], [buf])

        def memset(eng, buf, ap, val):
            kb.op(eng, lambda e: e.memset(ap, val), [], [buf])

        cst = kb.sbuf("cst", [128, NCONST], F32, dma=True)
        kb.dma('sp', cst, cst[:, :], consts, consts[:, :])
        cstb = kb.sbuf("cstb", [128, NCONST], BF16)
        cp('dve', cstb[:, :], cst[:, :], [cst], [cstb])
        ident = cst[:, CI_ID:CI_ID + 128]

        xT = [kb.sbuf("xT%d" % h, [128, 8 * 2048], BF16) for h in range(2)]
        Gall = kb.sbuf("Gall", [128, 32 * 32], F32)
        cnt = [0]

        def evac(out_ap, in_ap, reads, writes):
            cnt[0] += 1
            cp('act' if cnt[0] % 2 else 'dve', out_ap, in_ap, reads, writes)

        def transpose_to_xT(src, tt_):
            h, tl = tt_ // 16, (tt_ % 16) * 128
            for g in range(2):
                ps = kb.psum()
                for j in range(4):
                    kc = g * 4 + j
                    kb.op('pe', lambda e, ps=ps, j=j, kc=kc: e.transpose(
                        out=ps[:, j * 128:(j + 1) * 128], in_=src[:, kc * 128:(kc + 1) * 128],
                        identity=ident), [src, cst], [ps])
                dst = xT[h][:, :].rearrange("p (k t) -> p k t", k=8)[:, g * 4:(g + 1) * 4, tl:tl + 128]
                srcp = ps[:, :].rearrange("p (k t) -> p k t", k=4)
                evac(dst, srcp, [ps], [xT[h]])

        def transpose_tile(src, tt_, x32=None):
            h, tl = tt_ // 16, (tt_ % 16) * 128
            for g in range(2):
                ps = kb.psum()
                for j in range(4):
                    kc = g * 4 + j
                    kb.op('pe', lambda e, ps=ps, j=j, kc=kc: e.transpose(
                        out=ps[:, j * 128:(j + 1) * 128], in_=src[:, kc * 128:(kc + 1) * 128],
                        identity=ident), [src, cst], [ps])
                dst = xT[h][:, :].rearrange("p (k t) -> p k t", k=8)[:, g * 4:(g + 1) * 4, tl:tl + 128]
                if x32 is None:
                    evac(dst, ps[:, :].rearrange("p (k t) -> p k t", k=4), [ps], [xT[h]])
                else:
                    cp('act', x32[:, g * 512:(g + 1) * 512], ps[:, :], [ps], [x32])
                    cp('pool', dst, x32[:, g * 512:(g + 1) * 512].rearrange("p (k t) -> p k t", k=4), [x32], [xT[h]])

        def layer_norm_tile(r, gb, o_g, o_b, eps, wk):
            for hf_ in range(2):
                kb.op('dve', lambda e, hf_=hf_: e.bn_stats(out=wk[:, hf_ * 6:(hf_ + 1) * 6], in_=r[:, hf_ * 512:(hf_ + 1) * 512]), [r], [wk])
            kb.op('dve', lambda e: e.bn_aggr(out=wk[:, 12:14], in_=wk[:, 0:12]), [wk], [wk])
            rsqrt(wk, wk[:, 14:15], wk[:, 13:14], 1.0, eps)
            ts('dve', r[:, :], r[:, :], wk[:, 12:13], wk[:, 14:15], ALU.subtract, ALU.mult, [r, wk], [r])
            tt('pool', r[:, :], r[:, :], gb[:, o_g:o_g + 1024], ALU.mult, [r, gb], [r])
            tt('pool', r[:, :], r[:, :], gb[:, o_b:o_b + 1024], ALU.add, [r, gb], [r])

        mk = kb.mark()
        xt_ring = Ring(kb, "xt", 3, [128, D], F32, dma=True)
        for tt_ in range(32):
            xt = xt_ring.next()
            kb.dma('sp', xt, xt[:, :], x_in, x_in[tt_ * 128:(tt_ + 1) * 128, :])
            transpose_to_xT(xt, tt_)
        kb.release(mk)

        mk = kb.mark()
        posi = kb.sbuf("posi", [128, 1024], I32, dma=True)
        ang = kb.sbuf("ang", [128, 1024], F32)
        kf = kb.sbuf("kf", [128, 1024], F32)
        cs_r = Ring(kb, "cs", 2, [128, 1024], F32, dma=True)
        PI = float(np.pi)
        for q4 in range(4):
            kb.dma('sp', posi, posi[:, :], pos_in, bass.AP(pos_in.h, q4 * 1024, [[0, 128], [1, 1024]]))
            cp('dve', ang[:, :], posi[:, :], [posi], [ang])
            ts('dve', ang[:, :], ang[:, :], cst[:, CI_FREQ:CI_FREQ + 1], None, ALU.mult, ALU.bypass, [ang, cst], [ang])
            for which, shift, dst in ((0, 0.5 * PI, COS2), (1, 0.0, SIN2)):
                t_ = cs_r.next()
                ts('dve', t_[:, :], ang[:, :], shift, 1.0 / (2 * PI), ALU.add, ALU.mult, [ang], [t_])
                cp('dve', posi[:, :], t_[:, :], [t_], [posi])
                cp('dve', kf[:, :], posi[:, :], [posi], [kf])
                ts('dve', t_[:, :], ang[:, :], shift, None, ALU.add, ALU.bypass, [ang], [t_])
                stt('dve', t_[:, :], kf[:, :], -2 * PI, t_[:, :], ALU.mult, ALU.add, [kf, t_], [t_])
                ts('dve', kf[:, :], t_[:, :], PI, -2 * PI, ALU.is_gt, ALU.mult, [t_], [kf])
                tt('dve', t_[:, :], t_[:, :], kf[:, :], ALU.add, [t_, kf], [t_])
                ts('dve', kf[:, :], t_[:, :], -PI, 2 * PI, ALU.is_lt, ALU.mult, [t_], [kf])
                tt('dve', t_[:, :], t_[:, :], kf[:, :], ALU.add, [t_, kf], [t_])
                act(t_[:, :], t_[:, :], AF.Sin, [t_], [t_])
                if which == 1:
                    ts('dve', t_[:, :], t_[:, :], cst[:, CI_SGN:CI_SGN + 1], None, ALU.mult, ALU.bypass, [t_, cst], [t_])
                kb.dma('sp', dst, dst[:, q4 * 1024:(q4 + 1) * 1024], t_, t_[:, :])
        kb.release(mk)

        for l in range(n_layers):
            if l > 0:
                kb.new_epoch()
            if only is None or 'P1' in only:
                mk = kb.mark()
                Win = kb.sbuf("Win", [128, 8 * DIN], BF16, dma='sw')
                st_ring = Ring(kb, "stg", 4, [128, 512], F32, dma=True)
                for kc in range(8):
                    kb.dma('pool', Win, Win[:, kc * DIN:(kc + 1) * DIN], w_in, w_in[l, kc * 128:(kc + 1) * 128, :])
                for name, c0, w in FM:
                    for ci in range((w + 127) // 128):
                        cc = c0 + ci * 128
                        n = min(128, c0 + w - cc)
                        row0 = FMOFF[name] + ci * 128
                        for tc in range(8):
                            h, tl = tc // 4, (tc % 4) * 512
                            ps = kb.psum()
                            for kc in range(8):
                                mm(ps, ps[0:n, :], Win[:, kc * DIN + cc:kc * DIN + cc + n],
                                   xT[h][:, kc * 2048 + tl:kc * 2048 + tl + 512], [Win, xT[h]],
                                   start=(kc == 0), stop=(kc == 7))
                            stg = st_ring.next()
                            evac(stg[0:n, :], ps[0:n, :], [ps], [stg])
                            kb.dma('sp', UF, UF[row0:row0 + n, tc * 512:(tc + 1) * 512], stg, stg[0:n, :])
                for tt_ in range(32):
                    h, tl = tt_ // 16, (tt_ % 16) * 128
                    for g in range(4):
                        g0 = g * 512
                        n = min(512, UTN - g0)
                        ps = kb.psum()
                        for kc in range(8):
                            mm(ps, ps[:, 0:n], xT[h][:, kc * 2048 + tl:kc * 2048 + tl + 128],
                               Win[:, kc * DIN + UT0 + g0:kc * DIN + UT0 + g0 + n], [Win, xT[h]],
                               start=(kc == 0), stop=(kc == 7))
                        stg = st_ring.next()
                        evac(stg[:, 0:n], ps[:, 0:n], [ps], [stg])
                        kb.dma('sp', UT, UT[tt_ * 128:(tt_ + 1) * 128, g0:g0 + n], stg, stg[:, 0:n])
                kb.release(mk)
            if stop == 'P1':
                break


            if only is None or 'A' in only:
                mk = kb.mark()
                cpl = kb.sbuf("cplA", [128, NCOL], F32, dma=True)
                kb.dma('sp', cpl, cpl[:, :], colp, colp[l, :, :])
                omu, ow0, oa0, okk, oka, ork = [CP[n_][0] for n_ in ('rwkv_mu', 'rwkv_w0', 'rwkv_a0', 'rwkv_k_k', 'rwkv_k_a', 'rwkv_r_k')]
                gnb = kb.sbuf("gnb", [128, 512], F32, dma=True)
                kb.dma('sp', gnb, gnb[:, 0:256], rowp, row_bc_ap(l, 'rwkv_gn_g'))
                kb.dma('sp', gnb, gnb[:, 256:512], rowp, row_bc_ap(l, 'rwkv_gn_b'))
                WP = kb.sbuf("WP", [128, 768], BF16, dma='sw')
                memset('pool', WP, WP[:, :], 0.0)
                kb.dma('pool', WP, WP[0:32, 0:256], rw_wup, rw_wup[l, :, :])
                kb.dma('pool', WP, WP[32:64, 256:512], rw_aup, rw_aup[l, :, :])
                kb.dma('pool', WP, WP[64:128, 512:768], rw_gup, rw_gup[l, :, :])
                ST32 = [kb.sbuf("ST32_%d" % p_, [128, 64], F32) for p_ in range(2)]
                STb = [kb.sbuf("STb_%d" % p_, [128, 64], BF16) for p_ in range(2)]
                for p_ in range(2):
                    memset('dve', ST32[p_], ST32[p_][:, :], 0.0)
                    memset('dve', STb[p_], STb[p_][:, :], 0.0)
                BKM_r = Ring(kb, "BKM", 2, [128, 1024], BF16)
                for b_ in BKM_r.bufs:
                    memset('pool', b_, b_[:, :], 0.0)
                raw_r = Ring(kb, "rawA", 2, [128, 7 * 129], F32, dma=True)
                us_r = Ring(kb, "us", 2, [128, 896], F32)
                dd_r = Ring(kb, "dd", 1, [128, 896], F32)
                T6_r = Ring(kb, "T6", 2, [128, 128], BF16)
                f_r = Ring(kb, "fA", 2, [128, 4096], F32)
                bA_r = Ring(kb, "bA", 2, [128, 3072], BF16)
                mats_r = Ring(kb, "mats", 2, [128, 2560], BF16)
                Mt_r = Ring(kb, "Mt", 2, [128, 7 * 512], BF16)
                X_r = Ring(kb, "XA", 2, [128, 256], F32)
                Xb_r = Ring(kb, "XbA", 3, [128, 256], BF16)
                ep_r = Ring(kb, "epA", 2, [128, 1024], F32)
                sm_r = Ring(kb, "smA", 2, [128, 32], F32)
                ya_r = Ring(kb, "ya", 2, [128, 256], F32, dma=True)
                oA = FMOFF['A']
                ONES = cst[:, CI_ONES:CI_ONES + 128]
                BLK = cst[:, CI_BLK64:CI_BLK64 + 128]
                mTRI = cstb[:, CI_TRI:CI_TRI + 128]
                mTRISL = cstb[:, CI_TRISL:CI_TRISL + 128]
                mTRIS = cstb[:, CI_TRIS:CI_TRIS + 128]
                HM2b = cstb[:, CI_HM2:CI_HM2 + 2]
                DK = 0.6065306597126334

                def v3(ap, a):
                    return ap.rearrange("p (a t) -> p a t", a=a)
                for c in range(32):
                    t0 = c * 128
                    raw = raw_r.next()
                    r3 = v3(raw[:, :], 7)
                    if c == 0:
                        memset('dve', raw, raw[:, :], 0.0)
                        kb.dma('sp', raw, r3[:, :, 1:129], UF, UF[oA:oA + 896, 0:128].rearrange("(c p) t -> p c t", p=128))
                    else:
                        kb.dma('sp', raw, r3[:, :, :], UF, UF[oA:oA + 896, t0 - 1:t0 + 128].rearrange("(c p) t -> p c t", p=128))
                    if CUTA < 1:
                        continue
                    dd = dd_r.next()
                    us = us_r.next()
                    tt('dve', v3(dd[:, :], 7), r3[:, :, 0:128], r3[:, :, 1:129], ALU.subtract, [raw], [dd])
                    tt('pool', v3(dd[:, :], 7), v3(dd[:, :], 7), bc3(cpl[:, omu:omu + 7], [128, 7, 128], 2), ALU.mult, [dd, cpl], [dd])
                    tt('dve', v3(us[:, :], 7), v3(dd[:, :], 7), r3[:, :, 1:129], ALU.add, [dd, raw], [us])
                    R_ = us[:, 0:256]
                    K_ = us[:, 256:512]
                    V_ = us[:, 512:768]
                    if CUTA < 2:
                        continue
                    T6 = T6_r.next()
                    act(T6[0:32, :], us[0:32, 768:896], AF.Tanh, [us], [T6])
                    cp('pool', T6[32:64, :], us[32:64, 768:896], [us], [T6])
                    act(T6[64:128, :], us[64:128, 768:896], AF.Sigmoid, [us], [T6])
                    if CUTA < 3:
                        continue
                    psz = kb.psum()
                    for p_ in range(2):
                        mm(psz, psz[:, p_ * 128:(p_ + 1) * 128], WP[:, p_ * 128:(p_ + 1) * 128], T6[:, :], [WP, T6])
                        mm(psz, psz[:, 256 + p_ * 128:256 + (p_ + 1) * 128], WP[:, 256 + p_ * 128:256 + (p_ + 1) * 128], T6[:, :], [WP, T6])
                    f = f_r.next()
                    F_ = lambda i_: f[:, i_ * 256:(i_ + 1) * 256]
                    SG, AI, KKt, SQ, CS, E1, E1m, E2, E3, BV, KP, TMP = [F_(i_) for i_ in range(12)]
                    for p_ in range(2):
                        act(SG[:, p_ * 128:(p_ + 1) * 128], psz[:, p_ * 128:(p_ + 1) * 128], AF.Sigmoid, [psz, cpl], [f],
                            bias=cpl[:, ow0 + p_:ow0 + p_ + 1])
                        act(AI[:, p_ * 128:(p_ + 1) * 128], psz[:, 256 + p_ * 128:256 + (p_ + 1) * 128], AF.Sigmoid, [psz, cpl], [f],
                            bias=cpl[:, oa0 + p_:oa0 + p_ + 1])
                    if CUTA < 4:
                        continue
                    tt('dve', v3(KKt, 2), v3(K_, 2), bc3(cpl[:, okk:okk + 2], [128, 2, 128], 2), ALU.mult, [us, cpl], [f])
                    act(SQ, KKt, AF.Square, [f], [f])
                    psn = kb.psum()
                    mm(psn, psn[:, 0:256], BLK, SQ, [cst, f])
                    act(SQ, psn[:, 0:256], AF.Sqrt, [psn], [f])
                    ts('dve', SQ, SQ, 1e-12, None, ALU.max, ALU.bypass, [f], [f])
                    kb.op('dve', lambda e, SQ=SQ: e.reciprocal(out=SQ, in_=SQ), [f], [f])
                    tt('dve', KKt, KKt, SQ, ALU.mult, [f], [f])
                    if CUTA < 5:
                        continue
                    tt('pool', BV, KKt, AI, ALU.mult, [f], [f])
                    ts('dve', TMP, AI, -1.0, None, ALU.add, ALU.bypass, [f], [f])
                    tt('dve', v3(TMP, 2), v3(TMP, 2), bc3(cpl[:, oka:oka + 2], [128, 2, 128], 2), ALU.mult, [f, cpl], [f])
                    stt('dve', KP, TMP, 1.0, K_, ALU.add, ALU.mult, [f, us], [f])
                    if CUTA < 6:
                        continue
                    for p_ in range(2):
                        kb.op('dve', lambda e, CS=CS, SG=SG, p_=p_: e.tensor_tensor_scan(
                            out=CS[:, p_ * 128:(p_ + 1) * 128], data0=ONES, data1=SG[:, p_ * 128:(p_ + 1) * 128], initial=0.0,
                            op0=ALU.mult, op1=ALU.add), [f, cst], [f])
                    sm = sm_r.next()
                    ts('dve', sm[:, 0:2], v3(CS, 2)[:, :, 127], -DK, None, ALU.mult, ALU.bypass, [f], [sm])
                    act(E1, CS, AF.Exp, [f], [f], scale=-DK)
                    act(E2, CS, AF.Exp, [f], [f], scale=DK)
                    tt('pool', TMP, CS, SG, ALU.subtract, [f], [f])
                    act(E1m, TMP, AF.Exp, [f], [f], scale=-DK)
                    for p_ in range(2):
                        act(E3[:, p_ * 128:(p_ + 1) * 128], CS[:, p_ * 128:(p_ + 1) * 128], AF.Exp, [f, sm], [f], scale=DK,
                            bias=sm[:, p_:p_ + 1])
                    if CUTA < 7:
                        continue
                    bA = bA_r.next()
                    At, Am, Bt, Kt, Rm, RKP, Vb = (bA[:, 0:256], bA[:, 256:768], bA[:, 768:1024], bA[:, 1024:1280],
                                                   bA[:, 1280:1792], bA[:, 1792:2048], bA[:, 2048:2304])
                    Rt = bA[:, 2304:2560]
                    stt('dve', At, KKt, -1.0, E1m, ALU.mult, ALU.mult, [f], [bA])
                    tt('pool', Bt, BV, E2, ALU.mult, [f], [bA])
                    tt('pool', Kt, KP, E2, ALU.mult, [f], [bA])
                    tt('dve', Rt, R_, E1, ALU.mult, [us, f], [bA])
                    for p_ in range(2):
                        tt('dve' if p_ else 'pool', v3(Am[:, p_ * 256:(p_ + 1) * 256], 2), bc3(At[:, p_ * 128:(p_ + 1) * 128], [128, 2, 128], 1),
                           bc3(HM2b, [128, 2, 128], 2), ALU.mult, [bA, cstb], [bA])
                        tt('pool' if p_ else 'dve', v3(Rm[:, p_ * 256:(p_ + 1) * 256], 2), bc3(Rt[:, p_ * 128:(p_ + 1) * 128], [128, 2, 128], 1),
                           bc3(HM2b, [128, 2, 128], 2), ALU.mult, [bA, cstb], [bA])
                    tt('pool', TMP, R_, KP, ALU.mult, [us, f], [f])
                    tt('dve', v3(RKP, 2), v3(TMP, 2), bc3(cpl[:, ork:ork + 2], [128, 2, 128], 2), ALU.mult, [f, cpl], [bA])
                    tt('pool', E1m, BV, E3, ALU.mult, [f], [f])
                    tt('pool', E2, KP, E3, ALU.mult, [f], [f])
                    pstr = kb.psum()
                    for p_ in range(2):
                        kb.op('pe', lambda e, pstr=pstr, p_=p_, E1m=E1m: e.transpose(
                            out=pstr[:, p_ * 128:(p_ + 1) * 128], in_=E1m[:, p_ * 128:(p_ + 1) * 128], identity=ident), [f, cst], [pstr])
                        kb.op('pe', lambda e, pstr=pstr, p_=p_, E2=E2: e.transpose(
                            out=pstr[:, 256 + p_ * 128:256 + (p_ + 1) * 128], in_=E2[:, p_ * 128:(p_ + 1) * 128], identity=ident), [f, cst], [pstr])
                    BKM = BKM_r.next()
                    for w_ in range(2):
                        src = v3(pstr[:, w_ * 256:(w_ + 1) * 256], 2)
                        dst = v3(BKM[:, w_ * 512:(w_ + 1) * 512], 2)
                        cp('act', dst[:, :, 0:64], src[:, :, 0:64], [pstr], [BKM])
                        cp('dve', dst[:, :, 192:256], src[:, :, 64:128], [pstr], [BKM])
                    pstv = kb.psum()
                    for p_ in range(2):
                        kb.op('pe', lambda e, pstv=pstv, p_=p_, us=us: e.transpose(
                            out=pstv[:, p_ * 128:(p_ + 1) * 128], in_=us[:, 512 + p_ * 128:512 + (p_ + 1) * 128], identity=ident), [us, cst], [pstv])
                    ep = ep_r.next()
                    V32 = ep[:, 0:256]
                    cp('act', V32, pstv[:, 0:256], [pstv], [ep])
                    cp('dve', Vb, V32, [ep], [bA])
                    if CUTA < 8:
                        continue
                    mats = mats_r.next()
                    Akt, Rbt, Rkt = mats[:, 0:512], mats[:, 512:1024], mats[:, 1024:1536]
                    Nn = [mats[:, 1536:2048], mats[:, 2048:2560]]
                    Mt = Mt_r.next()
                    Mtk = lambda k_: Mt[:, k_ * 512:(k_ + 1) * 512]

                    def quad(lhs_of, rhs_of, reads, dst, dbuf, mask, eng):
                        ps = kb.psum()
                        for h in range(4):
                            mm(ps, ps[:, h * 128:(h + 1) * 128], lhs_of(h), rhs_of(h), reads)
                        if mask is None:
                            cp(eng, dst, ps[:, :], [ps], [dbuf])
                        else:
                            tt('dve', v3(dst, 4), v3(ps[:, :], 4), bc3(mask, [128, 4, 128], 1), ALU.mult, [ps, cstb], [dbuf])
                    Bt_p = lambda h: Bt[:, (h // 2) * 128:(h // 2 + 1) * 128]
                    Kt_p = lambda h: Kt[:, (h // 2) * 128:(h // 2 + 1) * 128]
                    Am_h = lambda h: Am[:, h * 128:(h + 1) * 128]
                    Rm_h = lambda h: Rm[:, h * 128:(h + 1) * 128]
                    M0 = Mtk(0)
                    quad(Bt_p, Am_h, [bA], M0, Mt, mTRISL, None)
                    quad(Am_h, Bt_p, [bA], Nn[0], mats, mTRIS, None)
                    quad(Kt_p, Am_h, [bA], Akt, mats, mTRISL, None)
                    quad(Bt_p, Rm_h, [bA], Rbt, mats, mTRI, None)
                    quad(Kt_p, Rm_h, [bA], Rkt, mats, mTRI, None)
                    if CUTA < 9:
                        continue
                    for k_ in range(6):
                        Mk, Nk = Mtk(k_), Nn[k_ % 2]
                        Mn, Nx = Mtk(k_ + 1), Nn[(k_ + 1) % 2]
                        quad(lambda h, Nk=Nk: Nk[:, h * 128:(h + 1) * 128], lambda h, Mk=Mk: Mk[:, h * 128:(h + 1) * 128], [mats, Mt], Mn, Mt, None, 'act')
                        if k_ < 5:
                            quad(lambda h, Mk=Mk: Mk[:, h * 128:(h + 1) * 128], lambda h, Nk=Nk: Nk[:, h * 128:(h + 1) * 128], [mats, Mt], Nx, mats, None, 'dve')
                    if CUTA < 10:
                        continue
                    psx = kb.psum()
                    for h in range(4):
                        mm(psx, psx[:, h * 64:(h + 1) * 64], Am_h(h), STb[h // 2][:, :], [bA, STb[h // 2]], start=True, stop=False)
                        mm(psx, psx[:, h * 64:(h + 1) * 64], Akt[:, h * 128:(h + 1) * 128], Vb[:, h * 64:(h + 1) * 64], [mats, bA], start=False, stop=True)
                    X = X_r.next()
                    Xb = Xb_r.next()
                    cp('dve', X[:, :], psx[:, 0:256], [psx], [X])
                    cp('act', Xb[:, :], X[:, :], [X], [Xb])
                    for k_ in range(7):
                        psx = kb.psum()
                        Mk = Mtk(k_)
                        for h in range(4):
                            mm(psx, psx[:, h * 64:(h + 1) * 64], Mk[:, h * 128:(h + 1) * 128], Xb[:, h * 64:(h + 1) * 64], [Mt, Xb])
                        tt('dve', X[:, :], X[:, :], psx[:, 0:256], ALU.add, [X, psx], [X])
                        Xb = Xb_r.next()
                        cp('act', Xb[:, :], X[:, :], [X], [Xb])
                    Ub = Xb
                    if CUTA < 11:
                        continue
                    psy = kb.psum()
                    for h in range(4):
                        o_ = psy[:, h * 64:(h + 1) * 64]
                        mm(psy, o_, Rm_h(h), STb[h // 2][:, :], [bA, STb[h // 2]], start=True, stop=False)
                        mm(psy, o_, Rbt[:, h * 128:(h + 1) * 128], Ub[:, h * 64:(h + 1) * 64], [mats, Ub], start=False, stop=False)
                        mm(psy, o_, Rkt[:, h * 128:(h + 1) * 128], Vb[:, h * 64:(h + 1) * 64], [mats, bA], start=False, stop=True)
                    if CUTA < 12:
                        continue
                    pss_ = kb.psum()
                    for p_ in range(2):
                        o_ = pss_[:, p_ * 64:(p_ + 1) * 64]
                        for hh in range(2):
                            h = p_ * 2 + hh
                            mm(pss_, o_, BKM[:, h * 128:(h + 1) * 128], Ub[:, h * 64:(h + 1) * 64], [BKM, Ub], start=(hh == 0), stop=False)
                            mm(pss_, o_, BKM[:, 512 + h * 128:512 + (h + 1) * 128], Vb[:, h * 64:(h + 1) * 64], [BKM, bA], start=False, stop=(hh == 1))
                    for p_ in range(2):
                        stt('dve', ST32[p_][:, :], ST32[p_][:, :], E1[:, p_ * 128 + 127:p_ * 128 + 128], pss_[:, p_ * 64:(p_ + 1) * 64],
                            ALU.mult, ALU.add, [ST32[p_], f, pss_], [ST32[p_]])
                        cp('act', STb[p_][:, :], ST32[p_][:, :], [ST32[p_]], [STb[p_]])
                    if CUTA < 13:
                        continue
                    psg = kb.psum()
                    mm(psg, psg[:, 0:256], T6[:, :], WP[:, 512:768], [T6, WP])
                    for p_ in range(2):
                        mm(psg, psg[:, 256 + 2 * p_:256 + 2 * p_ + 2], RKP[:, p_ * 128:(p_ + 1) * 128], HM2b, [bA, cstb])
                    HV, SQe, GG = ep[:, 256:512], ep[:, 512:768], ep[:, 768:1024]
                    cp('act', GG, psg[:, 0:256], [psg], [ep])
                    cp('dve', sm[:, 4:8], psg[:, 256:260], [psg], [sm])
                    kb.op('dve', lambda e, sm=sm, psy=psy: e.tensor_reduce(out=sm[:, 8:12], in_=v3(psy[:, 0:256], 4), axis=AX.X, op=ALU.add), [psy], [sm])
                    ts('dve', sm[:, 8:12], sm[:, 8:12], 1.0 / 64, None, ALU.mult, ALU.bypass, [sm], [sm])
                    tt('dve', v3(HV, 4), v3(psy[:, 0:256], 4), bc3(sm[:, 8:12], [128, 4, 64], 2), ALU.subtract, [psy, sm], [ep])
                    act(SQe, HV, AF.Square, [ep], [ep])
                    kb.op('dve', lambda e, sm=sm, SQe=SQe: e.tensor_reduce(out=sm[:, 12:16], in_=v3(SQe, 4), axis=AX.X, op=ALU.add), [ep], [sm])
                    rsqrt(sm, sm[:, 12:16], sm[:, 12:16], 1.0 / 64, 64e-5)
                    tt('dve', v3(HV, 4), v3(HV, 4), bc3(sm[:, 12:16], [128, 4, 64], 2), ALU.mult, [ep, sm], [ep])
                    tt('pool', HV, HV, gnb[:, 0:256], ALU.mult, [ep, gnb], [ep])
                    tt('pool', HV, HV, gnb[:, 256:512], ALU.add, [ep, gnb], [ep])
                    tt('dve', v3(SQe, 4), v3(V32, 4), bc3(sm[:, 4:8], [128, 4, 64], 2), ALU.mult, [ep, sm], [ep])
                    tt('pool', HV, HV, SQe, ALU.add, [ep], [ep])
                    ya = ya_r.next()
                    tt('dve', ya[:, :], HV, GG, ALU.mult, [ep], [ya])
                    kb.dma('sp', Y, Y[t0:t0 + 128, 0:256], ya, ya[:, :])
                kb.release(mk)

            if only is None or 'B' in only:
                mk = kb.mark()
                AUP = kb.sbuf("AUP", [17, 128], BF16, dma='sw')
                kb.dma('pool', AUP, AUP[0:16, :], gla_aup, gla_aup[l, :, :])
                kb.dma('pool', AUP, AUP[16:17, :], gla_ab, gla_ab[l:l + 1, :])
                ngb = kb.sbuf("ngb", [128, 256], F32, dma=True)
                kb.dma('sp', ngb, ngb[:, :], rowp, row_bc_ap(l, 'gla_norm_g'))
                S32 = kb.sbuf("S32", [128, 64], F32)
                Sb = kb.sbuf("Sb", [128, 64], BF16)
                memset('dve', S32, S32[:, :], 0.0)
                memset('dve', Sb, Sb[:, :], 0.0)
                adT_r = Ring(kb, "adT", 2, [17, 128], BF16, dma='sw')
                for b_ in adT_r.bufs:
                    memset('dve', b_, b_[:, :], 1.0)
                qk_r = Ring(kb, "qk32", 2, [128, 256], F32, dma=True)
                tok_r = Ring(kb, "tok", 2, [128, 656], F32, dma=True)
                vb_r = Ring(kb, "vb", 2, [128, 256], BF16)
                w1 = Ring(kb, "w1", 2, [128, 128], F32)
                nl_r = Ring(kb, "nl", 2, [128, 128], F32)
                E_r = Ring(kb, "E", 2, [128, 384], F32)
                qe_r = Ring(kb, "qe", 2, [128, 128], BF16)
                ke_r = Ring(kb, "ke", 2, [128, 128], BF16)
                kd_r = Ring(kb, "kd", 2, [128, 128], BF16)
                Qbd_r = Ring(kb, "Qbd", 2, [128, 512], BF16)
                KDbd_r = Ring(kb, "KDbd", 2, [128, 512], BF16)
                sT_r = Ring(kb, "sT", 2, [128, 512], BF16)
                sq_r = Ring(kb, "sq", 2, [128, 256], F32)
                ss_r = Ring(kb, "ss", 2, [128, 8], F32)
                on_r = Ring(kb, "on", 2, [128, 256], F32)
                sg_r = Ring(kb, "sg", 2, [128, 256], F32)
                yb_r = Ring(kb, "yb", 2, [128, 256], F32, dma=True)
                TRI = cst[:, CI_TRI:CI_TRI + 128]
                TRIS = cst[:, CI_TRIS:CI_TRIS + 128]
                oq, ok_, oad = FMOFF['Bq'], FMOFF['Bk'], FMOFF['Bad']
                for c in range(32):
                    t0 = c * 128
                    adT = adT_r.next()
                    kb.dma('pool', adT, adT[0:16, :], UF, UF[oad:oad + 16, t0:t0 + 128])
                    qk = qk_r.next()
                    kb.dma('sp', qk, qk[:, 0:128], UF, UF[oq:oq + 128, t0:t0 + 128])
                    kb.dma('sp', qk, qk[:, 128:256], UF, UF[ok_:ok_ + 128, t0:t0 + 128])
                    tok = tok_r.next()
                    kb.dma('sp', tok, tok[:, :], UT, UT[t0:t0 + 128, 0:656])
                    vb = vb_r.next()
                    cp('pool', vb[:, :], tok[:, UT_BV:UT_BV + 256], [tok], [vb])
                    psz = kb.psum()
                    mm(psz, psz[:, 0:128], adT[0:17, :], AUP[0:17, :], [adT, AUP])
                    ez = w1.next()
                    act(ez[:, :], psz[:, 0:128], AF.Exp, [psz], [ez], scale=-1.0)
                    nl = nl_r.next()
                    act(nl[:, :], ez[:, :], AF.Ln, [ez], [nl], bias=1.0)
                    psb = kb.psum()
                    mm(psb, psb[:, 0:128], nl[:, :], TRI, [nl, cst])
                    mm(psb, psb[:, 128:256], TRIS, nl[:, :], [nl, cst])
                    E = E_r.next()
                    act(E[:, 0:128], psb[:, 0:128], AF.Exp, [psb], [E], scale=-1.0 / 16)
                    act(E[:, 128:256], psb[:, 0:128], AF.Exp, [psb], [E], scale=1.0 / 16)
                    act(E[:, 256:384], psb[:, 128:256], AF.Exp, [psb], [E], scale=-1.0 / 16)
                    qe = qe_r.next()
                    stt('dve', qe[:, :], qk[:, 0:128], 32.0 ** -0.5, E[:, 0:128], ALU.mult, ALU.mult, [qk, E], [qe])
                    ke = ke_r.next()
                    tt('pool', ke[:, :], qk[:, 128:256], E[:, 128:256], ALU.mult, [qk, E], [ke])
                    kd = kd_r.next()
                    tt('pool', kd[:, :], tok[:, UT_BK:UT_BK + 128], E[:, 256:384], ALU.mult, [tok, E], [kd])
                    Qbd = Qbd_r.next()
                    tt('dve', Qbd[:, :].rearrange("p (h i) -> p h i", h=4), bc3(qe[:, :], [128, 4, 128], 1),
                       bc3(cstb[:, CI_HM:CI_HM + 4], [128, 4, 128], 2), ALU.mult, [qe, cstb], [Qbd])
                    KDbd = KDbd_r.next()
                    tt('pool', KDbd[:, :].rearrange("p (h i) -> p h i", h=4), bc3(kd[:, :], [128, 4, 128], 1),
                       cstb[:, CI_HMROW:CI_HMROW + 512].rearrange("p (h i) -> p h i", h=4), ALU.mult, [kd, cstb], [KDbd])
                    pss = kb.psum()
                    mm(pss, pss[:, 0:512], ke[:, :], Qbd[:, :], [ke, Qbd])
                    sT = sT_r.next()
                    tt('dve', sT[:, :].rearrange("p (h i) -> p h i", h=4), pss[:, :].rearrange("p (h i) -> p h i", h=4),
                       bc3(cstb[:, CI_TRI:CI_TRI + 128], [128, 4, 128], 1), ALU.mult, [pss, cstb], [sT])
                    pso = kb.psum()
                    for h in range(4):
                        mm(pso, pso[:, h * 64:(h + 1) * 64], Qbd[:, h * 128:(h + 1) * 128], Sb[:, :], [Qbd, Sb],
                           start=True, stop=False)
                        mm(pso, pso[:, h * 64:(h + 1) * 64], sT[:, h * 128:(h + 1) * 128], vb[:, h * 64:(h + 1) * 64],
                           [sT, vb], start=False, stop=True)
                    psu = kb.psum()
                    for h in range(4):
                        mm(psu, psu[:, 0:64], KDbd[:, h * 128:(h + 1) * 128], vb[:, h * 64:(h + 1) * 64], [KDbd, vb],
                           start=(h == 0), stop=(h == 3))
                    stt('dve', S32[:, :], S32[:, :], E[:, 127:128], psu[:, 0:64], ALU.mult, ALU.add, [S32, E, psu], [S32])
                    cp('act', Sb[:, :], S32[:, :], [S32], [Sb])
                    sq = sq_r.next()
                    act(sq[:, :], pso[:, 0:256], AF.Square, [pso], [sq])
                    ss = ss_r.next()
                    kb.op('dve', lambda e, ss=ss, sq=sq: e.tensor_reduce(
                        out=ss[:, 0:4], in_=sq[:, :].rearrange("p (h v) -> p h v", h=4), axis=AX.X, op=ALU.add), [sq], [ss])
                    rsqrt(ss, ss[:, 4:8], ss[:, 0:4], 1.0 / 64, 1e-6)
                    on = on_r.next()
                    tt('dve', on[:, :].rearrange("p (h v) -> p h v", h=4), pso[:, 0:256].rearrange("p (h v) -> p h v", h=4),
                       bc3(ss[:, 4:8], [128, 4, 64], 2), ALU.mult, [pso, ss], [on])
                    sg = sg_r.next()
                    act(sg[:, :], tok[:, UT_BG:UT_BG + 256], AF.Silu, [tok], [sg])
                    tt('pool', sg[:, :], sg[:, :], ngb[:, :], ALU.mult, [sg, ngb], [sg])
                    yb = yb_r.next()
                    tt('pool', yb[:, :], on[:, :], sg[:, :], ALU.mult, [on, sg], [yb])
                    kb.dma('sp', Y, Y[t0:t0 + 128, 256:512], yb, yb[:, :])
                kb.release(mk)


            if only is None or 'C' in only:
                mk = kb.mark()
                cpl = kb.sbuf("cplC", [128, NCOL], F32, dma=True)
                kb.dma('sp', cpl, cpl[:, :], colp, colp[l, :, :])
                ngc = kb.sbuf("ngc", [128, 256], F32, dma=True)
                kb.dma('sp', ngc, ngc[:, :], rowp, row_bc_ap(l, 'mlstm_norm_g'))
                ifb = kb.sbuf("ifb", [128, 8], F32, dma=True)
                kb.dma('sp', ifb, ifb[:, 0:4], rowp, row_bc_ap(l, 'mlstm_i_b'))
                kb.dma('sp', ifb, ifb[:, 4:8], rowp, row_bc_ap(l, 'mlstm_f_b'))
                M32 = [kb.sbuf("M32_%d" % p_, [128, 65], F32) for p_ in range(2)]
                Mb = [kb.sbuf("Mb_%d" % p_, [128, 65], BF16) for p_ in range(2)]
                for p_ in range(2):
                    memset('dve', M32[p_], M32[p_][:, :], 0.0)
                    memset('dve', Mb[p_], Mb[p_][:, :], 0.0)
                raw_r = Ring(kb, "raw", 2, [128, 4 * 131], F32, dma=True)
                tokc_r = Ring(kb, "tokc", 2, [128, 520], F32, dma=True)
                vaug_r = Ring(kb, "vaug", 2, [128, 260], BF16)
                KD2_r = Ring(kb, "KD2", 2, [128, 512], BF16)
                for b_ in vaug_r.bufs:
                    memset('dve', b_, b_[:, :], 1.0)
                for b_ in KD2_r.bufs:
                    memset('dve', b_, b_[:, :], 0.0)
                acc_r = Ring(kb, "cacc", 2, [128, 512], F32)
                qkc_r = Ring(kb, "qkc", 2, [128, 512], F32)
                g_r = Ring(kb, "gts", 2, [128, 32], F32)
                X_r = Ring(kb, "gX", 2, [128, 512], F32)
                EFG_r = Ring(kb, "EFG", 2, [128, 512], F32)
                qkt_r = Ring(kb, "qkt", 2, [128, 512], BF16)
                kdf_r = Ring(kb, "kdf", 2, [128, 256], BF16)
                sTc_r = Ring(kb, "sTc", 2, [128, 512], BF16)
                Q2_r = Ring(kb, "Q2", 2, [128, 512], BF16)
                hv_r = Ring(kb, "hv", 2, [128, 256], F32)
                sqc_r = Ring(kb, "sqc", 2, [128, 256], F32)
                so_r = Ring(kb, "so", 2, [128, 256], F32)
                yc_r = Ring(kb, "yc", 2, [128, 256], F32, dma=True)
                TRI = cst[:, CI_TRI:CI_TRI + 128]
                TRIS = cst[:, CI_TRIS:CI_TRIS + 128]
                oqk = FMOFF['Cqk']
                cw0, cb0 = CP['conv_w'][0], CP['conv_b'][0]
                for c in range(32):
                    t0 = c * 128
                    raw = raw_r.next()
                    r3 = raw[:, :].rearrange("p (c t) -> p c t", c=4)
                    if c == 0:
                        memset('dve', raw, raw[:, :], 0.0)
                        kb.dma('sp', raw, r3[:, :, 3:131], UF, UF[oqk:oqk + 512, 0:128].rearrange("(c p) t -> p c t", p=128))
                    else:
                        kb.dma('sp', raw, r3[:, :, :], UF, UF[oqk:oqk + 512, t0 - 3:t0 + 128].rearrange("(c p) t -> p c t", p=128))
                    tokc = tokc_r.next()
                    kb.dma('sp', tokc, tokc[:, :], UT, UT[t0:t0 + 128, UT_CV:UT_CV + 520])
                    vaug = vaug_r.next()
                    cp('pool', vaug[:, :].rearrange("p (h v) -> p h v", h=4)[:, :, 0:64],
                       tokc[:, 0:256].rearrange("p (h v) -> p h v", h=4), [tokc], [vaug])
                    if CUTC < 1:
                        continue
                    acc = acc_r.next()
                    for ci in range(4):
                        eng = 'dve'
                        a_ = acc[:, ci * 128:(ci + 1) * 128]
                        ts(eng, a_, r3[:, ci, 0:128], cpl[:, cw0 + ci:cw0 + ci + 1], cpl[:, cb0 + ci:cb0 + ci + 1],
                           ALU.mult, ALU.add, [raw, cpl], [acc])
                        for j in range(1, 4):
                            stt(eng, a_, r3[:, ci, j:j + 128], cpl[:, cw0 + j * 4 + ci:cw0 + j * 4 + ci + 1], a_,
                                ALU.mult, ALU.add, [raw, cpl, acc], [acc])
                    qkc = qkc_r.next()
                    act(qkc[:, :], acc[:, :], AF.Silu, [acc], [qkc])
                    if CUTC < 2:
                        continue
                    g = g_r.next()
                    tt('dve', g[:, 0:8], tokc[:, 512:520], ifb[:, 0:8], ALU.add, [tokc, ifb], [g])
                    act(g[:, 4:8], g[:, 4:8], AF.Exp, [g], [g], scale=-1.0)
                    act(g[:, 4:8], g[:, 4:8], AF.Ln, [g], [g], bias=1.0)
                    X = X_r.next()
                    cp('dve', X[:, 0:256].rearrange("p (h v) -> p h v", h=4), bc3(g[:, 4:8], [128, 4, 64], 2), [g], [X])
                    psF = kb.psum()
                    for p_ in range(2):
                        mm(psF, psF[:, p_ * 128:(p_ + 1) * 128], X[:, p_ * 128:(p_ + 1) * 128], TRI, [X, cst])
                    mm(psF, psF[:, 256:260], TRI, g[:, 4:8], [g, cst])
                    mm(psF, psF[:, 260:264], TRIS, g[:, 4:8], [g, cst])
                    tt('dve', g[:, 8:12], g[:, 0:4], psF[:, 256:260], ALU.add, [g, psF], [g])
                    tt('dve', g[:, 16:20], g[:, 0:4], psF[:, 260:264], ALU.subtract, [g, psF], [g])
                    act(g[:, 12:16], g[:, 16:20], AF.Exp, [g], [g])
                    cp('dve', X[:, 256:512].rearrange("p (h v) -> p h v", h=4), bc3(g[:, 8:12], [128, 4, 64], 2), [g], [X])
                    psG = kb.psum()
                    for p_ in range(2):
                        mm(psG, psG[:, p_ * 128:(p_ + 1) * 128], X[:, 256 + p_ * 128:256 + (p_ + 1) * 128], ident, [X, cst])
                    EFG = EFG_r.next()
                    act(EFG[:, 0:256], psF[:, 0:256], AF.Exp, [psF], [EFG], scale=-1.0)
                    act(EFG[:, 256:512], psG[:, 0:256], AF.Exp, [psG], [EFG])
                    if CUTC < 3:
                        continue
                    qkt = qkt_r.next()
                    tt('dve', qkt[:, 0:256], qkc[:, 0:256], EFG[:, 0:256], ALU.mult, [qkc, EFG], [qkt])
                    stt('pool', qkt[:, 256:512], qkc[:, 256:512], 0.125, EFG[:, 256:512], ALU.mult, ALU.mult, [qkc, EFG], [qkt])
                    if CUTC < 4:
                        continue
                    psT = kb.psum()
                    for p_ in range(2):
                        kb.op('pe', lambda e, psT=psT, p_=p_, qkc=qkc: e.transpose(
                            out=psT[:, p_ * 128:(p_ + 1) * 128], in_=qkc[:, 256 + p_ * 128:256 + (p_ + 1) * 128],
                            identity=ident), [qkc, cst], [psT])
                    kdf = kdf_r.next()
                    stt('dve', kdf[:, :].rearrange("p (h v) -> p h v", h=4), psT[:, 0:256].rearrange("p (h v) -> p h v", h=4),
                        0.125, bc3(g[:, 12:16], [128, 4, 64], 2), ALU.mult, ALU.mult, [psT, g], [kdf])
                    KD2 = KD2_r.next()
                    cp('pool', KD2[:, :].rearrange("p (a b) -> p a b", a=2)[:, :, 0:64],
                       kdf[:, :].rearrange("p (a b) -> p a b", a=2)[:, :, 0:64], [kdf], [KD2])
                    cp('pool', KD2[:, :].rearrange("p (a b) -> p a b", a=2)[:, :, 192:256],
                       kdf[:, :].rearrange("p (a b) -> p a b", a=2)[:, :, 64:128], [kdf], [KD2])
                    if CUTC < 5:
                        continue
                    Q2 = Q2_r.next()
                    for p_ in range(2):
                        tt('pool' if p_ else 'dve', Q2[:, p_ * 256:(p_ + 1) * 256].rearrange("p (a i) -> p a i", a=2),
                           bc3(qkt[:, p_ * 128:(p_ + 1) * 128], [128, 2, 128], 1),
                           bc3(cstb[:, CI_HM2:CI_HM2 + 2], [128, 2, 128], 2), ALU.mult, [qkt, cstb], [Q2])
                    pss = kb.psum()
                    for p_ in range(2):
                        mm(pss, pss[:, p_ * 256:(p_ + 1) * 256], qkt[:, 256 + p_ * 128:256 + (p_ + 1) * 128],
                           Q2[:, p_ * 256:(p_ + 1) * 256], [qkt, Q2])
                    sT = sTc_r.next()
                    tt('dve', sT[:, :].rearrange("p (h i) -> p h i", h=4), pss[:, :].rearrange("p (h i) -> p h i", h=4),
                       bc3(cstb[:, CI_TRI:CI_TRI + 128], [128, 4, 128], 1), ALU.mult, [pss, cstb], [sT])
                    if CUTC < 6:
                        continue
                    pso = kb.psum()
                    for h in range(4):
                        p_, hh = h // 2, h % 2
                        mm(pso, pso[:, h * 65:(h + 1) * 65], Q2[:, h * 128:(h + 1) * 128],
                           Mb[p_][:, :], [Q2, Mb[p_]], start=True, stop=False)
                        mm(pso, pso[:, h * 65:(h + 1) * 65], sT[:, h * 128:(h + 1) * 128], vaug[:, h * 65:(h + 1) * 65],
                           [sT, vaug], start=False, stop=True)
                    if CUTC < 7:
                        continue
                    psM = kb.psum()
                    for p_ in range(2):
                        for hh in range(2):
                            h = p_ * 2 + hh
                            mm(psM, psM[:, p_ * 128:p_ * 128 + 65], KD2[:, h * 128:(h + 1) * 128], vaug[:, h * 65:(h + 1) * 65],
                               [KD2, vaug], start=(hh == 0), stop=(hh == 1))
                    for p_ in range(2):
                        stt('dve', M32[p_][:, :], M32[p_][:, :], EFG[:, p_ * 128 + 127:p_ * 128 + 128], psM[:, p_ * 128:p_ * 128 + 65],
                            ALU.mult, ALU.add, [M32[p_], EFG, psM], [M32[p_]])
                        cp('act', Mb[p_][:, :], M32[p_][:, :], [M32[p_]], [Mb[p_]])
                    if CUTC < 8:
                        continue
                    po3 = pso[:, 0:260].rearrange("p (h v) -> p h v", h=4)
                    act(g[:, 20:24], po3[:, :, 64], AF.Abs, [pso], [g])
                    ts('dve', g[:, 20:24], g[:, 20:24], 1.0, None, ALU.max, ALU.bypass, [g], [g])
                    kb.op('dve', lambda e, g=g: e.reciprocal(out=g[:, 20:24], in_=g[:, 20:24]), [g], [g])
                    hv = hv_r.next()
                    hv3 = hv[:, :].rearrange("p (h v) -> p h v", h=4)
                    tt('dve', hv3, po3[:, :, 0:64], bc3(g[:, 20:24], [128, 4, 64], 2), ALU.mult, [pso, g], [hv])
                    kb.op('dve', lambda e, g=g, hv3=hv3: e.tensor_reduce(out=g[:, 24:28], in_=hv3, axis=AX.X, op=ALU.add), [hv], [g])
                    ts('dve', g[:, 24:28], g[:, 24:28], 1.0 / 64, None, ALU.mult, ALU.bypass, [g], [g])
                    tt('dve', hv3, hv3, bc3(g[:, 24:28], [128, 4, 64], 2), ALU.subtract, [hv, g], [hv])
                    sq = sqc_r.next()
                    act(sq[:, :], hv[:, :], AF.Square, [hv], [sq])
                    kb.op('dve', lambda e, g=g, sq=sq: e.tensor_reduce(
                        out=g[:, 28:32], in_=sq[:, :].rearrange("p (h v) -> p h v", h=4), axis=AX.X, op=ALU.add), [sq], [g])
                    rsqrt(g, g[:, 28:32], g[:, 28:32], 1.0 / 64, 1e-5)
                    so = so_r.next()
                    act(so[:, :], tokc[:, 256:512], AF.Sigmoid, [tokc], [so])
                    tt('pool', so[:, :], so[:, :], ngc[:, :], ALU.mult, [so, ngc], [so])
                    tt('dve', hv3, hv3, bc3(g[:, 28:32], [128, 4, 64], 2), ALU.mult, [hv, g], [hv])
                    yc = yc_r.next()
                    tt('pool', yc[:, :], hv[:, :], so[:, :], ALU.mult, [hv, so], [yc])
                    kb.dma('sp', Y, Y[t0:t0 + 128, 512:768], yc, yc[:, :])
                kb.release(mk)


            if only is None or 'D' in only:
                mk = kb.mark()
                kb.NROT = 6
                cpl = kb.sbuf("cplD", [128, NCOL], F32, dma=True)
                kb.dma('sp', cpl, cpl[:, :], colp, colp[l, :, :])
                oqn, okvn = CP['mla_q_norm_g'][0], CP['mla_kv_norm_g'][0]
                SC = 96.0 ** -0.5
                wq32 = kb.sbuf("wq32", [128, 768], F32, dma=True)
                kb.dma('sp', wq32, wq32[:, :].rearrange("p (k c) -> p k c", k=2), w_uq,
                       w_uq[l, :, :].rearrange("(k p) c -> p k c", p=128))
                Wq = kb.sbuf("Wq", [128, 768], BF16)
                Wqs = kb.sbuf("Wqs", [128, 768], BF16)
                for kc in range(2):
                    ts('dve', Wq[:, kc * 384:(kc + 1) * 384], wq32[:, kc * 384:(kc + 1) * 384], cpl[:, oqn + kc:oqn + kc + 1], SC,
                       ALU.mult, ALU.mult, [wq32, cpl], [Wq])
                cp('pool', Wqs[:, :], Wq[:, :], [Wq], [Wqs])
                w6 = Wq[:, :].rearrange("p (g c) -> p g c", c=96)
                ws6 = Wqs[:, :].rearrange("p (g c) -> p g c", c=96)
                cp('pool', ws6[:, :, 64:80], w6[:, :, 80:96], [Wq], [Wqs])
                cp('pool', ws6[:, :, 80:96], w6[:, :, 64:80], [Wq], [Wqs])
                wkv32 = kb.sbuf("wkv32", [128, 512], F32, dma=True)
                kb.dma('sp', wkv32, wkv32[:, :], w_ukv, w_ukv[l, :, :])
                Wkv = kb.sbuf("Wkv", [128, 512], BF16)
                ts('dve', Wkv[:, :], wkv32[:, :], cpl[:, okvn:okvn + 1], None, ALU.mult, ALU.bypass, [wkv32, cpl], [Wkv])
                QT = [kb.sbuf("QT%d" % h, [128, S], BF16) for h in range(4)]
                KT = [kb.sbuf("KT%d" % h, [128, S], BF16) for h in range(4)]
                Vaug = kb.sbuf("Vaug", [128, 32 * 260], BF16)
                memset('pool', Vaug, Vaug[:, :], 1.0)
                TW = 256
                mk2 = kb.mark()
                cq_r = Ring(kb, "cq32", 2, [128, 2 * TW], F32, dma=True)
                ckv_r = Ring(kb, "ckv32", 2, [128, TW], F32, dma=True)
                kr_r = Ring(kb, "kr", 2, [128, 2 * TW], F32, dma=True)
                cs2_r = Ring(kb, "cs2", 2, [128, 2 * TW], F32, dma=True)
                cqb_r = Ring(kb, "cqb", 2, [128, 2 * TW], BF16)
                ckvb_r = Ring(kb, "ckvb", 2, [128, TW], BF16)
                sqq_r = Ring(kb, "sqq", 2, [128, 2 * TW], F32)
                sqkv_r = Ring(kb, "sqkv", 2, [128, TW], F32)
                rq_r = Ring(kb, "rq", 2, [128, TW], F32)
                rkv_r = Ring(kb, "rkv", 2, [128, TW], F32)
                krot_r = Ring(kb, "krot", 2, [128, 2 * TW], F32)
                t12_r = Ring(kb, "t12", 2, [128, 2 * TW], F32)
                rc_r = Ring(kb, "rc", 4, [128, 8], F32)
                ocq, ockv, okr = FMOFF['Dcq'], FMOFF['Dckv'], FMOFF['Dkr']
                ONES = cst[:, CI_ONES:CI_ONES + 128]
                for tc in range(S // TW):
                    t0 = tc * TW
                    cq = cq_r.next()
                    kb.dma('sp', cq, cq[:, :].rearrange("p (k t) -> p k t", k=2), UF,
                           UF[ocq:ocq + 256, t0:t0 + TW].rearrange("(k p) t -> p k t", p=128))
                    ckv = ckv_r.next()
                    kb.dma('sp', ckv, ckv[:, :], UF, UF[ockv:ockv + 128, t0:t0 + TW])
                    kr = kr_r.next()
                    kb.dma('sp', kr, kr[64:96, 0:TW], UF, UF[okr:okr + 32, t0:t0 + TW])
                    kb.dma('sp', kr, kr[64:80, TW:2 * TW], UF, UF[okr + 16:okr + 32, t0:t0 + TW])
                    kb.dma('sp', kr, kr[80:96, TW:2 * TW], UF, UF[okr:okr + 16, t0:t0 + TW])
                    cs2 = cs2_r.next()
                    kb.dma('sp', cs2, cs2[64:96, 0:TW], COS2, COS2[64:96, t0:t0 + TW])
                    kb.dma('sp', cs2, cs2[64:96, TW:2 * TW], SIN2, SIN2[64:96, t0:t0 + TW])
                    cqb = cqb_r.next()
                    cp('pool', cqb[:, :], cq[:, :], [cq], [cqb])
                    ckvb = ckvb_r.next()
                    cp('pool', ckvb[:, :], ckv[:, :], [ckv], [ckvb])
                    sqq = sqq_r.next()
                    act(sqq[:, :], cq[:, :], AF.Square, [cq], [sqq])
                    sqkv = sqkv_r.next()
                    act(sqkv[:, :], ckv[:, :], AF.Square, [ckv], [sqkv])
                    psr = kb.psum()
                    for kc in range(2):
                        mm(psr, psr[0:96, 0:TW], ONES[:, 0:96], sqq[:, kc * TW:(kc + 1) * TW], [cst, sqq], start=(kc == 0), stop=(kc == 1))
                    rq = rq_r.next()
                    act(rq[0:96, 0:TW], psr[0:96, 0:TW], AF.Sqrt, [psr], [rq], bias=1e-6, scale=1.0 / 256)
                    kb.op('dve', lambda e, rq=rq: e.reciprocal(out=rq[0:96, 0:TW], in_=rq[0:96, 0:TW]), [rq], [rq])
                    psr2 = kb.psum()
                    mm(psr2, psr2[0:64, 0:TW], ONES[:, 0:64], sqkv[:, :], [cst, sqkv])
                    rkv = rkv_r.next()
                    act(rkv[0:64, 0:TW], psr2[0:64, 0:TW], AF.Sqrt, [psr2], [rkv], bias=1e-6, scale=1.0 / 128)
                    kb.op('dve', lambda e, rkv=rkv: e.reciprocal(out=rkv[0:64, 0:TW], in_=rkv[0:64, 0:TW]), [rkv], [rkv])
                    krot = krot_r.next()
                    tt('dve', krot[64:96, 0:2 * TW], kr[64:96, 0:2 * TW], cs2[64:96, 0:2 * TW], ALU.mult, [kr, cs2], [krot])
                    tt('dve', krot[64:96, 0:TW], krot[64:96, 0:TW], krot[64:96, TW:2 * TW], ALU.add, [krot], [krot])
                    for h in range(4):
                        psq = kb.psum()
                        psqs = kb.psum()
                        for kc in range(2):
                            g_ = kc * 4 + h
                            mm(psq, psq[0:96, 0:TW], w6[:, g_, :], cqb[:, kc * TW:(kc + 1) * TW], [Wq, cqb], start=(kc == 0), stop=(kc == 1))
                        for kc in range(2):
                            g_ = kc * 4 + h
                            mm(psqs, psqs[0:96, 0:TW], ws6[:, g_, :], cqb[:, kc * TW:(kc + 1) * TW], [Wqs, cqb], start=(kc == 0), stop=(kc == 1))
                        tt('dve', QT[h][0:64, t0:t0 + TW], psq[0:64, 0:TW], rq[0:64, 0:TW], ALU.mult, [psq, rq], [QT[h]])
                        t12 = t12_r.next()
                        tt('dve', t12[64:96, 0:TW], psq[64:96, 0:TW], cs2[64:96, 0:TW], ALU.mult, [psq, cs2], [t12])
                        tt('dve', t12[64:96, TW:2 * TW], psqs[64:96, 0:TW], cs2[64:96, TW:2 * TW], ALU.mult, [psqs, cs2], [t12])
                        tt('pool', t12[64:96, 0:TW], t12[64:96, 0:TW], t12[64:96, TW:2 * TW], ALU.add, [t12], [t12])
                        tt('pool', QT[h][64:96, t0:t0 + TW], t12[64:96, 0:TW], rq[64:96, 0:TW], ALU.mult, [t12, rq], [QT[h]])
                        psk = kb.psum()
                        mm(psk, psk[0:64, 0:TW], Wkv[:, h * 128:h * 128 + 64], ckvb[:, :], [Wkv, ckvb])
                        tt('dve', KT[h][0:64, t0:t0 + TW], psk[0:64, 0:TW], rkv[0:64, 0:TW], ALU.mult, [psk, rkv], [KT[h]])
                        cp('act', KT[h][64:96, t0:t0 + TW], krot[64:96, 0:TW], [krot], [KT[h]])
                    for j in range(TW // 128):
                        tt_ = tc * (TW // 128) + j
                        psv = kb.psum()
                        mm(psv, psv[:, :], ckvb[:, j * 128:(j + 1) * 128], Wkv[:, :], [ckvb, Wkv])
                        psc = kb.psum()
                        mm(psc, psc[:, 0:1], sqkv[:, j * 128:(j + 1) * 128], ONES[:, 0:1], [sqkv, cst])
                        rc = rc_r.next()
                        act(rc[:, 0:1], psc[:, 0:1], AF.Sqrt, [psc], [rc], bias=1e-6, scale=1.0 / 128)
                        kb.op('dve', lambda e, rc=rc: e.reciprocal(out=rc[:, 0:1], in_=rc[:, 0:1]), [rc], [rc])
                        ts('dve', Vaug[:, tt_ * 260:(tt_ + 1) * 260].rearrange("p (h c) -> p h c", h=4)[:, :, 0:64],
                           psv[:, :].rearrange("p (h c) -> p h c", h=4)[:, :, 64:128], rc[:, 0:1], None, ALU.mult, ALU.bypass,
                           [psv, rc], [Vaug])
                kb.release(mk2)
                mx_r = Ring(kb, "mx", 4, [128, 16], F32)
                nrow_r = Ring(kb, "nrow", 4, [1, 128], BF16)
                PT_r = Ring(kb, "PT", 3, [128, 512], BF16)
                yd_r = Ring(kb, "yd", 2, [128, 256], F32, dma=True)
                onesb = cstb[0:1, CI_ONES:CI_ONES + 128]
                TRIb = cstb[:, CI_TRI:CI_TRI + 128]
                oacc = 0
                for i in range(32):
                    yd = yd_r.next()
                    for h in range(4):
                        q_ap = QT[h][0:96, i * 128:(i + 1) * 128]
                        nk = (i + 1) * 128
                        mx = mx_r.next()
                        ngr = (i + 4) // 4
                        for g_ in range(ngr):
                            k0 = g_ * 512
                            n = min(512, nk - k0)
                            ps = kb.psum()
                            mm(ps, ps[:, 0:n], q_ap, KT[h][0:96, k0:k0 + n], [QT[h], KT[h]])
                            kb.op('dve', lambda e, mx=mx, ps=ps, n=n, g_=g_: e.reduce_max(
                                out=mx[:, g_:g_ + 1], in_=ps[:, 0:n], axis=AX.X), [ps], [mx])
                        kb.op('dve', lambda e, mx=mx, ngr=ngr: e.reduce_max(out=mx[:, 8:9], in_=mx[:, 0:ngr], axis=AX.X), [mx], [mx])
                        ts('dve', mx[:, 9:10], mx[:, 8:9], -1.0, None, ALU.mult, ALU.bypass, [mx], [mx])
                        pst = kb.psum()
                        kb.op('pe', lambda e, pst=pst, mx=mx: e.transpose(out=pst[0:1, 0:128], in_=mx[:, 9:10], identity=ident), [mx, cst], [pst])
                        nrow = nrow_r.next()
                        cp('act', nrow[0:1, :], pst[0:1, 0:128], [pst], [nrow])
                        po = kb.psum_banks[6 + (oacc % 2)]
                        oacc += 1
                        for g_ in range(ngr):
                            kts = list(range(g_ * 4, min(g_ * 4 + 4, i + 1)))
                            ps = kb.psum()
                            for a_, kt in enumerate(kts):
                                mm(ps, ps[:, a_ * 128:(a_ + 1) * 128], KT[h][0:96, kt * 128:(kt + 1) * 128], q_ap, [KT[h], QT[h]],
                                   start=True, stop=False)
                                mm(ps, ps[:, a_ * 128:(a_ + 1) * 128], onesb, nrow[0:1, :], [cstb, nrow], start=False, stop=True)
                            PT = PT_r.next()
                            nn = len(kts) * 128
                            act(PT[:, 0:nn], ps[:, 0:nn], AF.Exp, [ps], [PT])
                            if kts[-1] == i:
                                a_ = len(kts) - 1
                                tt('pool', PT[:, a_ * 128:(a_ + 1) * 128], PT[:, a_ * 128:(a_ + 1) * 128], TRIb, ALU.mult, [PT, cstb], [PT])
                            for a_, kt in enumerate(kts):
                                mm(po, po[:, 0:65], PT[:, a_ * 128:(a_ + 1) * 128], Vaug[:, kt * 260 + h * 65:kt * 260 + (h + 1) * 65],
                                   [PT, Vaug], start=(kt == 0), stop=(kt == i))
                        kb.op('dve', lambda e, mx=mx, po=po: e.reciprocal(out=mx[:, 10:11], in_=po[:, 64:65]), [po], [mx])
                        ts('dve', yd[:, h * 64:(h + 1) * 64], po[:, 0:64], mx[:, 10:11], None, ALU.mult, ALU.bypass, [po, mx], [yd])
                    kb.dma('sp', Y, Y[i * 128:(i + 1) * 128, 768:1024], yd, yd[:, :])
                kb.NROT = 8
                kb.release(mk)

            if stop == 'MIX':
                break

            Xcur = x_in if l == 0 else X2
            if only is None or 'WO' in only:
                mk = kb.mark()
                Wo = kb.sbuf("Wo", [128, 8 * 1024], BF16, dma='sw')
                kb.dma('pool', Wo, Wo[:, :].rearrange("p (k d) -> p k d", k=8), w_out, w_out[l, :, :].rearrange("(k p) d -> p k d", p=128))
                gb1 = kb.sbuf("gb1", [128, 2048], F32, dma=True)
                kb.dma('sp', gb1, gb1[:, 0:1024], rowp, row_bc_ap(l, 'ln1_g'))
                kb.dma('sp', gb1, gb1[:, 1024:2048], rowp, row_bc_ap(l, 'ln1_b'))
                Wr = kb.sbuf("Wr", [128, 8 * 36], F32, dma=True)
                wr3 = Wr[:, :].rearrange("p (k e) -> p k e", k=8)
                kb.dma('sp', Wr, wr3[:, :, 0:4], w_rg, w_rg[l, :, :].rearrange("(k p) e -> p k e", p=128))
                kb.dma('sp', Wr, wr3[:, :, 4:36], w_re, w_re[l, :, :].rearrange("(k p) e -> p k e", p=128))
                rb = kb.sbuf("rb", [128, 36], F32, dma=True)
                kb.dma('sp', rb, rb[:, 0:4], rowp, row_bc_ap(l, 'moe_b_rg'))
                kb.dma('sp', rb, rb[:, 4:36], rowp, row_bc_ap(l, 'moe_b_re'))
                yt_r = Ring(kb, "yt", 2, [128, 1024], F32, dma=True)
                xr_r = Ring(kb, "xr", 2, [128, 1024], F32, dma=True)
                yTb_r = Ring(kb, "yTb", 2, [128, 1024], BF16)
                r_r = Ring(kb, "r1", 2, [128, 1024], F32, dma=True)
                x32_r = Ring(kb, "x32", 2, [128, 1024], F32)
                wk_r = Ring(kb, "wk1", 2, [128, 16], F32)
                rt_r = Ring(kb, "rt", 2, [128, 160], F32, dma=True)
                for tt_ in range(32):
                    yt = yt_r.next()
                    kb.dma('sp', yt, yt[:, :], Y, Y[tt_ * 128:(tt_ + 1) * 128, :])
                    xr = xr_r.next()
                    kb.dma('sp', xr, xr[:, :], Xcur, Xcur[tt_ * 128:(tt_ + 1) * 128, :])
                    yTb = yTb_r.next()
                    for g in range(2):
                        ps = kb.psum()
                        for j in range(4):
                            kc = g * 4 + j
                            kb.op('pe', lambda e, ps=ps, j=j, kc=kc, yt=yt: e.transpose(
                                out=ps[:, j * 128:(j + 1) * 128], in_=yt[:, kc * 128:(kc + 1) * 128], identity=ident), [yt, cst], [ps])
                        evac(yTb[:, g * 512:(g + 1) * 512], ps[:, :], [ps], [yTb])
                    r = r_r.next()
                    for dh in range(2):
                        ps = kb.psum()
                        for kc in range(8):
                            mm(ps, ps[:, :], yTb[:, kc * 128:(kc + 1) * 128], Wo[:, kc * 1024 + dh * 512:kc * 1024 + (dh + 1) * 512],
                               [yTb, Wo], start=(kc == 0), stop=(kc == 7))
                        stt('dve', r[:, dh * 512:(dh + 1) * 512], xr[:, dh * 512:(dh + 1) * 512], ALPHA, ps[:, :], ALU.mult, ALU.add, [xr, ps], [r])
                    wk = wk_r.next()
                    layer_norm_tile(r, gb1, 0, 1024, 1e-5, wk)
                    kb.dma('sp', X1, X1[tt_ * 128:(tt_ + 1) * 128, :], r, r[:, :])
                    x32 = x32_r.next()
                    transpose_tile(r, tt_, x32)
                    psr = kb.psum()
                    for kc in range(8):
                        mm(psr, psr[:, 0:36], x32[:, kc * 128:(kc + 1) * 128], wr3[:, kc, :], [x32, Wr], start=(kc == 0), stop=(kc == 7))
                    rt = rt_r.next()
                    tt('dve', rt[:, 0:36], psr[:, 0:36], rb[:, :], ALU.add, [psr, rb], [rt])
                    kb.op('dve', lambda e, rt=rt: e.reduce_max(out=rt[:, 116:117], in_=rt[:, 0:4], axis=AX.X), [rt], [rt])
                    ts('dve', rt[:, 40:44], rt[:, 0:4], rt[:, 116:117], None, ALU.is_ge, ALU.bypass, [rt], [rt])
                    ts('dve', rt[:, 117:118], rt[:, 116:117], -1.0, None, ALU.mult, ALU.bypass, [rt], [rt])
                    act(rt[:, 36:40], rt[:, 0:4], AF.Exp, [rt], [rt], bias=rt[:, 117:118])
                    kb.op('dve', lambda e, rt=rt: e.reduce_sum(out=rt[:, 118:119], in_=rt[:, 36:40], axis=AX.X), [rt], [rt])
                    cp('dve', rt[:, 44:76].rearrange("p (g e) -> p g e", g=4), bc3(rt[:, 40:44], [128, 4, 8], 2), [rt], [rt])
                    ts('dve', rt[:, 76:108], rt[:, 44:76], -1.0, 1e30, ALU.add, ALU.mult, [rt], [rt])
                    tt('dve', rt[:, 128:160], rt[:, 4:36], rt[:, 44:76], ALU.mult, [rt], [rt])
                    tt('dve', rt[:, 128:160], rt[:, 128:160], rt[:, 76:108], ALU.add, [rt], [rt])
                    kb.op('dve', lambda e, rt=rt: e.max(out=rt[:, 108:116], in_=rt[:, 128:160]), [rt], [rt])
                    ts('dve', rt[:, 44:76], rt[:, 128:160], rt[:, 109:110], None, ALU.is_ge, ALU.bypass, [rt], [rt])
                    ts('dve', rt[:, 119:120], rt[:, 108:109], -1.0, None, ALU.mult, ALU.bypass, [rt], [rt])
                    act(rt[:, 76:108], rt[:, 128:160], AF.Exp, [rt], [rt], bias=rt[:, 119:120])
                    act(rt[:, 120:121], rt[:, 109:110], AF.Exp, [rt], [rt], bias=rt[:, 119:120])
                    ts('dve', rt[:, 120:121], rt[:, 120:121], 1.0, None, ALU.add, ALU.bypass, [rt], [rt])
                    tt('dve', rt[:, 120:121], rt[:, 120:121], rt[:, 118:119], ALU.mult, [rt], [rt])
                    kb.op('dve', lambda e, rt=rt: e.reciprocal(out=rt[:, 121:122], in_=rt[:, 120:121]), [rt], [rt])
                    tt('dve', rt[:, 76:108], rt[:, 76:108], rt[:, 44:76], ALU.mult, [rt], [rt])
                    ts('dve', Gall[:, tt_ * 32:(tt_ + 1) * 32], rt[:, 76:108], rt[:, 121:122], None, ALU.mult, ALU.bypass, [rt], [Gall])
                    if stop == 'WO':
                        cp('dve', rt[:, 0:32], Gall[:, tt_ * 32:(tt_ + 1) * 32], [Gall], [rt])
                        kb.dma('sp', GD, GD[tt_ * 128:(tt_ + 1) * 128, :], rt, rt[:, 0:32])
                kb.release(mk)
            if stop == 'WO':
                break

            last = (l == n_layers - 1)
            if only is None or 'FF' in only:
                mk = kb.mark()
                gb2 = kb.sbuf("gb2", [128, 3072], F32, dma=True)
                kb.dma('sp', gb2, gb2[:, 0:1024], rowp, row_bc_ap(l, 'ln2_g'))
                kb.dma('sp', gb2, gb2[:, 1024:2048], rowp, row_bc_ap(l, 'ln2_b'))
                kb.dma('sp', gb2, gb2[:, 2048:3072], rowp, row_bc_ap(l, 'ple_b_gate'))
                acc = kb.sbuf("acc", [128, 8 * 1024], F32)
                Wg_r = Ring(kb, "Wg", 2, [128, 8 * 512], BF16, dma='sw')
                Wu_r = Ring(kb, "Wu", 2, [128, 8 * 512], BF16, dma='sw')
                Wd_r = Ring(kb, "Wd", 2, [128, 4 * 1024], BF16, dma='sw')
                hT_r = Ring(kb, "hT", 2, [128, 4 * 512], BF16)
                sgm_r = Ring(kb, "sgm", 3, [128, 512], F32)
                x1_r = Ring(kb, "x1t", 2, [128, 1024], F32, dma=True)
                pt_r = Ring(kb, "pt", 2, [128, 256], F32, dma=True)
                pTb_r = Ring(kb, "pTb", 2, [128, 256], BF16)
                r2_r = Ring(kb, "r2", 2, [128, 1024], F32, dma=True)
                wk_r = Ring(kb, "wk2", 2, [128, 16], F32)
                for qt in range(4):
                    hf, qo = qt // 2, (qt % 2) * 1024
                    memset('pool', acc, acc[:, :], 0.0)
                    for e_ in range(32):
                        Wg, Wu, Wd = Wg_r.next(), Wu_r.next(), Wd_r.next()
                        kb.dma('pool', Wg, Wg[:, :].rearrange("p (k f) -> p k f", k=8), w_gate,
                               w_gate[l, e_, :, :].rearrange("(k p) f -> p k f", p=128))
                        kb.dma('pool', Wu, Wu[:, :].rearrange("p (k f) -> p k f", k=8), w_up,
                               w_up[l, e_, :, :].rearrange("(k p) f -> p k f", p=128))
                        kb.dma('pool', Wd, Wd[:, :].rearrange("p (k d) -> p k d", k=4), w_down,
                               w_down[l, e_, :, :].rearrange("(k p) d -> p k d", p=128))
                        for tcq in range(2):
                            to = qo + tcq * 512
                            hT = hT_r.next()
                            for fc in range(4):
                                psg = kb.psum()
                                for kc in range(8):
                                    mm(psg, psg[:, :], Wg[:, kc * 512 + fc * 128:kc * 512 + (fc + 1) * 128],
                                       xT[hf][:, kc * 2048 + to:kc * 2048 + to + 512], [Wg, xT[hf]], start=(kc == 0), stop=(kc == 7))
                                psu = kb.psum()
                                for kc in range(8):
                                    mm(psu, psu[:, :], Wu[:, kc * 512 + fc * 128:kc * 512 + (fc + 1) * 128],
                                       xT[hf][:, kc * 2048 + to:kc * 2048 + to + 512], [Wu, xT[hf]], start=(kc == 0), stop=(kc == 7))
                                sgm = sgm_r.next()
                                act(sgm[:, :], psg[:, :], AF.Silu, [psg], [sgm])
                                tt('dve', hT[:, fc * 512:(fc + 1) * 512], sgm[:, :], psu[:, :], ALU.mult, [sgm, psu], [hT])
                            for tl in range(4):
                                ti = tcq * 4 + tl
                                tt_ = qt * 8 + ti
                                for dh in range(2):
                                    psd = kb.psum()
                                    for fc in range(4):
                                        mm(psd, psd[:, :], hT[:, fc * 512 + tl * 128:fc * 512 + (tl + 1) * 128],
                                           Wd[:, fc * 1024 + dh * 512:fc * 1024 + (dh + 1) * 512], [hT, Wd], start=(fc == 0), stop=(fc == 3))
                                    a_ = acc[:, ti * 1024 + dh * 512:ti * 1024 + (dh + 1) * 512]
                                    stt('dve', a_, psd[:, :], Gall[:, tt_ * 32 + e_:tt_ * 32 + e_ + 1], a_, ALU.mult, ALU.add, [psd, Gall, acc], [acc])
                    Wpg, Wpg2, Wp = Wg_r.next(), Wu_r.next(), Wd_r.next()
                    kb.dma('pool', Wpg, Wpg[:, :].rearrange("p (k d) -> p k d", k=4), ple_wg,
                           ple_wg[l, 0:512, :].rearrange("(k p) d -> p k d", p=128))
                    kb.dma('pool', Wpg2, Wpg2[:, :].rearrange("p (k d) -> p k d", k=4), ple_wg,
                           ple_wg[l, 512:1024, :].rearrange("(k p) d -> p k d", p=128))
                    kb.dma('pool', Wp, Wp[:, 0:2048].rearrange("p (k d) -> p k d", k=2), ple_w,
                           ple_w[l, :, :].rearrange("(k p) d -> p k d", p=128))
                    for ti in range(8):
                        tt_ = qt * 8 + ti
                        tl = qo + ti * 128
                        x1t = x1_r.next()
                        kb.dma('sp', x1t, x1t[:, :], X1, X1[tt_ * 128:(tt_ + 1) * 128, :])
                        pt = pt_r.next()
                        kb.dma('sp', pt, pt[:, :], p_in, p_in[l, tt_ * 128:(tt_ + 1) * 128, :])
                        pst = kb.psum()
                        for j in range(2):
                            kb.op('pe', lambda e, pst=pst, j=j, pt=pt: e.transpose(
                                out=pst[:, j * 128:(j + 1) * 128], in_=pt[:, j * 128:(j + 1) * 128], identity=ident), [pt, cst], [pst])
                        pTb = pTb_r.next()
                        cp('act', pTb[:, :], pst[:, 0:256], [pst], [pTb])
                        r2 = r2_r.next()
                        for dh in range(2):
                            psq = kb.psum()
                            for kc in range(8):
                                W_ = Wpg if kc < 4 else Wpg2
                                k4 = kc % 4
                                mm(psq, psq[:, :], xT[hf][:, kc * 2048 + tl:kc * 2048 + tl + 128],
                                   W_[:, k4 * 1024 + dh * 512:k4 * 1024 + (dh + 1) * 512], [xT[hf], W_], start=(kc == 0), stop=(kc == 7))
                            psp = kb.psum()
                            for kc in range(2):
                                mm(psp, psp[:, :], pTb[:, kc * 128:(kc + 1) * 128], Wp[:, kc * 1024 + dh * 512:kc * 1024 + (dh + 1) * 512],
                                   [pTb, Wp], start=(kc == 0), stop=(kc == 1))
                            sgm = sgm_r.next()
                            tt('dve', sgm[:, :], psq[:, :], gb2[:, 2048 + dh * 512:2048 + (dh + 1) * 512], ALU.add, [psq, gb2], [sgm])
                            act(sgm[:, :], sgm[:, :], AF.Sigmoid, [sgm], [sgm])
                            tt('dve', sgm[:, :], sgm[:, :], psp[:, :], ALU.mult, [sgm, psp], [sgm])
                            a_ = acc[:, ti * 1024 + dh * 512:ti * 1024 + (dh + 1) * 512]
                            stt('dve', r2[:, dh * 512:(dh + 1) * 512], x1t[:, dh * 512:(dh + 1) * 512], ALPHA, a_, ALU.mult, ALU.add, [x1t, acc], [r2])
                            tt('pool', r2[:, dh * 512:(dh + 1) * 512], r2[:, dh * 512:(dh + 1) * 512], sgm[:, :], ALU.add, [r2, sgm], [r2])
                        wk = wk_r.next()
                        layer_norm_tile(r2, gb2, 0, 1024, 1e-5, wk)
                        dst = OUT if last else X2
                        kb.dma('sp', dst, dst[tt_ * 128:(tt_ + 1) * 128, :], r2, r2[:, :])
                        if not last:
                            transpose_tile(r2, tt_)
                kb.release(mk)
            if stop == 'FF':
                break

        outs = {'P1': [UF, UT], 'MIX': [Y], 'WO': [X1, GD]}.get(stop, [OUT])
        kb.wait_all_writes('sp', outs)
        stats = kb.emit()
        print("instr stats", stats)
    return nc


_NC_CACHE = {}


def kernel(**inputs):
    inp = {k: np.asarray(v) for k, v in inputs.items()}
    n = 8
    if 'nc' not in _NC_CACHE:
        _NC_CACHE['nc'] = build(n_layers=L)
    nc = _NC_CACHE['nc']
    rowp, colp = pack_params(inp)
    shared = {'consts': make_consts(), 'rowp': rowp, 'colp': colp}
    for nm in IN_NAMES:
        shared[nm] = np.ascontiguousarray(inp[nm], dtype=np.float32)
    in_maps = [core_inputs(inp, b, shared) for b in range(n)]
    res = run_bass_kernel_spmd(nc, in_maps, core_ids=list(range(n)))
    out = np.stack([np.asarray(res.results[b]['out'], dtype=np.float32) for b in range(n)], axis=0)
    return out
```

```python
import numpy as np
import concourse.bass as bass
import concourse.mybir as mybir
from contextlib import ExitStack
from concourse.bass_utils import run_bass_kernel_spmd

F32 = mybir.dt.float32
BF16 = mybir.dt.bfloat16
I32 = mybir.dt.int32
AF = mybir.ActivationFunctionType
ALU = mybir.AluOpType
AX = mybir.AxisListType
ENGS = ['pe', 'dve', 'act', 'pool', 'sp']


class Buf:
    def __init__(self, name, h, dsem=None, dkey=None):
        self.name = name
        self.h = h
        self.last_write = None
        self.reads = {}
        self.dsem = dsem
        self.dkey = dkey
        self.dcount = 0
        self.wtoks = {}
        self.is_dram = False

    def __getitem__(self, key):
        return self.h[key]


class KB:
    def __init__(self, nc, es):
        self.nc = nc
        self.es = es
        self.ins = {e: [] for e in ENGS}
        self.waited = {e: {} for e in ENGS}
        self.miles = {}
        self.psem = {}
        self.epoch = -1
        self.new_epoch()
        self.dsems = {}
        self.nbuf = 0
        self.psum_banks = []
        self.psum_rr = 0

    def new_epoch(self):
        self.epoch += 1
        self.ekey = {}
        for e in ENGS:
            k = "%s#%d" % (e, self.epoch)
            self.ekey[e] = k
            self.psem[k] = self.es.enter_context(self.nc.semaphore("ps_%s_%d" % (e, self.epoch)))
            self.miles[k] = set()

    def _dsem(self, name):
        s = self.es.enter_context(self.nc.semaphore("d_" + name))
        key = len(self.dsems)
        self.dsems[key] = s
        return s, key

    ARENA = 52500

    def _arena_init(self):
        self.big = self.es.enter_context(self.nc.sbuf_tensor("arena", [128, self.ARENA], F32))
        self.aptr = 0
        self.live = []
        self.inherit = {}
        self.dpool = []
        self.dpool_sw = []

    def sbuf(self, name, shape, dt, dma=False):
        if not hasattr(self, 'big'):
            self._arena_init()
        P, Fd = shape
        esz = 2 if dt == BF16 else 4
        ncol = (Fd * esz + 3) // 4
        ncol = (ncol + 7) // 8 * 8
        assert self.aptr + ncol <= self.ARENA, ("SBUF arena overflow", name, self.aptr, ncol)
        v = self.big[0:P, self.aptr:self.aptr + ncol]
        if dt != F32:
            v = v.bitcast(dt)
        v = v[:, 0:Fd]
        self.aptr += ncol
        b = Buf(name, v)
        if dma:
            b.sw = (dma == 'sw')
            pool = self.dpool_sw if b.sw else self.dpool
            if pool:
                b.ds = pool.pop()
            else:
                s, k = self._dsem(name)
                b.ds = [s, k, 0]
            b.dsem, b.dkey, b.dcount = b.ds
            b.dbase = b.dcount * 16
        b.reads = dict(self.inherit)
        self.live.append(b)
        return b

    def retire(self, b):
        toks = list(b.reads.values()) + list(b.wtoks.values())
        if b.last_write is not None:
            toks.append(b.last_write)
        for t in toks:
            k = t[:2]
            if k not in self.inherit or self.inherit[k][2] < t[2]:
                self.inherit[k] = t

    def mark(self):
        if not hasattr(self, 'big'):
            self._arena_init()
        return (self.aptr, len(self.live))

    def release(self, mk):
        aptr, nl = mk
        for b in self.live[nl:]:
            toks = list(b.reads.values())
            if b.last_write is not None:
                toks.append(b.last_write)
            for t in toks:
                k = t[:2]
                if k not in self.inherit or self.inherit[k][2] < t[2]:
                    self.inherit[k] = t
            if b.dsem is not None:
                b.ds[2] = b.dcount
                (self.dpool_sw if b.sw else self.dpool).append(b.ds)
        del self.live[nl:]
        self.aptr = aptr

    def psum_init(self):
        for i in range(8):
            h = self.es.enter_context(self.nc.psum_tensor("psb%d" % i, [128, 512], F32))
            self.psum_banks.append(Buf("psb%d" % i, h))

    NROT = 8
    pctx = None

    def psum(self):
        if self.pctx is not None:
            c = self.pctx
            b = self.psum_banks[c['banks'][c['i'] % len(c['banks'])]]
            c['i'] += 1
            return b
        b = self.psum_banks[self.psum_rr % self.NROT]
        self.psum_rr += 1
        return b

    def dram(self, name, shape, dt, kind=None):
        if kind:
            h = self.nc.dram_tensor(name, list(shape), dt, kind=kind)
        else:
            h = self.nc.dram_tensor(name, list(shape), dt)
        b = Buf(name, h)
        b.is_dram = True
        return b

    def _deps(self, eng, reads, writes, skip_dkey=None, skip_base=0):
        toks = []
        for b in reads:
            if b.last_write is not None:
                toks.append(b.last_write)
            toks.extend(b.wtoks.values())
        for b in writes:
            if b.last_write is not None:
                toks.append(b.last_write)
            toks.extend(b.wtoks.values())
            toks.extend(b.reads.values())
        need = []
        for t in toks:
            key = t[:2]
            val = t[2]
            if t[0] == 'e' and eng == 'pe' and t[1].startswith('pe#'):
                continue
            if t[0] == 'd' and skip_dkey is not None and t[1] == skip_dkey and val > skip_base:
                continue
            if self.waited[eng].get(key, -1) >= val:
                continue
            self.waited[eng][key] = val
            need.append(t)
            if t[0] == 'e':
                self.miles[t[1]].add(val)
        return need

    def op(self, eng, fn, reads=(), writes=()):
        need = self._deps(eng, reads, writes)
        idx = len(self.ins[eng])
        key = self.ekey[eng]
        self.ins[eng].append((fn, need, None, key))
        tok = ('e', key, idx)
        for b in reads:
            b.reads[('e', key)] = tok
        for b in writes:
            b.last_write = tok
            b.reads = {}
        return tok

    def dma(self, eng, out_buf, out_ap, in_buf, in_ap):
        if out_buf.is_dram:
            sb = in_buf
            assert not in_buf.is_dram
        else:
            sb = out_buf
        assert sb.dsem is not None, sb.name
        assert (eng == 'pool') == bool(getattr(sb, 'sw', False)), ("dma queue/sem kind mismatch", sb.name, eng)
        need = self._deps(eng, [in_buf], [out_buf], skip_dkey=sb.dkey, skip_base=getattr(sb, 'dbase', 0))
        sb.dcount += 1
        tok = ('d', sb.dkey, 16 * sb.dcount)
        self.ins[eng].append((lambda e: e.dma_start(out=out_ap, in_=in_ap), need, sb.dsem, None))
        in_buf.reads[('d', sb.dkey)] = tok
        if out_buf.is_dram:
            out_buf.wtoks[sb.dkey] = tok
        else:
            out_buf.last_write = tok
        return tok

    def wait_all_writes(self, eng, bufs):
        need = self._deps(eng, bufs, [])
        self.ins[eng].append((None, need, None, None))

    def emit(self):
        nc = self.nc
        ranks = {k: {idx: r + 1 for r, idx in enumerate(sorted(v))} for k, v in self.miles.items()}
        kb = self

        def run(name, eng):
            for idx, (fn, need, dsem, key) in enumerate(kb.ins[name]):
                for t in need:
                    if t[0] == 'e':
                        eng.wait_ge(kb.psem[t[1]], ranks[t[1]][t[2]])
                    else:
                        eng.wait_ge(kb.dsems[t[1]], t[2])
                if fn is None:
                    continue
                ins = fn(eng)
                if dsem is not None:
                    ins.then_inc(dsem, 16)
                elif key is not None and idx in ranks[key]:
                    ins.then_inc(kb.psem[key], 1)

        with nc.Block() as block:
            @block.tensor
            def _(e):
                run('pe', e)

            @block.vector
            def _(e):
                run('dve', e)

            @block.scalar
            def _(e):
                run('act', e)

            @block.gpsimd
            def _(e):
                run('pool', e)

            @block.sync
            def _(e):
                run('sp', e)
        return {e: (len(self.ins[e]), max(len(v) for k, v in ranks.items() if k.startswith(e + '#'))) for e in ENGS}


S = 4096
D = 1024
DIN = 3128
L = 4
ALPHA = 8.0 ** 0.25

A0, B0, C0, D0 = 0, 896, 1680, 2712
UT0, UTN = 1024, 1688
UT_BK, UT_BV, UT_BG = 0, 128, 400
UT_CV, UT_CO, UT_CG = 1168, 1424, 1680
FM = [('A', 0, 896), ('Bq', 896, 128), ('Bk', 1024, 128), ('Bad', 1408, 16),
      ('Cqk', 1680, 512), ('Dcq', 2712, 256), ('Dckv', 2968, 128), ('Dkr', 3096, 32)]
FMOFF = {}
_r = 0
for _n, _c, _w in FM:
    FMOFF[_n] = _r
    _r += ((_w + 127) // 128) * 128
UFN = _r

CI_ID, CI_TRI, CI_TRIS, CI_ONES, CI_HM, CI_HMROW = 0, 128, 256, 384, 512, 520
CI_HM2 = CI_HMROW + 512
CI_FREQ = CI_HM2 + 2
CI_SGN = CI_HM2 + 3
CI_TRISL = CI_HM2 + 8
CI_BLK64 = CI_TRISL + 128
NCONST = CI_BLK64 + 128

RP = {}
_o = 0
for _n, _w in [('ln1_g', 1024), ('ln1_b', 1024), ('ln2_g', 1024), ('ln2_b', 1024), ('ple_b_gate', 1024),
               ('gla_norm_g', 256), ('mlstm_norm_g', 256), ('rwkv_gn_g', 256), ('rwkv_gn_b', 256),
               ('moe_b_rg', 4), ('moe_b_re', 32), ('mlstm_i_b', 4), ('mlstm_f_b', 4)]:
    RP[_n] = (_o, _w)
    _o += _w
NROW = (_o + 7) // 8 * 8
CP = {'rwkv_mu': (0, 7), 'rwkv_w0': (7, 2), 'rwkv_a0': (9, 2), 'rwkv_k_k': (11, 2), 'rwkv_k_a': (13, 2),
      'rwkv_r_k': (15, 2), 'conv_w': (17, 16), 'conv_b': (33, 4), 'mla_q_norm_g': (37, 2), 'mla_kv_norm_g': (39, 1)}
NCOL = 40


def make_consts():
    c = np.zeros((128, NCONST), np.float32)
    c[:, CI_ID:CI_ID + 128] = np.eye(128, dtype=np.float32)
    j = np.arange(128)[:, None]
    i = np.arange(128)[None, :]
    c[:, CI_TRI:CI_TRI + 128] = (j <= i)
    c[:, CI_TRIS:CI_TRIS + 128] = (j > i)
    c[:, CI_ONES:CI_ONES + 128] = 1.0
    c[:, CI_TRISL:CI_TRISL + 128] = (j < i)
    c[0:64, CI_BLK64:CI_BLK64 + 64] = 1.0
    c[64:128, CI_BLK64 + 64:CI_BLK64 + 128] = 1.0
    for h in range(4):
        c[h * 32:(h + 1) * 32, CI_HM + h] = 1.0
        c[:, CI_HMROW + h * 128 + h * 32:CI_HMROW + h * 128 + (h + 1) * 32] = 1.0
    c[0:64, CI_HM2] = 1.0
    inv = (10000.0 ** (-np.arange(0, 32, 2, dtype=np.float32) / 32)).astype(np.float32)
    c[64:96, CI_FREQ] = np.concatenate([inv, inv])
    c[64:80, CI_SGN] = -1.0
    c[80:96, CI_SGN] = 1.0
    c[64:128, CI_HM2 + 1] = 1.0
    return c


def pack_params(inp):
    rowp = np.zeros((L, NROW), np.float32)
    for n, (o, w) in RP.items():
        rowp[:, o:o + w] = np.asarray(inp[n]).reshape(L, w)
    colp = np.zeros((L, 128, NCOL), np.float32)

    def put(name, arr, ncols):
        o, w = CP[name]
        assert w == ncols
        colp[:, :, o:o + w] = np.asarray(arr).reshape(L, ncols, 128).transpose(0, 2, 1)
    put('rwkv_mu', inp['rwkv_mu'], 7)
    put('rwkv_w0', inp['rwkv_w0'], 2)
    put('rwkv_a0', inp['rwkv_a0'], 2)
    put('rwkv_k_k', inp['rwkv_k_k'], 2)
    put('rwkv_k_a', inp['rwkv_k_a'], 2)
    put('rwkv_r_k', np.asarray(inp['rwkv_r_k']).reshape(L, 256), 2)
    put('conv_w', np.asarray(inp['mlstm_conv_w']).reshape(L, 4 * 512), 16)
    put('conv_b', inp['mlstm_conv_b'], 4)
    put('mla_q_norm_g', inp['mla_q_norm_g'], 2)
    put('mla_kv_norm_g', inp['mla_kv_norm_g'], 1)
    return rowp, colp


IN_NAMES = ['w_in', 'gla_alpha_up', 'gla_alpha_b', 'mla_w_uq', 'mla_w_ukv', 'rwkv_w_up', 'rwkv_a_up', 'rwkv_g_up',
            'w_out', 'moe_w_rg', 'moe_w_re', 'moe_w_gate', 'moe_w_up', 'moe_w_down', 'ple_w_gate', 'ple_w']


def core_inputs(inp, b, shared=None):
    if shared is None:
        rowp, colp = pack_params(inp)
        shared = {'consts': make_consts(), 'rowp': rowp, 'colp': colp}
        for n in IN_NAMES:
            shared[n] = np.ascontiguousarray(inp[n], dtype=np.float32)
    m = dict(shared)
    m['x'] = np.ascontiguousarray(inp['x'][b])
    m['p'] = np.ascontiguousarray(inp['p'][:, b])
    m['positions'] = np.ascontiguousarray(inp['positions'][b:b + 1]).astype(np.int32)
    return m


class Ring:
    def __init__(self, kb, name, n, shape, dt, dma=False):
        self.bufs = [kb.sbuf("%s%d" % (name, i), shape, dt, dma=dma) for i in range(n)]
        self.i = 0

    def next(self):
        b = self.bufs[self.i % len(self.bufs)]
        self.i += 1
        return b


def bc3(ap, shape, axis):
    return ap.unsqueeze(axis).to_broadcast(list(shape))


def build(n_layers=L, stop=None, only=None):
    nc = bass.Bass("TRN2", target_bir_lowering=False)
    with ExitStack() as es:
        kb = KB(nc, es)
        kb.psum_init()
        x_in = kb.dram("x", [S, D], F32, "ExternalInput")
        w_in = kb.dram("w_in", [L, D, DIN], F32, "ExternalInput")
        consts = kb.dram("consts", [128, NCONST], F32, "ExternalInput")
        rowp = kb.dram("rowp", [L, NROW], F32, "ExternalInput")
        colp = kb.dram("colp", [L, 128, NCOL], F32, "ExternalInput")
        gla_aup = kb.dram("gla_alpha_up", [L, 16, 128], F32, "ExternalInput")
        gla_ab = kb.dram("gla_alpha_b", [L, 128], F32, "ExternalInput")
        pos_in = kb.dram("positions", [1, S], I32, "ExternalInput")
        rw_wup = kb.dram("rwkv_w_up", [L, 32, 256], F32, "ExternalInput")
        w_out = kb.dram("w_out", [L, D, D], F32, "ExternalInput")
        w_rg = kb.dram("moe_w_rg", [L, D, 4], F32, "ExternalInput")
        w_re = kb.dram("moe_w_re", [L, D, 32], F32, "ExternalInput")
        w_gate = kb.dram("moe_w_gate", [L, 32, D, 512], F32, "ExternalInput")
        w_up = kb.dram("moe_w_up", [L, 32, D, 512], F32, "ExternalInput")
        w_down = kb.dram("moe_w_down", [L, 32, 512, D], F32, "ExternalInput")
        ple_wg = kb.dram("ple_w_gate", [L, D, D], F32, "ExternalInput")
        ple_w = kb.dram("ple_w", [L, 256, D], F32, "ExternalInput")
        p_in = kb.dram("p", [L, S, 256], F32, "ExternalInput")
        X1 = kb.dram("X1", [S, D], F32, "ExternalOutput" if stop == 'WO' else None)
        X2 = kb.dram("X2", [S, D], F32)
        OUT = kb.dram("out", [S, D], F32, "ExternalOutput" if stop in (None, 'FF') else None)
        GD = kb.dram("GD", [S, 32], F32, "ExternalOutput" if stop == 'WO' else None)
        rw_aup = kb.dram("rwkv_a_up", [L, 32, 256], F32, "ExternalInput")
        rw_gup = kb.dram("rwkv_g_up", [L, 64, 256], F32, "ExternalInput")
        w_uq = kb.dram("mla_w_uq", [L, 256, 384], F32, "ExternalInput")
        w_ukv = kb.dram("mla_w_ukv", [L, 128, 512], F32, "ExternalInput")
        COS2 = kb.dram("COS2", [128, S], F32)
        SIN2 = kb.dram("SIN2", [128, S], F32)
        dbg = stop is not None
        UF = kb.dram("UF", [UFN, S], F32, "ExternalOutput" if stop == 'P1' else None)
        UT = kb.dram("UT", [S, UTN], F32, "ExternalOutput" if stop == 'P1' else None)
        Y = kb.dram("Y", [S, D], F32, "ExternalOutput" if stop == 'MIX' else None)

        def row_bc_ap(l, name, n0=0, n=None):
            o, w = RP[name]
            if n is None:
                n = w
            return bass.AP(rowp.h, l * NROW + o + n0, [[0, 128], [1, n]])

        def mm(ps, out_ap, lhsT, rhs, reads, start=True, stop=True):
            kb.op('pe', lambda e: e.matmul(out_ap, lhsT=lhsT, rhs=rhs, start=start, stop=stop), reads, [ps])

        def act(out_ap, in_ap, func, reads, writes, bias=0.0, scale=1.0):
            kb.op('act', lambda e: e.activation(out=out_ap, in_=in_ap, func=func, bias=bias, scale=scale), reads, writes)

        def tt(eng, out_ap, in0, in1, op, reads, writes):
            kb.op(eng, lambda e: e.tensor_tensor(out=out_ap, in0=in0, in1=in1, op=op), reads, writes)

        def ts(eng, out_ap, in0, s1, s2, op0, op1, reads, writes):
            kb.op(eng, lambda e: e.tensor_scalar(out=out_ap, in0=in0, scalar1=s1, scalar2=s2, op0=op0, op1=op1), reads, writes)

        def stt(eng, out_ap, in0, scalar, in1, op0, op1, reads, writes):
            eng = 'dve'
            kb.op(eng, lambda e: e.scalar_tensor_tensor(out=out_ap, in0=in0, scalar=scalar, in1=in1, op0=op0, op1=op1), reads, writes)

        def cp(eng, out_ap, in_ap, reads, writes):
            if eng == 'act':
                kb.op('act', lambda e: e.copy(out=out_ap, in_=in_ap), reads, writes)
            else:
                kb.op(eng, lambda e: e.tensor_copy(out=out_ap, in_=in_ap), reads, writes)

        def rsqrt(buf, out_ap, in_ap, scale, eps):
            act(out_ap, in_ap, AF.Sqrt, [buf], [buf], bias=eps, scale=scale)
            kb.op('dve', lambda e: e.reciprocal(out=out_ap, in_=out_ap), [buf], [buf])

        def memset(eng, buf, ap, val):
            kb.op(eng, lambda e: e.memset(ap, val), [], [buf])

        cst = kb.sbuf("cst", [128, NCONST], F32, dma=True)
        kb.dma('sp', cst, cst[:, :], consts, consts[:, :])
        cstb = kb.sbuf("cstb", [128, NCONST], BF16)
        cp('dve', cstb[:, :], cst[:, :], [cst], [cstb])
        ident = cst[:, CI_ID:CI_ID + 128]

        Gall = kb.sbuf("Gall", [128, 32 * 32], F32)
        xT_base = kb.aptr
        xT = [kb.sbuf("xT%d" % h, [128, 8 * 2048], BF16) for h in range(2)]
        cnt = [0]

        def evac(out_ap, in_ap, reads, writes):
            cnt[0] += 1
            cp('act' if cnt[0] % 2 else 'dve', out_ap, in_ap, reads, writes)

        def transpose_to_xT(src, tt_):
            h, tl = tt_ // 16, (tt_ % 16) * 128
            for g in range(2):
                ps = kb.psum()
                for j in range(4):
                    kc = g * 4 + j
                    kb.op('pe', lambda e, ps=ps, j=j, kc=kc: e.transpose(
                        out=ps[:, j * 128:(j + 1) * 128], in_=src[:, kc * 128:(kc + 1) * 128],
                        identity=ident), [src, cst], [ps])
                dst = xT[h][:, :].rearrange("p (k t) -> p k t", k=8)[:, g * 4:(g + 1) * 4, tl:tl + 128]
                srcp = ps[:, :].rearrange("p (k t) -> p k t", k=4)
                evac(dst, srcp, [ps], [xT[h]])

        def transpose_tile(src, tt_, x32=None):
            h, tl = tt_ // 16, (tt_ % 16) * 128
            for g in range(2):
                ps = kb.psum()
                for j in range(4):
                    kc = g * 4 + j
                    kb.op('pe', lambda e, ps=ps, j=j, kc=kc: e.transpose(
                        out=ps[:, j * 128:(j + 1) * 128], in_=src[:, kc * 128:(kc + 1) * 128],
                        identity=ident), [src, cst], [ps])
                dst = xT[h][:, :].rearrange("p (k t) -> p k t", k=8)[:, g * 4:(g + 1) * 4, tl:tl + 128]
                if x32 is None:
                    evac(dst, ps[:, :].rearrange("p (k t) -> p k t", k=4), [ps], [xT[h]])
                else:
                    cp('act', x32[:, g * 512:(g + 1) * 512], ps[:, :], [ps], [x32])
                    cp('pool', dst, x32[:, g * 512:(g + 1) * 512].rearrange("p (k t) -> p k t", k=4), [x32], [xT[h]])

        def layer_norm_tile(r, gb, o_g, o_b, eps, wk):
            for hf_ in range(2):
                kb.op('dve', lambda e, hf_=hf_: e.bn_stats(out=wk[:, hf_ * 6:(hf_ + 1) * 6], in_=r[:, hf_ * 512:(hf_ + 1) * 512]), [r], [wk])
            kb.op('dve', lambda e: e.bn_aggr(out=wk[:, 12:14], in_=wk[:, 0:12]), [wk], [wk])
            rsqrt(wk, wk[:, 14:15], wk[:, 13:14], 1.0, eps)
            ts('dve', r[:, :], r[:, :], wk[:, 12:13], wk[:, 14:15], ALU.subtract, ALU.mult, [r, wk], [r])
            tt('pool', r[:, :], r[:, :], gb[:, o_g:o_g + 1024], ALU.mult, [r, gb], [r])
            tt('pool', r[:, :], r[:, :], gb[:, o_b:o_b + 1024], ALU.add, [r, gb], [r])

        mk = kb.mark()
        xt_ring = Ring(kb, "xt", 3, [128, D], F32, dma=True)
        for tt_ in range(32):
            xt = xt_ring.next()
            kb.dma('sp', xt, xt[:, :], x_in, x_in[tt_ * 128:(tt_ + 1) * 128, :])
            transpose_to_xT(xt, tt_)
        kb.release(mk)

        mk = kb.mark()
        posi = kb.sbuf("posi", [128, 1024], I32, dma=True)
        ang = kb.sbuf("ang", [128, 1024], F32)
        kf = kb.sbuf("kf", [128, 1024], F32)
        cs_r = Ring(kb, "cs", 2, [128, 1024], F32, dma=True)
        PI = float(np.pi)
        for q4 in range(4):
            kb.dma('sp', posi, posi[:, :], pos_in, bass.AP(pos_in.h, q4 * 1024, [[0, 128], [1, 1024]]))
            cp('dve', ang[:, :], posi[:, :], [posi], [ang])
            ts('dve', ang[:, :], ang[:, :], cst[:, CI_FREQ:CI_FREQ + 1], None, ALU.mult, ALU.bypass, [ang, cst], [ang])
            for which, shift, dst in ((0, 0.5 * PI, COS2), (1, 0.0, SIN2)):
                t_ = cs_r.next()
                ts('dve', t_[:, :], ang[:, :], shift, 1.0 / (2 * PI), ALU.add, ALU.mult, [ang], [t_])
                cp('dve', posi[:, :], t_[:, :], [t_], [posi])
                cp('dve', kf[:, :], posi[:, :], [posi], [kf])
                ts('dve', t_[:, :], ang[:, :], shift, None, ALU.add, ALU.bypass, [ang], [t_])
                stt('dve', t_[:, :], kf[:, :], -2 * PI, t_[:, :], ALU.mult, ALU.add, [kf, t_], [t_])
                ts('dve', kf[:, :], t_[:, :], PI, -2 * PI, ALU.is_gt, ALU.mult, [t_], [kf])
                tt('dve', t_[:, :], t_[:, :], kf[:, :], ALU.add, [t_, kf], [t_])
                ts('dve', kf[:, :], t_[:, :], -PI, 2 * PI, ALU.is_lt, ALU.mult, [t_], [kf])
                tt('dve', t_[:, :], t_[:, :], kf[:, :], ALU.add, [t_, kf], [t_])
                act(t_[:, :], t_[:, :], AF.Sin, [t_], [t_])
                if which == 1:
                    ts('dve', t_[:, :], t_[:, :], cst[:, CI_SGN:CI_SGN + 1], None, ALU.mult, ALU.bypass, [t_, cst], [t_])
                kb.dma('sp', dst, dst[:, q4 * 1024:(q4 + 1) * 1024], t_, t_[:, :])
        kb.release(mk)

        for l in range(n_layers):
            if l > 0:
                kb.new_epoch()
            if only is None or 'P1' in only:
                mk = kb.mark()
                Win = kb.sbuf("Win", [128, 8 * DIN], BF16, dma='sw')
                st_ring = Ring(kb, "stg", 4, [128, 512], F32, dma=True)
                for kc in range(8):
                    kb.dma('pool', Win, Win[:, kc * DIN:(kc + 1) * DIN], w_in, w_in[l, kc * 128:(kc + 1) * 128, :])
                for name, c0, w in FM:
                    for ci in range((w + 127) // 128):
                        cc = c0 + ci * 128
                        n = min(128, c0 + w - cc)
                        row0 = FMOFF[name] + ci * 128
                        for tc in range(8):
                            h, tl = tc // 4, (tc % 4) * 512
                            ps = kb.psum()
                            for kc in range(8):
                                mm(ps, ps[0:n, :], Win[:, kc * DIN + cc:kc * DIN + cc + n],
                                   xT[h][:, kc * 2048 + tl:kc * 2048 + tl + 512], [Win, xT[h]],
                                   start=(kc == 0), stop=(kc == 7))
                            stg = st_ring.next()
                            evac(stg[0:n, :], ps[0:n, :], [ps], [stg])
                            kb.dma('sp', UF, UF[row0:row0 + n, tc * 512:(tc + 1) * 512], stg, stg[0:n, :])
                for tt_ in range(32):
                    h, tl = tt_ // 16, (tt_ % 16) * 128
                    for g in range(4):
                        g0 = g * 512
                        n = min(512, UTN - g0)
                        ps = kb.psum()
                        for kc in range(8):
                            mm(ps, ps[:, 0:n], xT[h][:, kc * 2048 + tl:kc * 2048 + tl + 128],
                               Win[:, kc * DIN + UT0 + g0:kc * DIN + UT0 + g0 + n], [Win, xT[h]],
                               start=(kc == 0), stop=(kc == 7))
                        stg = st_ring.next()
                        evac(stg[:, 0:n], ps[:, 0:n], [ps], [stg])
                        kb.dma('sp', UT, UT[tt_ * 128:(tt_ + 1) * 128, g0:g0 + n], stg, stg[:, 0:n])
                kb.release(mk)
            if stop == 'P1':
                break


            def genA():
                cpl = kb.sbuf("cplA", [128, NCOL], F32, dma=True)
                kb.dma('sp', cpl, cpl[:, :], colp, colp[l, :, :])
                omu, ow0, oa0, okk, oka, ork = [CP[n_][0] for n_ in ('rwkv_mu', 'rwkv_w0', 'rwkv_a0', 'rwkv_k_k', 'rwkv_k_a', 'rwkv_r_k')]
                gnb = kb.sbuf("gnb", [128, 512], F32, dma=True)
                kb.dma('sp', gnb, gnb[:, 0:256], rowp, row_bc_ap(l, 'rwkv_gn_g'))
                kb.dma('sp', gnb, gnb[:, 256:512], rowp, row_bc_ap(l, 'rwkv_gn_b'))
                WP = kb.sbuf("WP", [128, 768], BF16, dma='sw')
                memset('pool', WP, WP[:, :], 0.0)
                kb.dma('pool', WP, WP[0:32, 0:256], rw_wup, rw_wup[l, :, :])
                kb.dma('pool', WP, WP[32:64, 256:512], rw_aup, rw_aup[l, :, :])
                kb.dma('pool', WP, WP[64:128, 512:768], rw_gup, rw_gup[l, :, :])
                ST32 = [kb.sbuf("ST32_%d" % p_, [128, 64], F32) for p_ in range(2)]
                STb = [kb.sbuf("STb_%d" % p_, [128, 64], BF16) for p_ in range(2)]
                for p_ in range(2):
                    memset('dve', ST32[p_], ST32[p_][:, :], 0.0)
                    memset('dve', STb[p_], STb[p_][:, :], 0.0)
                BKM_r = Ring(kb, "BKM", 2, [128, 1024], BF16)
                for b_ in BKM_r.bufs:
                    memset('pool', b_, b_[:, :], 0.0)
                raw_r = Ring(kb, "rawA", 2, [128, 7 * 129], F32, dma=True)
                us_r = Ring(kb, "us", 2, [128, 896], F32)
                dd_r = Ring(kb, "dd", 1, [128, 896], F32)
                T6_r = Ring(kb, "T6", 2, [128, 128], BF16)
                f_r = Ring(kb, "fA", 2, [128, 4096], F32)
                bA_r = Ring(kb, "bA", 2, [128, 3072], BF16)
                mats_r = Ring(kb, "mats", 2, [128, 2560], BF16)
                Mt_r = Ring(kb, "Mt", 2, [128, 7 * 512], BF16)
                X_r = Ring(kb, "XA", 2, [128, 256], F32)
                Xb_r = Ring(kb, "XbA", 3, [128, 256], BF16)
                ep_r = Ring(kb, "epA", 2, [128, 1024], F32)
                sm_r = Ring(kb, "smA", 2, [128, 32], F32)
                ya_r = Ring(kb, "ya", 2, [128, 256], F32, dma=True)
                oA = FMOFF['A']
                ONES = cst[:, CI_ONES:CI_ONES + 128]
                BLK = cst[:, CI_BLK64:CI_BLK64 + 128]
                mTRI = cstb[:, CI_TRI:CI_TRI + 128]
                mTRISL = cstb[:, CI_TRISL:CI_TRISL + 128]
                mTRIS = cstb[:, CI_TRIS:CI_TRIS + 128]
                HM2b = cstb[:, CI_HM2:CI_HM2 + 2]
                DK = 0.6065306597126334

                def v3(ap, a):
                    return ap.rearrange("p (a t) -> p a t", a=a)
                for c in range(32):
                    t0 = c * 128
                    raw = raw_r.next()
                    r3 = v3(raw[:, :], 7)
                    if c == 0:
                        memset('dve', raw, raw[:, :], 0.0)
                        kb.dma('sp', raw, r3[:, :, 1:129], UF, UF[oA:oA + 896, 0:128].rearrange("(c p) t -> p c t", p=128))
                    else:
                        kb.dma('sp', raw, r3[:, :, :], UF, UF[oA:oA + 896, t0 - 1:t0 + 128].rearrange("(c p) t -> p c t", p=128))
                    yield
                    dd = dd_r.next()
                    us = us_r.next()
                    tt('dve', v3(dd[:, :], 7), r3[:, :, 0:128], r3[:, :, 1:129], ALU.subtract, [raw], [dd])
                    tt('pool', v3(dd[:, :], 7), v3(dd[:, :], 7), bc3(cpl[:, omu:omu + 7], [128, 7, 128], 2), ALU.mult, [dd, cpl], [dd])
                    tt('dve', v3(us[:, :], 7), v3(dd[:, :], 7), r3[:, :, 1:129], ALU.add, [dd, raw], [us])
                    R_ = us[:, 0:256]
                    K_ = us[:, 256:512]
                    V_ = us[:, 512:768]
                    yield
                    T6 = T6_r.next()
                    act(T6[0:32, :], us[0:32, 768:896], AF.Tanh, [us], [T6])
                    cp('pool', T6[32:64, :], us[32:64, 768:896], [us], [T6])
                    act(T6[64:128, :], us[64:128, 768:896], AF.Sigmoid, [us], [T6])
                    yield
                    psz = kb.psum()
                    for p_ in range(2):
                        mm(psz, psz[:, p_ * 128:(p_ + 1) * 128], WP[:, p_ * 128:(p_ + 1) * 128], T6[:, :], [WP, T6])
                        mm(psz, psz[:, 256 + p_ * 128:256 + (p_ + 1) * 128], WP[:, 256 + p_ * 128:256 + (p_ + 1) * 128], T6[:, :], [WP, T6])
                    f = f_r.next()
                    F_ = lambda i_: f[:, i_ * 256:(i_ + 1) * 256]
                    SG, AI, KKt, SQ, CS, E1, E1m, E2, E3, BV, KP, TMP = [F_(i_) for i_ in range(12)]
                    for p_ in range(2):
                        act(SG[:, p_ * 128:(p_ + 1) * 128], psz[:, p_ * 128:(p_ + 1) * 128], AF.Sigmoid, [psz, cpl], [f],
                            bias=cpl[:, ow0 + p_:ow0 + p_ + 1])
                        act(AI[:, p_ * 128:(p_ + 1) * 128], psz[:, 256 + p_ * 128:256 + (p_ + 1) * 128], AF.Sigmoid, [psz, cpl], [f],
                            bias=cpl[:, oa0 + p_:oa0 + p_ + 1])
                    yield
                    tt('dve', v3(KKt, 2), v3(K_, 2), bc3(cpl[:, okk:okk + 2], [128, 2, 128], 2), ALU.mult, [us, cpl], [f])
                    act(SQ, KKt, AF.Square, [f], [f])
                    psn = kb.psum()
                    mm(psn, psn[:, 0:256], BLK, SQ, [cst, f])
                    act(SQ, psn[:, 0:256], AF.Sqrt, [psn], [f])
                    ts('dve', SQ, SQ, 1e-12, None, ALU.max, ALU.bypass, [f], [f])
                    kb.op('dve', lambda e, SQ=SQ: e.reciprocal(out=SQ, in_=SQ), [f], [f])
                    tt('dve', KKt, KKt, SQ, ALU.mult, [f], [f])
                    yield
                    tt('pool', BV, KKt, AI, ALU.mult, [f], [f])
                    ts('dve', TMP, AI, -1.0, None, ALU.add, ALU.bypass, [f], [f])
                    tt('dve', v3(TMP, 2), v3(TMP, 2), bc3(cpl[:, oka:oka + 2], [128, 2, 128], 2), ALU.mult, [f, cpl], [f])
                    stt('dve', KP, TMP, 1.0, K_, ALU.add, ALU.mult, [f, us], [f])
                    yield
                    for p_ in range(2):
                        kb.op('dve', lambda e, CS=CS, SG=SG, p_=p_: e.tensor_tensor_scan(
                            out=CS[:, p_ * 128:(p_ + 1) * 128], data0=ONES, data1=SG[:, p_ * 128:(p_ + 1) * 128], initial=0.0,
                            op0=ALU.mult, op1=ALU.add), [f, cst], [f])
                    sm = sm_r.next()
                    ts('dve', sm[:, 0:2], v3(CS, 2)[:, :, 127], -DK, None, ALU.mult, ALU.bypass, [f], [sm])
                    act(E1, CS, AF.Exp, [f], [f], scale=-DK)
                    act(E2, CS, AF.Exp, [f], [f], scale=DK)
                    tt('pool', TMP, CS, SG, ALU.subtract, [f], [f])
                    act(E1m, TMP, AF.Exp, [f], [f], scale=-DK)
                    for p_ in range(2):
                        act(E3[:, p_ * 128:(p_ + 1) * 128], CS[:, p_ * 128:(p_ + 1) * 128], AF.Exp, [f, sm], [f], scale=DK,
                            bias=sm[:, p_:p_ + 1])
                    yield
                    bA = bA_r.next()
                    At, Am, Bt, Kt, Rm, RKP, Vb = (bA[:, 0:256], bA[:, 256:768], bA[:, 768:1024], bA[:, 1024:1280],
                                                   bA[:, 1280:1792], bA[:, 1792:2048], bA[:, 2048:2304])
                    Rt = bA[:, 2304:2560]
                    stt('dve', At, KKt, -1.0, E1m, ALU.mult, ALU.mult, [f], [bA])
                    tt('pool', Bt, BV, E2, ALU.mult, [f], [bA])
                    tt('pool', Kt, KP, E2, ALU.mult, [f], [bA])
                    tt('dve', Rt, R_, E1, ALU.mult, [us, f], [bA])
                    for p_ in range(2):
                        tt('dve' if p_ else 'pool', v3(Am[:, p_ * 256:(p_ + 1) * 256], 2), bc3(At[:, p_ * 128:(p_ + 1) * 128], [128, 2, 128], 1),
                           bc3(HM2b, [128, 2, 128], 2), ALU.mult, [bA, cstb], [bA])
                        tt('pool' if p_ else 'dve', v3(Rm[:, p_ * 256:(p_ + 1) * 256], 2), bc3(Rt[:, p_ * 128:(p_ + 1) * 128], [128, 2, 128], 1),
                           bc3(HM2b, [128, 2, 128], 2), ALU.mult, [bA, cstb], [bA])
                    tt('pool', TMP, R_, KP, ALU.mult, [us, f], [f])
                    tt('dve', v3(RKP, 2), v3(TMP, 2), bc3(cpl[:, ork:ork + 2], [128, 2, 128], 2), ALU.mult, [f, cpl], [bA])
                    tt('pool', E1m, BV, E3, ALU.mult, [f], [f])
                    tt('pool', E2, KP, E3, ALU.mult, [f], [f])
                    pstr = kb.psum()
                    for p_ in range(2):
                        kb.op('pe', lambda e, pstr=pstr, p_=p_, E1m=E1m: e.transpose(
                            out=pstr[:, p_ * 128:(p_ + 1) * 128], in_=E1m[:, p_ * 128:(p_ + 1) * 128], identity=ident), [f, cst], [pstr])
                        kb.op('pe', lambda e, pstr=pstr, p_=p_, E2=E2: e.transpose(
                            out=pstr[:, 256 + p_ * 128:256 + (p_ + 1) * 128], in_=E2[:, p_ * 128:(p_ + 1) * 128], identity=ident), [f, cst], [pstr])
                    BKM = BKM_r.next()
                    for w_ in range(2):
                        src = v3(pstr[:, w_ * 256:(w_ + 1) * 256], 2)
                        dst = v3(BKM[:, w_ * 512:(w_ + 1) * 512], 2)
                        cp('act', dst[:, :, 0:64], src[:, :, 0:64], [pstr], [BKM])
                        cp('dve', dst[:, :, 192:256], src[:, :, 64:128], [pstr], [BKM])
                    pstv = kb.psum()
                    for p_ in range(2):
                        kb.op('pe', lambda e, pstv=pstv, p_=p_, us=us: e.transpose(
                            out=pstv[:, p_ * 128:(p_ + 1) * 128], in_=us[:, 512 + p_ * 128:512 + (p_ + 1) * 128], identity=ident), [us, cst], [pstv])
                    ep = ep_r.next()
                    V32 = ep[:, 0:256]
                    cp('act', V32, pstv[:, 0:256], [pstv], [ep])
                    cp('dve', Vb, V32, [ep], [bA])
                    yield
                    mats = mats_r.next()
                    Akt, Rbt, Rkt = mats[:, 0:512], mats[:, 512:1024], mats[:, 1024:1536]
                    Nn = [mats[:, 1536:2048], mats[:, 2048:2560]]
                    Mt = Mt_r.next()
                    Mtk = lambda k_: Mt[:, k_ * 512:(k_ + 1) * 512]

                    def quad(lhs_of, rhs_of, reads, dst, dbuf, mask, eng):
                        ps = kb.psum()
                        for h in range(4):
                            mm(ps, ps[:, h * 128:(h + 1) * 128], lhs_of(h), rhs_of(h), reads)
                        if mask is None:
                            cp(eng, dst, ps[:, :], [ps], [dbuf])
                        else:
                            tt('dve', v3(dst, 4), v3(ps[:, :], 4), bc3(mask, [128, 4, 128], 1), ALU.mult, [ps, cstb], [dbuf])
                    Bt_p = lambda h: Bt[:, (h // 2) * 128:(h // 2 + 1) * 128]
                    Kt_p = lambda h: Kt[:, (h // 2) * 128:(h // 2 + 1) * 128]
                    Am_h = lambda h: Am[:, h * 128:(h + 1) * 128]
                    Rm_h = lambda h: Rm[:, h * 128:(h + 1) * 128]
                    M0 = Mtk(0)
                    quad(Bt_p, Am_h, [bA], M0, Mt, mTRISL, None)
                    quad(Am_h, Bt_p, [bA], Nn[0], mats, mTRIS, None)
                    quad(Kt_p, Am_h, [bA], Akt, mats, mTRISL, None)
                    quad(Bt_p, Rm_h, [bA], Rbt, mats, mTRI, None)
                    quad(Kt_p, Rm_h, [bA], Rkt, mats, mTRI, None)
                    yield
                    for k_ in range(6):
                        Mk, Nk = Mtk(k_), Nn[k_ % 2]
                        Mn, Nx = Mtk(k_ + 1), Nn[(k_ + 1) % 2]
                        quad(lambda h, Nk=Nk: Nk[:, h * 128:(h + 1) * 128], lambda h, Mk=Mk: Mk[:, h * 128:(h + 1) * 128], [mats, Mt], Mn, Mt, None, 'act')
                        if k_ < 5:
                            quad(lambda h, Mk=Mk: Mk[:, h * 128:(h + 1) * 128], lambda h, Nk=Nk: Nk[:, h * 128:(h + 1) * 128], [mats, Mt], Nx, mats, None, 'dve')
                        yield
                    yield
                    psx = kb.psum()
                    for h in range(4):
                        mm(psx, psx[:, h * 64:(h + 1) * 64], Am_h(h), STb[h // 2][:, :], [bA, STb[h // 2]], start=True, stop=False)
                        mm(psx, psx[:, h * 64:(h + 1) * 64], Akt[:, h * 128:(h + 1) * 128], Vb[:, h * 64:(h + 1) * 64], [mats, bA], start=False, stop=True)
                    X = X_r.next()
                    Xb = Xb_r.next()
                    cp('dve', X[:, :], psx[:, 0:256], [psx], [X])
                    cp('act', Xb[:, :], X[:, :], [X], [Xb])
                    for k_ in range(7):
                        psx = kb.psum()
                        Mk = Mtk(k_)
                        for h in range(4):
                            mm(psx, psx[:, h * 64:(h + 1) * 64], Mk[:, h * 128:(h + 1) * 128], Xb[:, h * 64:(h + 1) * 64], [Mt, Xb])
                        tt('dve', X[:, :], X[:, :], psx[:, 0:256], ALU.add, [X, psx], [X])
                        Xb = Xb_r.next()
                        cp('act', Xb[:, :], X[:, :], [X], [Xb])
                        yield
                    Ub = Xb
                    yield
                    psy = kb.psum()
                    for h in range(4):
                        o_ = psy[:, h * 64:(h + 1) * 64]
                        mm(psy, o_, Rm_h(h), STb[h // 2][:, :], [bA, STb[h // 2]], start=True, stop=False)
                        mm(psy, o_, Rbt[:, h * 128:(h + 1) * 128], Ub[:, h * 64:(h + 1) * 64], [mats, Ub], start=False, stop=False)
                        mm(psy, o_, Rkt[:, h * 128:(h + 1) * 128], Vb[:, h * 64:(h + 1) * 64], [mats, bA], start=False, stop=True)
                    yield
                    pss_ = kb.psum()
                    for p_ in range(2):
                        o_ = pss_[:, p_ * 64:(p_ + 1) * 64]
                        for hh in range(2):
                            h = p_ * 2 + hh
                            mm(pss_, o_, BKM[:, h * 128:(h + 1) * 128], Ub[:, h * 64:(h + 1) * 64], [BKM, Ub], start=(hh == 0), stop=False)
                            mm(pss_, o_, BKM[:, 512 + h * 128:512 + (h + 1) * 128], Vb[:, h * 64:(h + 1) * 64], [BKM, bA], start=False, stop=(hh == 1))
                    for p_ in range(2):
                        stt('dve', ST32[p_][:, :], ST32[p_][:, :], E1[:, p_ * 128 + 127:p_ * 128 + 128], pss_[:, p_ * 64:(p_ + 1) * 64],
                            ALU.mult, ALU.add, [ST32[p_], f, pss_], [ST32[p_]])
                        cp('act', STb[p_][:, :], ST32[p_][:, :], [ST32[p_]], [STb[p_]])
                    yield
                    psg = kb.psum()
                    mm(psg, psg[:, 0:256], T6[:, :], WP[:, 512:768], [T6, WP])
                    for p_ in range(2):
                        mm(psg, psg[:, 256 + 2 * p_:256 + 2 * p_ + 2], RKP[:, p_ * 128:(p_ + 1) * 128], HM2b, [bA, cstb])
                    HV, SQe, GG = ep[:, 256:512], ep[:, 512:768], ep[:, 768:1024]
                    cp('act', GG, psg[:, 0:256], [psg], [ep])
                    cp('dve', sm[:, 4:8], psg[:, 256:260], [psg], [sm])
                    kb.op('dve', lambda e, sm=sm, psy=psy: e.tensor_reduce(out=sm[:, 8:12], in_=v3(psy[:, 0:256], 4), axis=AX.X, op=ALU.add), [psy], [sm])
                    ts('dve', sm[:, 8:12], sm[:, 8:12], 1.0 / 64, None, ALU.mult, ALU.bypass, [sm], [sm])
                    tt('dve', v3(HV, 4), v3(psy[:, 0:256], 4), bc3(sm[:, 8:12], [128, 4, 64], 2), ALU.subtract, [psy, sm], [ep])
                    act(SQe, HV, AF.Square, [ep], [ep])
                    kb.op('dve', lambda e, sm=sm, SQe=SQe: e.tensor_reduce(out=sm[:, 12:16], in_=v3(SQe, 4), axis=AX.X, op=ALU.add), [ep], [sm])
                    rsqrt(sm, sm[:, 12:16], sm[:, 12:16], 1.0 / 64, 64e-5)
                    tt('dve', v3(HV, 4), v3(HV, 4), bc3(sm[:, 12:16], [128, 4, 64], 2), ALU.mult, [ep, sm], [ep])
                    tt('pool', HV, HV, gnb[:, 0:256], ALU.mult, [ep, gnb], [ep])
                    tt('pool', HV, HV, gnb[:, 256:512], ALU.add, [ep, gnb], [ep])
                    tt('dve', v3(SQe, 4), v3(V32, 4), bc3(sm[:, 4:8], [128, 4, 64], 2), ALU.mult, [ep, sm], [ep])
                    tt('pool', HV, HV, SQe, ALU.add, [ep], [ep])
                    ya = ya_r.next()
                    tt('dve', ya[:, :], HV, GG, ALU.mult, [ep], [ya])
                    kb.dma('sp', Y, Y[t0:t0 + 128, 0:256], ya, ya[:, :])

            def genB():
                AUP = kb.sbuf("AUP", [17, 128], BF16, dma='sw')
                kb.dma('pool', AUP, AUP[0:16, :], gla_aup, gla_aup[l, :, :])
                kb.dma('pool', AUP, AUP[16:17, :], gla_ab, gla_ab[l:l + 1, :])
                ngb = kb.sbuf("ngb", [128, 256], F32, dma=True)
                kb.dma('sp', ngb, ngb[:, :], rowp, row_bc_ap(l, 'gla_norm_g'))
                S32 = kb.sbuf("S32", [128, 64], F32)
                Sb = kb.sbuf("Sb", [128, 64], BF16)
                memset('dve', S32, S32[:, :], 0.0)
                memset('dve', Sb, Sb[:, :], 0.0)
                adT_r = Ring(kb, "adT", 2, [17, 128], BF16, dma='sw')
                for b_ in adT_r.bufs:
                    memset('dve', b_, b_[:, :], 1.0)
                qk_r = Ring(kb, "qk32", 2, [128, 256], F32, dma=True)
                tok_r = Ring(kb, "tok", 2, [128, 656], F32, dma=True)
                vb_r = Ring(kb, "vb", 2, [128, 256], BF16)
                w1 = Ring(kb, "w1", 2, [128, 128], F32)
                nl_r = Ring(kb, "nl", 2, [128, 128], F32)
                E_r = Ring(kb, "E", 2, [128, 384], F32)
                qe_r = Ring(kb, "qe", 2, [128, 128], BF16)
                ke_r = Ring(kb, "ke", 2, [128, 128], BF16)
                kd_r = Ring(kb, "kd", 2, [128, 128], BF16)
                Qbd_r = Ring(kb, "Qbd", 2, [128, 512], BF16)
                KDbd_r = Ring(kb, "KDbd", 2, [128, 512], BF16)
                sT_r = Ring(kb, "sT", 2, [128, 512], BF16)
                sq_r = Ring(kb, "sq", 2, [128, 256], F32)
                ss_r = Ring(kb, "ss", 2, [128, 8], F32)
                on_r = Ring(kb, "on", 2, [128, 256], F32)
                sg_r = Ring(kb, "sg", 2, [128, 256], F32)
                yb_r = Ring(kb, "yb", 2, [128, 256], F32, dma=True)
                TRI = cst[:, CI_TRI:CI_TRI + 128]
                TRIS = cst[:, CI_TRIS:CI_TRIS + 128]
                oq, ok_, oad = FMOFF['Bq'], FMOFF['Bk'], FMOFF['Bad']
                for c in range(32):
                    t0 = c * 128
                    adT = adT_r.next()
                    kb.dma('pool', adT, adT[0:16, :], UF, UF[oad:oad + 16, t0:t0 + 128])
                    qk = qk_r.next()
                    kb.dma('sp', qk, qk[:, 0:128], UF, UF[oq:oq + 128, t0:t0 + 128])
                    kb.dma('sp', qk, qk[:, 128:256], UF, UF[ok_:ok_ + 128, t0:t0 + 128])
                    tok = tok_r.next()
                    kb.dma('sp', tok, tok[:, :], UT, UT[t0:t0 + 128, 0:656])
                    vb = vb_r.next()
                    cp('pool', vb[:, :], tok[:, UT_BV:UT_BV + 256], [tok], [vb])
                    yield
                    psz = kb.psum()
                    mm(psz, psz[:, 0:128], adT[0:17, :], AUP[0:17, :], [adT, AUP])
                    ez = w1.next()
                    act(ez[:, :], psz[:, 0:128], AF.Exp, [psz], [ez], scale=-1.0)
                    nl = nl_r.next()
                    act(nl[:, :], ez[:, :], AF.Ln, [ez], [nl], bias=1.0)
                    yield
                    psb = kb.psum()
                    mm(psb, psb[:, 0:128], nl[:, :], TRI, [nl, cst])
                    mm(psb, psb[:, 128:256], TRIS, nl[:, :], [nl, cst])
                    E = E_r.next()
                    act(E[:, 0:128], psb[:, 0:128], AF.Exp, [psb], [E], scale=-1.0 / 16)
                    act(E[:, 128:256], psb[:, 0:128], AF.Exp, [psb], [E], scale=1.0 / 16)
                    act(E[:, 256:384], psb[:, 128:256], AF.Exp, [psb], [E], scale=-1.0 / 16)
                    qe = qe_r.next()
                    stt('dve', qe[:, :], qk[:, 0:128], 32.0 ** -0.5, E[:, 0:128], ALU.mult, ALU.mult, [qk, E], [qe])
                    ke = ke_r.next()
                    tt('pool', ke[:, :], qk[:, 128:256], E[:, 128:256], ALU.mult, [qk, E], [ke])
                    kd = kd_r.next()
                    tt('pool', kd[:, :], tok[:, UT_BK:UT_BK + 128], E[:, 256:384], ALU.mult, [tok, E], [kd])
                    Qbd = Qbd_r.next()
                    tt('dve', Qbd[:, :].rearrange("p (h i) -> p h i", h=4), bc3(qe[:, :], [128, 4, 128], 1),
                       bc3(cstb[:, CI_HM:CI_HM + 4], [128, 4, 128], 2), ALU.mult, [qe, cstb], [Qbd])
                    KDbd = KDbd_r.next()
                    tt('pool', KDbd[:, :].rearrange("p (h i) -> p h i", h=4), bc3(kd[:, :], [128, 4, 128], 1),
                       cstb[:, CI_HMROW:CI_HMROW + 512].rearrange("p (h i) -> p h i", h=4), ALU.mult, [kd, cstb], [KDbd])
                    yield
                    pss = kb.psum()
                    mm(pss, pss[:, 0:512], ke[:, :], Qbd[:, :], [ke, Qbd])
                    sT = sT_r.next()
                    tt('dve', sT[:, :].rearrange("p (h i) -> p h i", h=4), pss[:, :].rearrange("p (h i) -> p h i", h=4),
                       bc3(cstb[:, CI_TRI:CI_TRI + 128], [128, 4, 128], 1), ALU.mult, [pss, cstb], [sT])
                    yield
                    pso = kb.psum()
                    for h in range(4):
                        mm(pso, pso[:, h * 64:(h + 1) * 64], Qbd[:, h * 128:(h + 1) * 128], Sb[:, :], [Qbd, Sb],
                           start=True, stop=False)
                        mm(pso, pso[:, h * 64:(h + 1) * 64], sT[:, h * 128:(h + 1) * 128], vb[:, h * 64:(h + 1) * 64],
                           [sT, vb], start=False, stop=True)
                    yield
                    psu = kb.psum()
                    for h in range(4):
                        mm(psu, psu[:, 0:64], KDbd[:, h * 128:(h + 1) * 128], vb[:, h * 64:(h + 1) * 64], [KDbd, vb],
                           start=(h == 0), stop=(h == 3))
                    stt('dve', S32[:, :], S32[:, :], E[:, 127:128], psu[:, 0:64], ALU.mult, ALU.add, [S32, E, psu], [S32])
                    cp('act', Sb[:, :], S32[:, :], [S32], [Sb])
                    yield
                    sq = sq_r.next()
                    act(sq[:, :], pso[:, 0:256], AF.Square, [pso], [sq])
                    ss = ss_r.next()
                    kb.op('dve', lambda e, ss=ss, sq=sq: e.tensor_reduce(
                        out=ss[:, 0:4], in_=sq[:, :].rearrange("p (h v) -> p h v", h=4), axis=AX.X, op=ALU.add), [sq], [ss])
                    rsqrt(ss, ss[:, 4:8], ss[:, 0:4], 1.0 / 64, 1e-6)
                    on = on_r.next()
                    tt('dve', on[:, :].rearrange("p (h v) -> p h v", h=4), pso[:, 0:256].rearrange("p (h v) -> p h v", h=4),
                       bc3(ss[:, 4:8], [128, 4, 64], 2), ALU.mult, [pso, ss], [on])
                    sg = sg_r.next()
                    act(sg[:, :], tok[:, UT_BG:UT_BG + 256], AF.Silu, [tok], [sg])
                    tt('pool', sg[:, :], sg[:, :], ngb[:, :], ALU.mult, [sg, ngb], [sg])
                    yb = yb_r.next()
                    tt('pool', yb[:, :], on[:, :], sg[:, :], ALU.mult, [on, sg], [yb])
                    kb.dma('sp', Y, Y[t0:t0 + 128, 256:512], yb, yb[:, :])


            def genC():
                cpl = kb.sbuf("cplC", [128, NCOL], F32, dma=True)
                kb.dma('sp', cpl, cpl[:, :], colp, colp[l, :, :])
                ngc = kb.sbuf("ngc", [128, 256], F32, dma=True)
                kb.dma('sp', ngc, ngc[:, :], rowp, row_bc_ap(l, 'mlstm_norm_g'))
                ifb = kb.sbuf("ifb", [128, 8], F32, dma=True)
                kb.dma('sp', ifb, ifb[:, 0:4], rowp, row_bc_ap(l, 'mlstm_i_b'))
                kb.dma('sp', ifb, ifb[:, 4:8], rowp, row_bc_ap(l, 'mlstm_f_b'))
                M32 = [kb.sbuf("M32_%d" % p_, [128, 65], F32) for p_ in range(2)]
                Mb = [kb.sbuf("Mb_%d" % p_, [128, 65], BF16) for p_ in range(2)]
                for p_ in range(2):
                    memset('dve', M32[p_], M32[p_][:, :], 0.0)
                    memset('dve', Mb[p_], Mb[p_][:, :], 0.0)
                raw_r = Ring(kb, "raw", 2, [128, 4 * 131], F32, dma=True)
                tokc_r = Ring(kb, "tokc", 2, [128, 520], F32, dma=True)
                vaug_r = Ring(kb, "vaug", 2, [128, 260], BF16)
                KD2_r = Ring(kb, "KD2", 2, [128, 512], BF16)
                for b_ in vaug_r.bufs:
                    memset('dve', b_, b_[:, :], 1.0)
                for b_ in KD2_r.bufs:
                    memset('dve', b_, b_[:, :], 0.0)
                acc_r = Ring(kb, "cacc", 2, [128, 512], F32)
                qkc_r = Ring(kb, "qkc", 2, [128, 512], F32)
                g_r = Ring(kb, "gts", 2, [128, 32], F32)
                X_r = Ring(kb, "gX", 2, [128, 512], F32)
                EFG_r = Ring(kb, "EFG", 2, [128, 512], F32)
                qkt_r = Ring(kb, "qkt", 2, [128, 512], BF16)
                kdf_r = Ring(kb, "kdf", 2, [128, 256], BF16)
                sTc_r = Ring(kb, "sTc", 2, [128, 512], BF16)
                Q2_r = Ring(kb, "Q2", 2, [128, 512], BF16)
                hv_r = Ring(kb, "hv", 2, [128, 256], F32)
                sqc_r = Ring(kb, "sqc", 2, [128, 256], F32)
                so_r = Ring(kb, "so", 2, [128, 256], F32)
                yc_r = Ring(kb, "yc", 2, [128, 256], F32, dma=True)
                TRI = cst[:, CI_TRI:CI_TRI + 128]
                TRIS = cst[:, CI_TRIS:CI_TRIS + 128]
                oqk = FMOFF['Cqk']
                cw0, cb0 = CP['conv_w'][0], CP['conv_b'][0]
                for c in range(32):
                    t0 = c * 128
                    raw = raw_r.next()
                    r3 = raw[:, :].rearrange("p (c t) -> p c t", c=4)
                    if c == 0:
                        memset('dve', raw, raw[:, :], 0.0)
                        kb.dma('sp', raw, r3[:, :, 3:131], UF, UF[oqk:oqk + 512, 0:128].rearrange("(c p) t -> p c t", p=128))
                    else:
                        kb.dma('sp', raw, r3[:, :, :], UF, UF[oqk:oqk + 512, t0 - 3:t0 + 128].rearrange("(c p) t -> p c t", p=128))
                    tokc = tokc_r.next()
                    kb.dma('sp', tokc, tokc[:, :], UT, UT[t0:t0 + 128, UT_CV:UT_CV + 520])
                    vaug = vaug_r.next()
                    cp('pool', vaug[:, :].rearrange("p (h v) -> p h v", h=4)[:, :, 0:64],
                       tokc[:, 0:256].rearrange("p (h v) -> p h v", h=4), [tokc], [vaug])
                    yield
                    acc = acc_r.next()
                    for ci in range(4):
                        eng = 'dve'
                        a_ = acc[:, ci * 128:(ci + 1) * 128]
                        ts(eng, a_, r3[:, ci, 0:128], cpl[:, cw0 + ci:cw0 + ci + 1], cpl[:, cb0 + ci:cb0 + ci + 1],
                           ALU.mult, ALU.add, [raw, cpl], [acc])
                        for j in range(1, 4):
                            stt(eng, a_, r3[:, ci, j:j + 128], cpl[:, cw0 + j * 4 + ci:cw0 + j * 4 + ci + 1], a_,
                                ALU.mult, ALU.add, [raw, cpl, acc], [acc])
                    qkc = qkc_r.next()
                    act(qkc[:, :], acc[:, :], AF.Silu, [acc], [qkc])
                    yield
                    g = g_r.next()
                    tt('dve', g[:, 0:8], tokc[:, 512:520], ifb[:, 0:8], ALU.add, [tokc, ifb], [g])
                    act(g[:, 4:8], g[:, 4:8], AF.Exp, [g], [g], scale=-1.0)
                    act(g[:, 4:8], g[:, 4:8], AF.Ln, [g], [g], bias=1.0)
                    X = X_r.next()
                    cp('dve', X[:, 0:256].rearrange("p (h v) -> p h v", h=4), bc3(g[:, 4:8], [128, 4, 64], 2), [g], [X])
                    psF = kb.psum()
                    for p_ in range(2):
                        mm(psF, psF[:, p_ * 128:(p_ + 1) * 128], X[:, p_ * 128:(p_ + 1) * 128], TRI, [X, cst])
                    mm(psF, psF[:, 256:260], TRI, g[:, 4:8], [g, cst])
                    mm(psF, psF[:, 260:264], TRIS, g[:, 4:8], [g, cst])
                    tt('dve', g[:, 8:12], g[:, 0:4], psF[:, 256:260], ALU.add, [g, psF], [g])
                    tt('dve', g[:, 16:20], g[:, 0:4], psF[:, 260:264], ALU.subtract, [g, psF], [g])
                    act(g[:, 12:16], g[:, 16:20], AF.Exp, [g], [g])
                    cp('dve', X[:, 256:512].rearrange("p (h v) -> p h v", h=4), bc3(g[:, 8:12], [128, 4, 64], 2), [g], [X])
                    psG = kb.psum()
                    for p_ in range(2):
                        mm(psG, psG[:, p_ * 128:(p_ + 1) * 128], X[:, 256 + p_ * 128:256 + (p_ + 1) * 128], ident, [X, cst])
                    EFG = EFG_r.next()
                    act(EFG[:, 0:256], psF[:, 0:256], AF.Exp, [psF], [EFG], scale=-1.0)
                    act(EFG[:, 256:512], psG[:, 0:256], AF.Exp, [psG], [EFG])
                    yield
                    qkt = qkt_r.next()
                    tt('dve', qkt[:, 0:256], qkc[:, 0:256], EFG[:, 0:256], ALU.mult, [qkc, EFG], [qkt])
                    stt('pool', qkt[:, 256:512], qkc[:, 256:512], 0.125, EFG[:, 256:512], ALU.mult, ALU.mult, [qkc, EFG], [qkt])
                    yield
                    psT = kb.psum()
                    for p_ in range(2):
                        kb.op('pe', lambda e, psT=psT, p_=p_, qkc=qkc: e.transpose(
                            out=psT[:, p_ * 128:(p_ + 1) * 128], in_=qkc[:, 256 + p_ * 128:256 + (p_ + 1) * 128],
                            identity=ident), [qkc, cst], [psT])
                    kdf = kdf_r.next()
                    stt('dve', kdf[:, :].rearrange("p (h v) -> p h v", h=4), psT[:, 0:256].rearrange("p (h v) -> p h v", h=4),
                        0.125, bc3(g[:, 12:16], [128, 4, 64], 2), ALU.mult, ALU.mult, [psT, g], [kdf])
                    KD2 = KD2_r.next()
                    cp('pool', KD2[:, :].rearrange("p (a b) -> p a b", a=2)[:, :, 0:64],
                       kdf[:, :].rearrange("p (a b) -> p a b", a=2)[:, :, 0:64], [kdf], [KD2])
                    cp('pool', KD2[:, :].rearrange("p (a b) -> p a b", a=2)[:, :, 192:256],
                       kdf[:, :].rearrange("p (a b) -> p a b", a=2)[:, :, 64:128], [kdf], [KD2])
                    yield
                    Q2 = Q2_r.next()
                    for p_ in range(2):
                        tt('pool' if p_ else 'dve', Q2[:, p_ * 256:(p_ + 1) * 256].rearrange("p (a i) -> p a i", a=2),
                           bc3(qkt[:, p_ * 128:(p_ + 1) * 128], [128, 2, 128], 1),
                           bc3(cstb[:, CI_HM2:CI_HM2 + 2], [128, 2, 128], 2), ALU.mult, [qkt, cstb], [Q2])
                    pss = kb.psum()
                    for p_ in range(2):
                        mm(pss, pss[:, p_ * 256:(p_ + 1) * 256], qkt[:, 256 + p_ * 128:256 + (p_ + 1) * 128],
                           Q2[:, p_ * 256:(p_ + 1) * 256], [qkt, Q2])
                    sT = sTc_r.next()
                    tt('dve', sT[:, :].rearrange("p (h i) -> p h i", h=4), pss[:, :].rearrange("p (h i) -> p h i", h=4),
                       bc3(cstb[:, CI_TRI:CI_TRI + 128], [128, 4, 128], 1), ALU.mult, [pss, cstb], [sT])
                    yield
                    pso = kb.psum()
                    for h in range(4):
                        p_, hh = h // 2, h % 2
                        mm(pso, pso[:, h * 65:(h + 1) * 65], Q2[:, h * 128:(h + 1) * 128],
                           Mb[p_][:, :], [Q2, Mb[p_]], start=True, stop=False)
                        mm(pso, pso[:, h * 65:(h + 1) * 65], sT[:, h * 128:(h + 1) * 128], vaug[:, h * 65:(h + 1) * 65],
                           [sT, vaug], start=False, stop=True)
                    yield
                    psM = kb.psum()
                    for p_ in range(2):
                        for hh in range(2):
                            h = p_ * 2 + hh
                            mm(psM, psM[:, p_ * 128:p_ * 128 + 65], KD2[:, h * 128:(h + 1) * 128], vaug[:, h * 65:(h + 1) * 65],
                               [KD2, vaug], start=(hh == 0), stop=(hh == 1))
                    for p_ in range(2):
                        stt('dve', M32[p_][:, :], M32[p_][:, :], EFG[:, p_ * 128 + 127:p_ * 128 + 128], psM[:, p_ * 128:p_ * 128 + 65],
                            ALU.mult, ALU.add, [M32[p_], EFG, psM], [M32[p_]])
                        cp('act', Mb[p_][:, :], M32[p_][:, :], [M32[p_]], [Mb[p_]])
                    yield
                    po3 = pso[:, 0:260].rearrange("p (h v) -> p h v", h=4)
                    act(g[:, 20:24], po3[:, :, 64], AF.Abs, [pso], [g])
                    ts('dve', g[:, 20:24], g[:, 20:24], 1.0, None, ALU.max, ALU.bypass, [g], [g])
                    kb.op('dve', lambda e, g=g: e.reciprocal(out=g[:, 20:24], in_=g[:, 20:24]), [g], [g])
                    hv = hv_r.next()
                    hv3 = hv[:, :].rearrange("p (h v) -> p h v", h=4)
                    tt('dve', hv3, po3[:, :, 0:64], bc3(g[:, 20:24], [128, 4, 64], 2), ALU.mult, [pso, g], [hv])
                    kb.op('dve', lambda e, g=g, hv3=hv3: e.tensor_reduce(out=g[:, 24:28], in_=hv3, axis=AX.X, op=ALU.add), [hv], [g])
                    ts('dve', g[:, 24:28], g[:, 24:28], 1.0 / 64, None, ALU.mult, ALU.bypass, [g], [g])
                    tt('dve', hv3, hv3, bc3(g[:, 24:28], [128, 4, 64], 2), ALU.subtract, [hv, g], [hv])
                    sq = sqc_r.next()
                    act(sq[:, :], hv[:, :], AF.Square, [hv], [sq])
                    kb.op('dve', lambda e, g=g, sq=sq: e.tensor_reduce(
                        out=g[:, 28:32], in_=sq[:, :].rearrange("p (h v) -> p h v", h=4), axis=AX.X, op=ALU.add), [sq], [g])
                    rsqrt(g, g[:, 28:32], g[:, 28:32], 1.0 / 64, 1e-5)
                    so = so_r.next()
                    act(so[:, :], tokc[:, 256:512], AF.Sigmoid, [tokc], [so])
                    tt('pool', so[:, :], so[:, :], ngc[:, :], ALU.mult, [so, ngc], [so])
                    tt('dve', hv3, hv3, bc3(g[:, 28:32], [128, 4, 64], 2), ALU.mult, [hv, g], [hv])
                    yc = yc_r.next()
                    tt('pool', yc[:, :], hv[:, :], so[:, :], ALU.mult, [hv, so], [yc])
                    kb.dma('sp', Y, Y[t0:t0 + 128, 512:768], yc, yc[:, :])


            if only is None or any(t_ in only for t_ in 'ABCD'):
                for b_ in xT:
                    kb.retire(b_)
                top_ptr = kb.aptr
                kb.aptr = xT_base
            if only is None or any(t_ in only for t_ in 'ABC'):
                mk = kb.mark()
                gens = [(g_(), {'banks': bk_, 'i': 0}) for n_, g_, bk_ in
                        (('A', genA, [0, 1, 2, 3]), ('B', genB, [4, 5]), ('C', genC, [6, 7])) if only is None or n_ in only]
                while gens:
                    for ge_ in list(gens):
                        kb.pctx = ge_[1]
                        try:
                            next(ge_[0])
                        except StopIteration:
                            gens.remove(ge_)
                kb.pctx = None
                kb.release(mk)

            if only is None or 'D' in only:
                mk = kb.mark()
                kb.NROT = 6
                cpl = kb.sbuf("cplD", [128, NCOL], F32, dma=True)
                kb.dma('sp', cpl, cpl[:, :], colp, colp[l, :, :])
                oqn, okvn = CP['mla_q_norm_g'][0], CP['mla_kv_norm_g'][0]
                SC = 96.0 ** -0.5
                wq32 = kb.sbuf("wq32", [128, 768], F32, dma=True)
                kb.dma('sp', wq32, wq32[:, :].rearrange("p (k c) -> p k c", k=2), w_uq,
                       w_uq[l, :, :].rearrange("(k p) c -> p k c", p=128))
                Wq = kb.sbuf("Wq", [128, 768], BF16)
                Wqs = kb.sbuf("Wqs", [128, 768], BF16)
                for kc in range(2):
                    ts('dve', Wq[:, kc * 384:(kc + 1) * 384], wq32[:, kc * 384:(kc + 1) * 384], cpl[:, oqn + kc:oqn + kc + 1], SC,
                       ALU.mult, ALU.mult, [wq32, cpl], [Wq])
                cp('pool', Wqs[:, :], Wq[:, :], [Wq], [Wqs])
                w6 = Wq[:, :].rearrange("p (g c) -> p g c", c=96)
                ws6 = Wqs[:, :].rearrange("p (g c) -> p g c", c=96)
                cp('pool', ws6[:, :, 64:80], w6[:, :, 80:96], [Wq], [Wqs])
                cp('pool', ws6[:, :, 80:96], w6[:, :, 64:80], [Wq], [Wqs])
                wkv32 = kb.sbuf("wkv32", [128, 512], F32, dma=True)
                kb.dma('sp', wkv32, wkv32[:, :], w_ukv, w_ukv[l, :, :])
                Wkv = kb.sbuf("Wkv", [128, 512], BF16)
                ts('dve', Wkv[:, :], wkv32[:, :], cpl[:, okvn:okvn + 1], None, ALU.mult, ALU.bypass, [wkv32, cpl], [Wkv])
                QT = [kb.sbuf("QT%d" % h, [128, S], BF16) for h in range(4)]
                KT = [kb.sbuf("KT%d" % h, [128, S], BF16) for h in range(4)]
                Vaug = kb.sbuf("Vaug", [128, 32 * 260], BF16)
                memset('pool', Vaug, Vaug[:, :], 1.0)
                TW = 256
                mk2 = kb.mark()
                cq_r = Ring(kb, "cq32", 2, [128, 2 * TW], F32, dma=True)
                ckv_r = Ring(kb, "ckv32", 2, [128, TW], F32, dma=True)
                kr_r = Ring(kb, "kr", 2, [128, 2 * TW], F32, dma=True)
                cs2_r = Ring(kb, "cs2", 2, [128, 2 * TW], F32, dma=True)
                cqb_r = Ring(kb, "cqb", 2, [128, 2 * TW], BF16)
                ckvb_r = Ring(kb, "ckvb", 2, [128, TW], BF16)
                sqq_r = Ring(kb, "sqq", 2, [128, 2 * TW], F32)
                sqkv_r = Ring(kb, "sqkv", 2, [128, TW], F32)
                rq_r = Ring(kb, "rq", 2, [128, TW], F32)
                rkv_r = Ring(kb, "rkv", 2, [128, TW], F32)
                krot_r = Ring(kb, "krot", 2, [128, 2 * TW], F32)
                t12_r = Ring(kb, "t12", 2, [128, 2 * TW], F32)
                rc_r = Ring(kb, "rc", 4, [128, 8], F32)
                ocq, ockv, okr = FMOFF['Dcq'], FMOFF['Dckv'], FMOFF['Dkr']
                ONES = cst[:, CI_ONES:CI_ONES + 128]
                for tc in range(S // TW):
                    t0 = tc * TW
                    cq = cq_r.next()
                    kb.dma('sp', cq, cq[:, :].rearrange("p (k t) -> p k t", k=2), UF,
                           UF[ocq:ocq + 256, t0:t0 + TW].rearrange("(k p) t -> p k t", p=128))
                    ckv = ckv_r.next()
                    kb.dma('sp', ckv, ckv[:, :], UF, UF[ockv:ockv + 128, t0:t0 + TW])
                    kr = kr_r.next()
                    kb.dma('sp', kr, kr[64:96, 0:TW], UF, UF[okr:okr + 32, t0:t0 + TW])
                    kb.dma('sp', kr, kr[64:80, TW:2 * TW], UF, UF[okr + 16:okr + 32, t0:t0 + TW])
                    kb.dma('sp', kr, kr[80:96, TW:2 * TW], UF, UF[okr:okr + 16, t0:t0 + TW])
                    cs2 = cs2_r.next()
                    kb.dma('sp', cs2, cs2[64:96, 0:TW], COS2, COS2[64:96, t0:t0 + TW])
                    kb.dma('sp', cs2, cs2[64:96, TW:2 * TW], SIN2, SIN2[64:96, t0:t0 + TW])
                    cqb = cqb_r.next()
                    cp('pool', cqb[:, :], cq[:, :], [cq], [cqb])
                    ckvb = ckvb_r.next()
                    cp('pool', ckvb[:, :], ckv[:, :], [ckv], [ckvb])
                    sqq = sqq_r.next()
                    act(sqq[:, :], cq[:, :], AF.Square, [cq], [sqq])
                    sqkv = sqkv_r.next()
                    act(sqkv[:, :], ckv[:, :], AF.Square, [ckv], [sqkv])
                    psr = kb.psum()
                    for kc in range(2):
                        mm(psr, psr[0:96, 0:TW], ONES[:, 0:96], sqq[:, kc * TW:(kc + 1) * TW], [cst, sqq], start=(kc == 0), stop=(kc == 1))
                    rq = rq_r.next()
                    act(rq[0:96, 0:TW], psr[0:96, 0:TW], AF.Sqrt, [psr], [rq], bias=1e-6, scale=1.0 / 256)
                    kb.op('dve', lambda e, rq=rq: e.reciprocal(out=rq[0:96, 0:TW], in_=rq[0:96, 0:TW]), [rq], [rq])
                    psr2 = kb.psum()
                    mm(psr2, psr2[0:64, 0:TW], ONES[:, 0:64], sqkv[:, :], [cst, sqkv])
                    rkv = rkv_r.next()
                    act(rkv[0:64, 0:TW], psr2[0:64, 0:TW], AF.Sqrt, [psr2], [rkv], bias=1e-6, scale=1.0 / 128)
                    kb.op('dve', lambda e, rkv=rkv: e.reciprocal(out=rkv[0:64, 0:TW], in_=rkv[0:64, 0:TW]), [rkv], [rkv])
                    krot = krot_r.next()
                    tt('dve', krot[64:96, 0:2 * TW], kr[64:96, 0:2 * TW], cs2[64:96, 0:2 * TW], ALU.mult, [kr, cs2], [krot])
                    tt('dve', krot[64:96, 0:TW], krot[64:96, 0:TW], krot[64:96, TW:2 * TW], ALU.add, [krot], [krot])
                    for h in range(4):
                        psq = kb.psum()
                        psqs = kb.psum()
                        for kc in range(2):
                            g_ = kc * 4 + h
                            mm(psq, psq[0:96, 0:TW], w6[:, g_, :], cqb[:, kc * TW:(kc + 1) * TW], [Wq, cqb], start=(kc == 0), stop=(kc == 1))
                        for kc in range(2):
                            g_ = kc * 4 + h
                            mm(psqs, psqs[0:96, 0:TW], ws6[:, g_, :], cqb[:, kc * TW:(kc + 1) * TW], [Wqs, cqb], start=(kc == 0), stop=(kc == 1))
                        tt('dve', QT[h][0:64, t0:t0 + TW], psq[0:64, 0:TW], rq[0:64, 0:TW], ALU.mult, [psq, rq], [QT[h]])
                        t12 = t12_r.next()
                        tt('dve', t12[64:96, 0:TW], psq[64:96, 0:TW], cs2[64:96, 0:TW], ALU.mult, [psq, cs2], [t12])
                        tt('dve', t12[64:96, TW:2 * TW], psqs[64:96, 0:TW], cs2[64:96, TW:2 * TW], ALU.mult, [psqs, cs2], [t12])
                        tt('pool', t12[64:96, 0:TW], t12[64:96, 0:TW], t12[64:96, TW:2 * TW], ALU.add, [t12], [t12])
                        tt('pool', QT[h][64:96, t0:t0 + TW], t12[64:96, 0:TW], rq[64:96, 0:TW], ALU.mult, [t12, rq], [QT[h]])
                        psk = kb.psum()
                        mm(psk, psk[0:64, 0:TW], Wkv[:, h * 128:h * 128 + 64], ckvb[:, :], [Wkv, ckvb])
                        tt('dve', KT[h][0:64, t0:t0 + TW], psk[0:64, 0:TW], rkv[0:64, 0:TW], ALU.mult, [psk, rkv], [KT[h]])
                        cp('act', KT[h][64:96, t0:t0 + TW], krot[64:96, 0:TW], [krot], [KT[h]])
                    for j in range(TW // 128):
                        tt_ = tc * (TW // 128) + j
                        psv = kb.psum()
                        mm(psv, psv[:, :], ckvb[:, j * 128:(j + 1) * 128], Wkv[:, :], [ckvb, Wkv])
                        psc = kb.psum()
                        mm(psc, psc[:, 0:1], sqkv[:, j * 128:(j + 1) * 128], ONES[:, 0:1], [sqkv, cst])
                        rc = rc_r.next()
                        act(rc[:, 0:1], psc[:, 0:1], AF.Sqrt, [psc], [rc], bias=1e-6, scale=1.0 / 128)
                        kb.op('dve', lambda e, rc=rc: e.reciprocal(out=rc[:, 0:1], in_=rc[:, 0:1]), [rc], [rc])
                        ts('dve', Vaug[:, tt_ * 260:(tt_ + 1) * 260].rearrange("p (h c) -> p h c", h=4)[:, :, 0:64],
                           psv[:, :].rearrange("p (h c) -> p h c", h=4)[:, :, 64:128], rc[:, 0:1], None, ALU.mult, ALU.bypass,
                           [psv, rc], [Vaug])
                kb.release(mk2)
                mx_r = Ring(kb, "mx", 6, [128, 16], F32)
                nrow_r = Ring(kb, "nrow", 6, [1, 128], BF16)
                PT_r = Ring(kb, "PT", 4, [128, 512], BF16)
                yd_r = Ring(kb, "yd", 2, [128, 256], F32, dma=True)
                onesb = cstb[0:1, CI_ONES:CI_ONES + 128]
                TRIb = cstb[:, CI_TRI:CI_TRI + 128]
                its = [(i, h) for i in range(32) for h in range(4)]
                st_ = {}

                def part1(n_):
                    i, h = its[n_]
                    q_ap = QT[h][0:96, i * 128:(i + 1) * 128]
                    nk = (i + 1) * 128
                    mx = mx_r.next()
                    ngr = (i + 4) // 4
                    for g_ in range(ngr):
                        k0 = g_ * 512
                        n = min(512, nk - k0)
                        ps = kb.psum()
                        mm(ps, ps[:, 0:n], q_ap, KT[h][0:96, k0:k0 + n], [QT[h], KT[h]])
                        kb.op('dve', lambda e, mx=mx, ps=ps, n=n, g_=g_: e.reduce_max(
                            out=mx[:, g_:g_ + 1], in_=ps[:, 0:n], axis=AX.X), [ps], [mx])
                    kb.op('dve', lambda e, mx=mx, ngr=ngr: e.reduce_max(out=mx[:, 8:9], in_=mx[:, 0:ngr], axis=AX.X), [mx], [mx])
                    ts('dve', mx[:, 9:10], mx[:, 8:9], -1.0, None, ALU.mult, ALU.bypass, [mx], [mx])
                    pst = kb.psum()
                    kb.op('pe', lambda e, pst=pst, mx=mx: e.transpose(out=pst[0:1, 0:128], in_=mx[:, 9:10], identity=ident), [mx, cst], [pst])
                    nrow = nrow_r.next()
                    cp('act', nrow[0:1, :], pst[0:1, 0:128], [pst], [nrow])
                    st_[n_] = (mx, nrow, ngr, q_ap)

                def part2(n_):
                    i, h = its[n_]
                    mx, nrow, ngr, q_ap = st_.pop(n_)
                    if h == 0:
                        st_['yd'] = yd_r.next()
                    yd = st_['yd']
                    po = kb.psum_banks[6 + (n_ % 2)]
                    for g_ in range(ngr):
                        kts = list(range(g_ * 4, min(g_ * 4 + 4, i + 1)))
                        ps = kb.psum()
                        for a_, kt in enumerate(kts):
                            mm(ps, ps[:, a_ * 128:(a_ + 1) * 128], KT[h][0:96, kt * 128:(kt + 1) * 128], q_ap, [KT[h], QT[h]],
                               start=True, stop=False)
                            mm(ps, ps[:, a_ * 128:(a_ + 1) * 128], onesb, nrow[0:1, :], [cstb, nrow], start=False, stop=True)
                        PT = PT_r.next()
                        nn = len(kts) * 128
                        act(PT[:, 0:nn], ps[:, 0:nn], AF.Exp, [ps], [PT])
                        if kts[-1] == i:
                            a_ = len(kts) - 1
                            tt('pool', PT[:, a_ * 128:(a_ + 1) * 128], PT[:, a_ * 128:(a_ + 1) * 128], TRIb, ALU.mult, [PT, cstb], [PT])
                        for a_, kt in enumerate(kts):
                            mm(po, po[:, 0:65], PT[:, a_ * 128:(a_ + 1) * 128], Vaug[:, kt * 260 + h * 65:kt * 260 + (h + 1) * 65],
                               [PT, Vaug], start=(kt == 0), stop=(kt == i))
                    kb.op('dve', lambda e, mx=mx, po=po: e.reciprocal(out=mx[:, 10:11], in_=po[:, 64:65]), [po], [mx])
                    ts('dve', yd[:, h * 64:(h + 1) * 64], po[:, 0:64], mx[:, 10:11], None, ALU.mult, ALU.bypass, [po, mx], [yd])
                    if h == 3:
                        kb.dma('sp', Y, Y[i * 128:(i + 1) * 128, 768:1024], yd, yd[:, :])

                part1(0)
                part1(1)
                for n_ in range(len(its)):
                    if n_ + 2 < len(its):
                        part1(n_ + 2)
                    part2(n_)
                kb.NROT = 8
                kb.release(mk)

            if only is None or any(t_ in only for t_ in 'ABCD'):
                kb.aptr = xT_base
                for h_ in range(2):
                    xT[h_] = kb.sbuf("xT%d_l%d" % (h_, l), [128, 8 * 2048], BF16)
                assert kb.aptr == top_ptr, (kb.aptr, top_ptr)
            if stop == 'MIX':
                break

            Xcur = x_in if l == 0 else X2
            if only is None or 'WO' in only:
                mk = kb.mark()
                Wo = kb.sbuf("Wo", [128, 8 * 1024], BF16, dma='sw')
                kb.dma('pool', Wo, Wo[:, :].rearrange("p (k d) -> p k d", k=8), w_out, w_out[l, :, :].rearrange("(k p) d -> p k d", p=128))
                gb1 = kb.sbuf("gb1", [128, 2048], F32, dma=True)
                kb.dma('sp', gb1, gb1[:, 0:1024], rowp, row_bc_ap(l, 'ln1_g'))
                kb.dma('sp', gb1, gb1[:, 1024:2048], rowp, row_bc_ap(l, 'ln1_b'))
                Wr = kb.sbuf("Wr", [128, 8 * 36], F32, dma=True)
                wr3 = Wr[:, :].rearrange("p (k e) -> p k e", k=8)
                kb.dma('sp', Wr, wr3[:, :, 0:4], w_rg, w_rg[l, :, :].rearrange("(k p) e -> p k e", p=128))
                kb.dma('sp', Wr, wr3[:, :, 4:36], w_re, w_re[l, :, :].rearrange("(k p) e -> p k e", p=128))
                rb = kb.sbuf("rb", [128, 36], F32, dma=True)
                kb.dma('sp', rb, rb[:, 0:4], rowp, row_bc_ap(l, 'moe_b_rg'))
                kb.dma('sp', rb, rb[:, 4:36], rowp, row_bc_ap(l, 'moe_b_re'))
                yt_r = Ring(kb, "yt", 2, [128, 1024], F32, dma=True)
                xr_r = Ring(kb, "xr", 2, [128, 1024], F32, dma=True)
                yTb_r = Ring(kb, "yTb", 2, [128, 1024], BF16)
                r_r = Ring(kb, "r1", 2, [128, 1024], F32, dma=True)
                x32_r = Ring(kb, "x32", 2, [128, 1024], F32)
                wk_r = Ring(kb, "wk1", 2, [128, 16], F32)
                rt_r = Ring(kb, "rt", 2, [128, 160], F32, dma=True)
                for tt_ in range(32):
                    yt = yt_r.next()
                    kb.dma('sp', yt, yt[:, :], Y, Y[tt_ * 128:(tt_ + 1) * 128, :])
                    xr = xr_r.next()
                    kb.dma('sp', xr, xr[:, :], Xcur, Xcur[tt_ * 128:(tt_ + 1) * 128, :])
                    yTb = yTb_r.next()
                    for g in range(2):
                        ps = kb.psum()
                        for j in range(4):
                            kc = g * 4 + j
                            kb.op('pe', lambda e, ps=ps, j=j, kc=kc, yt=yt: e.transpose(
                                out=ps[:, j * 128:(j + 1) * 128], in_=yt[:, kc * 128:(kc + 1) * 128], identity=ident), [yt, cst], [ps])
                        evac(yTb[:, g * 512:(g + 1) * 512], ps[:, :], [ps], [yTb])
                    r = r_r.next()
                    for dh in range(2):
                        ps = kb.psum()
                        for kc in range(8):
                            mm(ps, ps[:, :], yTb[:, kc * 128:(kc + 1) * 128], Wo[:, kc * 1024 + dh * 512:kc * 1024 + (dh + 1) * 512],
                               [yTb, Wo], start=(kc == 0), stop=(kc == 7))
                        stt('dve', r[:, dh * 512:(dh + 1) * 512], xr[:, dh * 512:(dh + 1) * 512], ALPHA, ps[:, :], ALU.mult, ALU.add, [xr, ps], [r])
                    wk = wk_r.next()
                    layer_norm_tile(r, gb1, 0, 1024, 1e-5, wk)
                    kb.dma('sp', X1, X1[tt_ * 128:(tt_ + 1) * 128, :], r, r[:, :])
                    x32 = x32_r.next()
                    transpose_tile(r, tt_, x32)
                    psr = kb.psum()
                    for kc in range(8):
                        mm(psr, psr[:, 0:36], x32[:, kc * 128:(kc + 1) * 128], wr3[:, kc, :], [x32, Wr], start=(kc == 0), stop=(kc == 7))
                    rt = rt_r.next()
                    tt('dve', rt[:, 0:36], psr[:, 0:36], rb[:, :], ALU.add, [psr, rb], [rt])
                    kb.op('dve', lambda e, rt=rt: e.reduce_max(out=rt[:, 116:117], in_=rt[:, 0:4], axis=AX.X), [rt], [rt])
                    ts('dve', rt[:, 40:44], rt[:, 0:4], rt[:, 116:117], None, ALU.is_ge, ALU.bypass, [rt], [rt])
                    ts('dve', rt[:, 117:118], rt[:, 116:117], -1.0, None, ALU.mult, ALU.bypass, [rt], [rt])
                    act(rt[:, 36:40], rt[:, 0:4], AF.Exp, [rt], [rt], bias=rt[:, 117:118])
                    kb.op('dve', lambda e, rt=rt: e.reduce_sum(out=rt[:, 118:119], in_=rt[:, 36:40], axis=AX.X), [rt], [rt])
                    cp('dve', rt[:, 44:76].rearrange("p (g e) -> p g e", g=4), bc3(rt[:, 40:44], [128, 4, 8], 2), [rt], [rt])
                    ts('dve', rt[:, 76:108], rt[:, 44:76], -1.0, 1e30, ALU.add, ALU.mult, [rt], [rt])
                    tt('dve', rt[:, 128:160], rt[:, 4:36], rt[:, 44:76], ALU.mult, [rt], [rt])
                    tt('dve', rt[:, 128:160], rt[:, 128:160], rt[:, 76:108], ALU.add, [rt], [rt])
                    kb.op('dve', lambda e, rt=rt: e.max(out=rt[:, 108:116], in_=rt[:, 128:160]), [rt], [rt])
                    ts('dve', rt[:, 44:76], rt[:, 128:160], rt[:, 109:110], None, ALU.is_ge, ALU.bypass, [rt], [rt])
                    ts('dve', rt[:, 119:120], rt[:, 108:109], -1.0, None, ALU.mult, ALU.bypass, [rt], [rt])
                    act(rt[:, 76:108], rt[:, 128:160], AF.Exp, [rt], [rt], bias=rt[:, 119:120])
                    act(rt[:, 120:121], rt[:, 109:110], AF.Exp, [rt], [rt], bias=rt[:, 119:120])
                    ts('dve', rt[:, 120:121], rt[:, 120:121], 1.0, None, ALU.add, ALU.bypass, [rt], [rt])
                    tt('dve', rt[:, 120:121], rt[:, 120:121], rt[:, 118:119], ALU.mult, [rt], [rt])
                    kb.op('dve', lambda e, rt=rt: e.reciprocal(out=rt[:, 121:122], in_=rt[:, 120:121]), [rt], [rt])
                    tt('dve', rt[:, 76:108], rt[:, 76:108], rt[:, 44:76], ALU.mult, [rt], [rt])
                    ts('dve', Gall[:, tt_ * 32:(tt_ + 1) * 32], rt[:, 76:108], rt[:, 121:122], None, ALU.mult, ALU.bypass, [rt], [Gall])
                    if stop == 'WO':
                        cp('dve', rt[:, 0:32], Gall[:, tt_ * 32:(tt_ + 1) * 32], [Gall], [rt])
                        kb.dma('sp', GD, GD[tt_ * 128:(tt_ + 1) * 128, :], rt, rt[:, 0:32])
                kb.release(mk)
            if stop == 'WO':
                break

            last = (l == n_layers - 1)
            if only is None or 'FF' in only:
                mk = kb.mark()
                gb2 = kb.sbuf("gb2", [128, 3072], F32, dma=True)
                kb.dma('sp', gb2, gb2[:, 0:1024], rowp, row_bc_ap(l, 'ln2_g'))
                kb.dma('sp', gb2, gb2[:, 1024:2048], rowp, row_bc_ap(l, 'ln2_b'))
                kb.dma('sp', gb2, gb2[:, 2048:3072], rowp, row_bc_ap(l, 'ple_b_gate'))
                acc = kb.sbuf("acc", [128, 8 * 1024], F32)
                Wg_r = Ring(kb, "Wg", 2, [128, 8 * 512], BF16, dma='sw')
                Wu_r = Ring(kb, "Wu", 2, [128, 8 * 512], BF16, dma='sw')
                Wd_r = Ring(kb, "Wd", 2, [128, 4 * 1024], BF16, dma='sw')
                hT_r = Ring(kb, "hT", 2, [128, 4 * 512], BF16)
                sgm_r = Ring(kb, "sgm", 3, [128, 512], F32)
                x1_r = Ring(kb, "x1t", 2, [128, 1024], F32, dma=True)
                pt_r = Ring(kb, "pt", 2, [128, 256], F32, dma=True)
                pTb_r = Ring(kb, "pTb", 2, [128, 256], BF16)
                r2_r = Ring(kb, "r2", 2, [128, 1024], F32, dma=True)
                wk_r = Ring(kb, "wk2", 2, [128, 16], F32)
                for qt in range(4):
                    hf, qo = qt // 2, (qt % 2) * 1024
                    memset('pool', acc, acc[:, :], 0.0)
                    for e_ in range(32):
                        Wg, Wu, Wd = Wg_r.next(), Wu_r.next(), Wd_r.next()
                        kb.dma('pool', Wg, Wg[:, :].rearrange("p (k f) -> p k f", k=8), w_gate,
                               w_gate[l, e_, :, :].rearrange("(k p) f -> p k f", p=128))
                        kb.dma('pool', Wu, Wu[:, :].rearrange("p (k f) -> p k f", k=8), w_up,
                               w_up[l, e_, :, :].rearrange("(k p) f -> p k f", p=128))
                        kb.dma('pool', Wd, Wd[:, :].rearrange("p (k d) -> p k d", k=4), w_down,
                               w_down[l, e_, :, :].rearrange("(k p) d -> p k d", p=128))
                        for tcq in range(2):
                            to = qo + tcq * 512
                            hT = hT_r.next()
                            for fc in range(4):
                                psg = kb.psum()
                                for kc in range(8):
                                    mm(psg, psg[:, :], Wg[:, kc * 512 + fc * 128:kc * 512 + (fc + 1) * 128],
                                       xT[hf][:, kc * 2048 + to:kc * 2048 + to + 512], [Wg, xT[hf]], start=(kc == 0), stop=(kc == 7))
                                psu = kb.psum()
                                for kc in range(8):
                                    mm(psu, psu[:, :], Wu[:, kc * 512 + fc * 128:kc * 512 + (fc + 1) * 128],
                                       xT[hf][:, kc * 2048 + to:kc * 2048 + to + 512], [Wu, xT[hf]], start=(kc == 0), stop=(kc == 7))
                                sgm = sgm_r.next()
                                act(sgm[:, :], psg[:, :], AF.Silu, [psg], [sgm])
                                tt('dve', hT[:, fc * 512:(fc + 1) * 512], sgm[:, :], psu[:, :], ALU.mult, [sgm, psu], [hT])
                            for tl in range(4):
                                ti = tcq * 4 + tl
                                tt_ = qt * 8 + ti
                                for dh in range(2):
                                    psd = kb.psum()
                                    for fc in range(4):
                                        mm(psd, psd[:, :], hT[:, fc * 512 + tl * 128:fc * 512 + (tl + 1) * 128],
                                           Wd[:, fc * 1024 + dh * 512:fc * 1024 + (dh + 1) * 512], [hT, Wd], start=(fc == 0), stop=(fc == 3))
                                    a_ = acc[:, ti * 1024 + dh * 512:ti * 1024 + (dh + 1) * 512]
                                    stt('dve', a_, psd[:, :], Gall[:, tt_ * 32 + e_:tt_ * 32 + e_ + 1], a_, ALU.mult, ALU.add, [psd, Gall, acc], [acc])
                    Wpg, Wpg2, Wp = Wg_r.next(), Wu_r.next(), Wd_r.next()
                    kb.dma('pool', Wpg, Wpg[:, :].rearrange("p (k d) -> p k d", k=4), ple_wg,
                           ple_wg[l, 0:512, :].rearrange("(k p) d -> p k d", p=128))
                    kb.dma('pool', Wpg2, Wpg2[:, :].rearrange("p (k d) -> p k d", k=4), ple_wg,
                           ple_wg[l, 512:1024, :].rearrange("(k p) d -> p k d", p=128))
                    kb.dma('pool', Wp, Wp[:, 0:2048].rearrange("p (k d) -> p k d", k=2), ple_w,
                           ple_w[l, :, :].rearrange("(k p) d -> p k d", p=128))
                    for ti in range(8):
                        tt_ = qt * 8 + ti
                        tl = qo + ti * 128
                        x1t = x1_r.next()
                        kb.dma('sp', x1t, x1t[:, :], X1, X1[tt_ * 128:(tt_ + 1) * 128, :])
                        pt = pt_r.next()
                        kb.dma('sp', pt, pt[:, :], p_in, p_in[l, tt_ * 128:(tt_ + 1) * 128, :])
                        pst = kb.psum()
                        for j in range(2):
                            kb.op('pe', lambda e, pst=pst, j=j, pt=pt: e.transpose(
                                out=pst[:, j * 128:(j + 1) * 128], in_=pt[:, j * 128:(j + 1) * 128], identity=ident), [pt, cst], [pst])
                        pTb = pTb_r.next()
                        cp('act', pTb[:, :], pst[:, 0:256], [pst], [pTb])
                        r2 = r2_r.next()
                        for dh in range(2):
                            psq = kb.psum()
                            for kc in range(8):
                                W_ = Wpg if kc < 4 else Wpg2
                                k4 = kc % 4
                                mm(psq, psq[:, :], xT[hf][:, kc * 2048 + tl:kc * 2048 + tl + 128],
                                   W_[:, k4 * 1024 + dh * 512:k4 * 1024 + (dh + 1) * 512], [xT[hf], W_], start=(kc == 0), stop=(kc == 7))
                            psp = kb.psum()
                            for kc in range(2):
                                mm(psp, psp[:, :], pTb[:, kc * 128:(kc + 1) * 128], Wp[:, kc * 1024 + dh * 512:kc * 1024 + (dh + 1) * 512],
                                   [pTb, Wp], start=(kc == 0), stop=(kc == 1))
                            sgm = sgm_r.next()
                            tt('dve', sgm[:, :], psq[:, :], gb2[:, 2048 + dh * 512:2048 + (dh + 1) * 512], ALU.add, [psq, gb2], [sgm])
                            act(sgm[:, :], sgm[:, :], AF.Sigmoid, [sgm], [sgm])
                            tt('dve', sgm[:, :], sgm[:, :], psp[:, :], ALU.mult, [sgm, psp], [sgm])
                            a_ = acc[:, ti * 1024 + dh * 512:ti * 1024 + (dh + 1) * 512]
                            stt('dve', r2[:, dh * 512:(dh + 1) * 512], x1t[:, dh * 512:(dh + 1) * 512], ALPHA, a_, ALU.mult, ALU.add, [x1t, acc], [r2])
                            tt('pool', r2[:, dh * 512:(dh + 1) * 512], r2[:, dh * 512:(dh + 1) * 512], sgm[:, :], ALU.add, [r2, sgm], [r2])
                        wk = wk_r.next()
                        layer_norm_tile(r2, gb2, 0, 1024, 1e-5, wk)
                        dst = OUT if last else X2
                        kb.dma('sp', dst, dst[tt_ * 128:(tt_ + 1) * 128, :], r2, r2[:, :])
                        if not last:
                            transpose_tile(r2, tt_)
                kb.release(mk)
            if stop == 'FF':
                break

        outs = {'P1': [UF, UT], 'MIX': [Y], 'WO': [X1, GD]}.get(stop, [OUT])
        kb.wait_all_writes('sp', outs)
        stats = kb.emit()
        print("instr stats", stats)
    return nc


_NC_CACHE = {}


def kernel(**inputs):
    inp = {k: np.asarray(v) for k, v in inputs.items()}
    n = 8
    if 'nc' not in _NC_CACHE:
        _NC_CACHE['nc'] = build(n_layers=L)
    nc = _NC_CACHE['nc']
    rowp, colp = pack_params(inp)
    shared = {'consts': make_consts(), 'rowp': rowp, 'colp': colp}
    for nm in IN_NAMES:
        shared[nm] = np.ascontiguousarray(inp[nm], dtype=np.float32)
    in_maps = [core_inputs(inp, b, shared) for b in range(n)]
    res = run_bass_kernel_spmd(nc, in_maps, core_ids=list(range(n)))
    out = np.stack([np.asarray(res.results[b]['out'], dtype=np.float32) for b in range(n)], axis=0)
    return out
```

```python
import numpy as np
import concourse.bass as bass
import concourse.mybir as mybir
from contextlib import ExitStack
from concourse.bass_utils import run_bass_kernel_spmd

F32 = mybir.dt.float32
BF16 = mybir.dt.bfloat16
I32 = mybir.dt.int32
AF = mybir.ActivationFunctionType
ALU = mybir.AluOpType
AX = mybir.AxisListType
ENGS = ['pe', 'dve', 'act', 'pool', 'sp']


class Buf:
    def __init__(self, name, h, dsem=None, dkey=None):
        self.name = name
        self.h = h
        self.last_write = None
        self.reads = {}
        self.dsem = dsem
        self.dkey = dkey
        self.dcount = 0
        self.wtoks = {}
        self.is_dram = False

    def __getitem__(self, key):
        return self.h[key]


class KB:
    def __init__(self, nc, es):
        self.nc = nc
        self.es = es
        self.ins = {e: [] for e in ENGS}
        self.waited = {e: {} for e in ENGS}
        self.miles = {}
        self.psem = {}
        self.epoch = -1
        self.new_epoch()
        self.dsems = {}
        self.nbuf = 0
        self.psum_banks = []
        self.psum_rr = 0

    def new_epoch(self):
        self.epoch += 1
        self.ekey = {}
        for e in ENGS:
            k = "%s#%d" % (e, self.epoch)
            self.ekey[e] = k
            self.psem[k] = self.es.enter_context(self.nc.semaphore("ps_%s_%d" % (e, self.epoch)))
            self.miles[k] = set()

    def _dsem(self, name):
        s = self.es.enter_context(self.nc.semaphore("d_" + name))
        key = len(self.dsems)
        self.dsems[key] = s
        return s, key

    ARENA = 52500

    def _arena_init(self):
        self.big = self.es.enter_context(self.nc.sbuf_tensor("arena", [128, self.ARENA], F32))
        self.aptr = 0
        self.live = []
        self.inherit = {}
        self.dpool = []
        self.dpool_sw = []

    def sbuf(self, name, shape, dt, dma=False):
        if not hasattr(self, 'big'):
            self._arena_init()
        P, Fd = shape
        esz = 2 if dt == BF16 else 4
        ncol = (Fd * esz + 3) // 4
        ncol = (ncol + 7) // 8 * 8
        assert self.aptr + ncol <= self.ARENA, ("SBUF arena overflow", name, self.aptr, ncol)
        v = self.big[0:P, self.aptr:self.aptr + ncol]
        if dt != F32:
            v = v.bitcast(dt)
        v = v[:, 0:Fd]
        self.aptr += ncol
        b = Buf(name, v)
        if dma:
            b.sw = (dma == 'sw')
            pool = self.dpool_sw if b.sw else self.dpool
            if pool:
                b.ds = pool.pop()
            else:
                s, k = self._dsem(name)
                b.ds = [s, k, 0]
            b.dsem, b.dkey, b.dcount = b.ds
            b.dbase = b.dcount * 16
        b.reads = dict(self.inherit)
        self.live.append(b)
        return b

    def retire(self, b):
        toks = list(b.reads.values()) + list(b.wtoks.values())
        if b.last_write is not None:
            toks.append(b.last_write)
        for t in toks:
            k = t[:2]
            if k not in self.inherit or self.inherit[k][2] < t[2]:
                self.inherit[k] = t

    def mark(self):
        if not hasattr(self, 'big'):
            self._arena_init()
        return (self.aptr, len(self.live))

    def release(self, mk):
        aptr, nl = mk
        for b in self.live[nl:]:
            toks = list(b.reads.values())
            if b.last_write is not None:
                toks.append(b.last_write)
            for t in toks:
                k = t[:2]
                if k not in self.inherit or self.inherit[k][2] < t[2]:
                    self.inherit[k] = t
            if b.dsem is not None:
                b.ds[2] = b.dcount
                (self.dpool_sw if b.sw else self.dpool).append(b.ds)
        del self.live[nl:]
        self.aptr = aptr

    def psum_init(self):
        for i in range(8):
            h = self.es.enter_context(self.nc.psum_tensor("psb%d" % i, [128, 512], F32))
            self.psum_banks.append(Buf("psb%d" % i, h))

    NROT = 8
    pctx = None

    def psum(self):
        if self.pctx is not None:
            c = self.pctx
            b = self.psum_banks[c['banks'][c['i'] % len(c['banks'])]]
            c['i'] += 1
            return b
        b = self.psum_banks[self.psum_rr % self.NROT]
        self.psum_rr += 1
        return b

    def dram(self, name, shape, dt, kind=None):
        if kind:
            h = self.nc.dram_tensor(name, list(shape), dt, kind=kind)
        else:
            h = self.nc.dram_tensor(name, list(shape), dt)
        b = Buf(name, h)
        b.is_dram = True
        return b

    def _deps(self, eng, reads, writes, skip_dkey=None, skip_base=0):
        toks = []
        for b in reads:
            if b.last_write is not None:
                toks.append(b.last_write)
            toks.extend(b.wtoks.values())
        for b in writes:
            if b.last_write is not None:
                toks.append(b.last_write)
            toks.extend(b.wtoks.values())
            toks.extend(b.reads.values())
        need = []
        for t in toks:
            key = t[:2]
            val = t[2]
            if t[0] == 'e' and eng == 'pe' and t[1].startswith('pe#'):
                continue
            if t[0] == 'd' and skip_dkey is not None and t[1] == skip_dkey and val > skip_base:
                continue
            if self.waited[eng].get(key, -1) >= val:
                continue
            self.waited[eng][key] = val
            need.append(t)
            if t[0] == 'e':
                self.miles[t[1]].add(val)
        return need

    def op(self, eng, fn, reads=(), writes=()):
        need = self._deps(eng, reads, writes)
        idx = len(self.ins[eng])
        key = self.ekey[eng]
        self.ins[eng].append((fn, need, None, key))
        tok = ('e', key, idx)
        for b in reads:
            b.reads[('e', key)] = tok
        for b in writes:
            b.last_write = tok
            b.reads = {}
        return tok

    def dma(self, eng, out_buf, out_ap, in_buf, in_ap):
        if out_buf.is_dram:
            sb = in_buf
            assert not in_buf.is_dram
        else:
            sb = out_buf
        assert sb.dsem is not None, sb.name
        assert (eng == 'pool') == bool(getattr(sb, 'sw', False)), ("dma queue/sem kind mismatch", sb.name, eng)
        need = self._deps(eng, [in_buf], [out_buf], skip_dkey=sb.dkey, skip_base=getattr(sb, 'dbase', 0))
        sb.dcount += 1
        tok = ('d', sb.dkey, 16 * sb.dcount)
        self.ins[eng].append((lambda e: e.dma_start(out=out_ap, in_=in_ap), need, sb.dsem, None))
        in_buf.reads[('d', sb.dkey)] = tok
        if out_buf.is_dram:
            out_buf.wtoks[sb.dkey] = tok
        else:
            out_buf.last_write = tok
        return tok

    def wait_all_writes(self, eng, bufs):
        need = self._deps(eng, bufs, [])
        self.ins[eng].append((None, need, None, None))

    def emit(self):
        nc = self.nc
        ranks = {k: {idx: r + 1 for r, idx in enumerate(sorted(v))} for k, v in self.miles.items()}
        kb = self

        def run(name, eng):
            for idx, (fn, need, dsem, key) in enumerate(kb.ins[name]):
                for t in need:
                    if t[0] == 'e':
                        eng.wait_ge(kb.psem[t[1]], ranks[t[1]][t[2]])
                    else:
                        eng.wait_ge(kb.dsems[t[1]], t[2])
                if fn is None:
                    continue
                ins = fn(eng)
                if dsem is not None:
                    ins.then_inc(dsem, 16)
                elif key is not None and idx in ranks[key]:
                    ins.then_inc(kb.psem[key], 1)

        with nc.Block() as block:
            @block.tensor
            def _(e):
                run('pe', e)

            @block.vector
            def _(e):
                run('dve', e)

            @block.scalar
            def _(e):
                run('act', e)

            @block.gpsimd
            def _(e):
                run('pool', e)

            @block.sync
            def _(e):
                run('sp', e)
        return {e: (len(self.ins[e]), max(len(v) for k, v in ranks.items() if k.startswith(e + '#'))) for e in ENGS}


S = 4096
D = 1024
DIN = 3128
L = 4
ALPHA = 8.0 ** 0.25

A0, B0, C0, D0 = 0, 896, 1680, 2712
UT0, UTN = 1024, 1688
UT_BK, UT_BV, UT_BG = 0, 128, 400
UT_CV, UT_CO, UT_CG = 1168, 1424, 1680
FM = [('A', 0, 896), ('Bq', 896, 128), ('Bk', 1024, 128), ('Bad', 1408, 16),
      ('Cqk', 1680, 512), ('Dcq', 2712, 256), ('Dckv', 2968, 128), ('Dkr', 3096, 32)]
FMOFF = {}
_r = 0
for _n, _c, _w in FM:
    FMOFF[_n] = _r
    _r += ((_w + 127) // 128) * 128
UFN = _r

CI_ID, CI_TRI, CI_TRIS, CI_ONES, CI_HM, CI_HMROW = 0, 128, 256, 384, 512, 520
CI_HM2 = CI_HMROW + 512
CI_FREQ = CI_HM2 + 2
CI_SGN = CI_HM2 + 3
CI_TRISL = CI_HM2 + 8
CI_BLK64 = CI_TRISL + 128
NCONST = CI_BLK64 + 128

RP = {}
_o = 0
for _n, _w in [('ln1_g', 1024), ('ln1_b', 1024), ('ln2_g', 1024), ('ln2_b', 1024), ('ple_b_gate', 1024),
               ('gla_norm_g', 256), ('mlstm_norm_g', 256), ('rwkv_gn_g', 256), ('rwkv_gn_b', 256),
               ('moe_b_rg', 4), ('moe_b_re', 32), ('mlstm_i_b', 4), ('mlstm_f_b', 4)]:
    RP[_n] = (_o, _w)
    _o += _w
NROW = (_o + 7) // 8 * 8
CP = {'rwkv_mu': (0, 7), 'rwkv_w0': (7, 2), 'rwkv_a0': (9, 2), 'rwkv_k_k': (11, 2), 'rwkv_k_a': (13, 2),
      'rwkv_r_k': (15, 2), 'conv_w': (17, 16), 'conv_b': (33, 4), 'mla_q_norm_g': (37, 2), 'mla_kv_norm_g': (39, 1)}
NCOL = 40


def make_consts():
    c = np.zeros((128, NCONST), np.float32)
    c[:, CI_ID:CI_ID + 128] = np.eye(128, dtype=np.float32)
    j = np.arange(128)[:, None]
    i = np.arange(128)[None, :]
    c[:, CI_TRI:CI_TRI + 128] = (j <= i)
    c[:, CI_TRIS:CI_TRIS + 128] = (j > i)
    c[:, CI_ONES:CI_ONES + 128] = 1.0
    c[:, CI_TRISL:CI_TRISL + 128] = (j < i)
    c[0:64, CI_BLK64:CI_BLK64 + 64] = 1.0
    c[64:128, CI_BLK64 + 64:CI_BLK64 + 128] = 1.0
    for h in range(4):
        c[h * 32:(h + 1) * 32, CI_HM + h] = 1.0
        c[:, CI_HMROW + h * 128 + h * 32:CI_HMROW + h * 128 + (h + 1) * 32] = 1.0
    c[0:64, CI_HM2] = 1.0
    inv = (10000.0 ** (-np.arange(0, 32, 2, dtype=np.float32) / 32)).astype(np.float32)
    c[64:96, CI_FREQ] = np.concatenate([inv, inv])
    c[64:80, CI_SGN] = -1.0
    c[80:96, CI_SGN] = 1.0
    c[64:128, CI_HM2 + 1] = 1.0
    return c


def pack_params(inp):
    rowp = np.zeros((L, NROW), np.float32)
    for n, (o, w) in RP.items():
        rowp[:, o:o + w] = np.asarray(inp[n]).reshape(L, w)
    colp = np.zeros((L, 128, NCOL), np.float32)

    def put(name, arr, ncols):
        o, w = CP[name]
        assert w == ncols
        colp[:, :, o:o + w] = np.asarray(arr).reshape(L, ncols, 128).transpose(0, 2, 1)
    put('rwkv_mu', inp['rwkv_mu'], 7)
    put('rwkv_w0', inp['rwkv_w0'], 2)
    put('rwkv_a0', inp['rwkv_a0'], 2)
    put('rwkv_k_k', inp['rwkv_k_k'], 2)
    put('rwkv_k_a', inp['rwkv_k_a'], 2)
    put('rwkv_r_k', np.asarray(inp['rwkv_r_k']).reshape(L, 256), 2)
    put('conv_w', np.asarray(inp['mlstm_conv_w']).reshape(L, 4 * 512), 16)
    put('conv_b', inp['mlstm_conv_b'], 4)
    put('mla_q_norm_g', inp['mla_q_norm_g'], 2)
    put('mla_kv_norm_g', inp['mla_kv_norm_g'], 1)
    return rowp, colp


IN_NAMES = ['w_in', 'gla_alpha_up', 'gla_alpha_b', 'mla_w_uq', 'mla_w_ukv', 'rwkv_w_up', 'rwkv_a_up', 'rwkv_g_up',
            'w_out', 'moe_w_rg', 'moe_w_re', 'moe_w_gate', 'moe_w_up', 'moe_w_down', 'ple_w_gate', 'ple_w']


def core_inputs(inp, b, shared=None):
    if shared is None:
        rowp, colp = pack_params(inp)
        shared = {'consts': make_consts(), 'rowp': rowp, 'colp': colp}
        for n in IN_NAMES:
            shared[n] = np.ascontiguousarray(inp[n], dtype=np.float32)
    m = dict(shared)
    m['x'] = np.ascontiguousarray(inp['x'][b])
    m['p'] = np.ascontiguousarray(inp['p'][:, b])
    m['positions'] = np.ascontiguousarray(inp['positions'][b:b + 1]).astype(np.int32)
    return m


class Ring:
    def __init__(self, kb, name, n, shape, dt, dma=False):
        self.bufs = [kb.sbuf("%s%d" % (name, i), shape, dt, dma=dma) for i in range(n)]
        self.i = 0

    def next(self):
        b = self.bufs[self.i % len(self.bufs)]
        self.i += 1
        return b


def bc3(ap, shape, axis):
    return ap.unsqueeze(axis).to_broadcast(list(shape))


def build(n_layers=L, stop=None, only=None):
    nc = bass.Bass("TRN2", target_bir_lowering=False)
    with ExitStack() as es:
        kb = KB(nc, es)
        kb.psum_init()
        x_in = kb.dram("x", [S, D], F32, "ExternalInput")
        w_in = kb.dram("w_in", [L, D, DIN], F32, "ExternalInput")
        consts = kb.dram("consts", [128, NCONST], F32, "ExternalInput")
        rowp = kb.dram("rowp", [L, NROW], F32, "ExternalInput")
        colp = kb.dram("colp", [L, 128, NCOL], F32, "ExternalInput")
        gla_aup = kb.dram("gla_alpha_up", [L, 16, 128], F32, "ExternalInput")
        gla_ab = kb.dram("gla_alpha_b", [L, 128], F32, "ExternalInput")
        pos_in = kb.dram("positions", [1, S], I32, "ExternalInput")
        rw_wup = kb.dram("rwkv_w_up", [L, 32, 256], F32, "ExternalInput")
        w_out = kb.dram("w_out", [L, D, D], F32, "ExternalInput")
        w_rg = kb.dram("moe_w_rg", [L, D, 4], F32, "ExternalInput")
        w_re = kb.dram("moe_w_re", [L, D, 32], F32, "ExternalInput")
        w_gate = kb.dram("moe_w_gate", [L, 32, D, 512], F32, "ExternalInput")
        w_up = kb.dram("moe_w_up", [L, 32, D, 512], F32, "ExternalInput")
        w_down = kb.dram("moe_w_down", [L, 32, 512, D], F32, "ExternalInput")
        ple_wg = kb.dram("ple_w_gate", [L, D, D], F32, "ExternalInput")
        ple_w = kb.dram("ple_w", [L, 256, D], F32, "ExternalInput")
        p_in = kb.dram("p", [L, S, 256], F32, "ExternalInput")
        X1 = kb.dram("X1", [S, D], F32, "ExternalOutput" if stop == 'WO' else None)
        X2 = kb.dram("X2", [S, D], F32)
        OUT = kb.dram("out", [S, D], F32, "ExternalOutput" if stop in (None, 'FF') else None)
        GD = kb.dram("GD", [S, 32], F32, "ExternalOutput" if stop == 'WO' else None)
        rw_aup = kb.dram("rwkv_a_up", [L, 32, 256], F32, "ExternalInput")
        rw_gup = kb.dram("rwkv_g_up", [L, 64, 256], F32, "ExternalInput")
        w_uq = kb.dram("mla_w_uq", [L, 256, 384], F32, "ExternalInput")
        w_ukv = kb.dram("mla_w_ukv", [L, 128, 512], F32, "ExternalInput")
        COS2 = kb.dram("COS2", [128, S], F32)
        SIN2 = kb.dram("SIN2", [128, S], F32)
        dbg = stop is not None
        UF = kb.dram("UF", [UFN, S], F32, "ExternalOutput" if stop == 'P1' else None)
        UT = kb.dram("UT", [S, UTN], F32, "ExternalOutput" if stop == 'P1' else None)
        Y = kb.dram("Y", [S, D], F32, "ExternalOutput" if stop == 'MIX' else None)

        def row_bc_ap(l, name, n0=0, n=None):
            o, w = RP[name]
            if n is None:
                n = w
            return bass.AP(rowp.h, l * NROW + o + n0, [[0, 128], [1, n]])

        def mm(ps, out_ap, lhsT, rhs, reads, start=True, stop=True):
            kb.op('pe', lambda e: e.matmul(out_ap, lhsT=lhsT, rhs=rhs, start=start, stop=stop), reads, [ps])

        def act(out_ap, in_ap, func, reads, writes, bias=0.0, scale=1.0):
            kb.op('act', lambda e: e.activation(out=out_ap, in_=in_ap, func=func, bias=bias, scale=scale), reads, writes)

        def tt(eng, out_ap, in0, in1, op, reads, writes):
            kb.op(eng, lambda e: e.tensor_tensor(out=out_ap, in0=in0, in1=in1, op=op), reads, writes)

        def ts(eng, out_ap, in0, s1, s2, op0, op1, reads, writes):
            kb.op(eng, lambda e: e.tensor_scalar(out=out_ap, in0=in0, scalar1=s1, scalar2=s2, op0=op0, op1=op1), reads, writes)

        def stt(eng, out_ap, in0, scalar, in1, op0, op1, reads, writes):
            eng = 'dve'
            kb.op(eng, lambda e: e.scalar_tensor_tensor(out=out_ap, in0=in0, scalar=scalar, in1=in1, op0=op0, op1=op1), reads, writes)

        def cp(eng, out_ap, in_ap, reads, writes):
            if eng == 'act':
                kb.op('act', lambda e: e.copy(out=out_ap, in_=in_ap), reads, writes)
            else:
                kb.op(eng, lambda e: e.tensor_copy(out=out_ap, in_=in_ap), reads, writes)

        def rsqrt(buf, out_ap, in_ap, scale, eps):
            act(out_ap, in_ap, AF.Sqrt, [buf], [buf], bias=eps, scale=scale)
            kb.op('dve', lambda e: e.reciprocal(out=out_ap, in_=out_ap), [buf], [buf])

        def memset(eng, buf, ap, val):
            kb.op(eng, lambda e: e.memset(ap, val), [], [buf])

        cst = kb.sbuf("cst", [128, NCONST], F32, dma=True)
        kb.dma('sp', cst, cst[:, :], consts, consts[:, :])
        cstb = kb.sbuf("cstb", [128, NCONST], BF16)
        cp('dve', cstb[:, :], cst[:, :], [cst], [cstb])
        ident = cst[:, CI_ID:CI_ID + 128]

        Gall = kb.sbuf("Gall", [128, 32 * 32], F32)
        xT_base = kb.aptr
        xT = [kb.sbuf("xT%d" % h, [128, 8 * 2048], BF16) for h in range(2)]
        cnt = [0]

        def evac(out_ap, in_ap, reads, writes):
            cnt[0] += 1
            cp('act' if cnt[0] % 2 else 'dve', out_ap, in_ap, reads, writes)

        def transpose_to_xT(src, tt_):
            h, tl = tt_ // 16, (tt_ % 16) * 128
            for g in range(2):
                ps = kb.psum()
                for j in range(4):
                    kc = g * 4 + j
                    kb.op('pe', lambda e, ps=ps, j=j, kc=kc: e.transpose(
                        out=ps[:, j * 128:(j + 1) * 128], in_=src[:, kc * 128:(kc + 1) * 128],
                        identity=ident), [src, cst], [ps])
                dst = xT[h][:, :].rearrange("p (k t) -> p k t", k=8)[:, g * 4:(g + 1) * 4, tl:tl + 128]
                srcp = ps[:, :].rearrange("p (k t) -> p k t", k=4)
                evac(dst, srcp, [ps], [xT[h]])

        def transpose_tile(src, tt_, x32=None):
            h, tl = tt_ // 16, (tt_ % 16) * 128
            for g in range(2):
                ps = kb.psum()
                for j in range(4):
                    kc = g * 4 + j
                    kb.op('pe', lambda e, ps=ps, j=j, kc=kc: e.transpose(
                        out=ps[:, j * 128:(j + 1) * 128], in_=src[:, kc * 128:(kc + 1) * 128],
                        identity=ident), [src, cst], [ps])
                dst = xT[h][:, :].rearrange("p (k t) -> p k t", k=8)[:, g * 4:(g + 1) * 4, tl:tl + 128]
                if x32 is None:
                    evac(dst, ps[:, :].rearrange("p (k t) -> p k t", k=4), [ps], [xT[h]])
                else:
                    cp('act', x32[:, g * 512:(g + 1) * 512], ps[:, :], [ps], [x32])
                    cp('pool', dst, x32[:, g * 512:(g + 1) * 512].rearrange("p (k t) -> p k t", k=4), [x32], [xT[h]])

        def layer_norm_tile(r, gb, o_g, o_b, eps, wk):
            for hf_ in range(2):
                kb.op('dve', lambda e, hf_=hf_: e.bn_stats(out=wk[:, hf_ * 6:(hf_ + 1) * 6], in_=r[:, hf_ * 512:(hf_ + 1) * 512]), [r], [wk])
            kb.op('dve', lambda e: e.bn_aggr(out=wk[:, 12:14], in_=wk[:, 0:12]), [wk], [wk])
            rsqrt(wk, wk[:, 14:15], wk[:, 13:14], 1.0, eps)
            ts('dve', r[:, :], r[:, :], wk[:, 12:13], wk[:, 14:15], ALU.subtract, ALU.mult, [r, wk], [r])
            tt('pool', r[:, :], r[:, :], gb[:, o_g:o_g + 1024], ALU.mult, [r, gb], [r])
            tt('pool', r[:, :], r[:, :], gb[:, o_b:o_b + 1024], ALU.add, [r, gb], [r])

        mk = kb.mark()
        xt_ring = Ring(kb, "xt", 3, [128, D], F32, dma=True)
        for tt_ in range(32):
            xt = xt_ring.next()
            kb.dma('sp', xt, xt[:, :], x_in, x_in[tt_ * 128:(tt_ + 1) * 128, :])
            transpose_to_xT(xt, tt_)
        kb.release(mk)

        mk = kb.mark()
        posi = kb.sbuf("posi", [128, 1024], I32, dma=True)
        ang = kb.sbuf("ang", [128, 1024], F32)
        kf = kb.sbuf("kf", [128, 1024], F32)
        cs_r = Ring(kb, "cs", 2, [128, 1024], F32, dma=True)
        PI = float(np.pi)
        for q4 in range(4):
            kb.dma('sp', posi, posi[:, :], pos_in, bass.AP(pos_in.h, q4 * 1024, [[0, 128], [1, 1024]]))
            cp('dve', ang[:, :], posi[:, :], [posi], [ang])
            ts('dve', ang[:, :], ang[:, :], cst[:, CI_FREQ:CI_FREQ + 1], None, ALU.mult, ALU.bypass, [ang, cst], [ang])
            for which, shift, dst in ((0, 0.5 * PI, COS2), (1, 0.0, SIN2)):
                t_ = cs_r.next()
                ts('dve', t_[:, :], ang[:, :], shift, 1.0 / (2 * PI), ALU.add, ALU.mult, [ang], [t_])
                cp('dve', posi[:, :], t_[:, :], [t_], [posi])
                cp('dve', kf[:, :], posi[:, :], [posi], [kf])
                ts('dve', t_[:, :], ang[:, :], shift, None, ALU.add, ALU.bypass, [ang], [t_])
                stt('dve', t_[:, :], kf[:, :], -2 * PI, t_[:, :], ALU.mult, ALU.add, [kf, t_], [t_])
                ts('dve', kf[:, :], t_[:, :], PI, -2 * PI, ALU.is_gt, ALU.mult, [t_], [kf])
                tt('dve', t_[:, :], t_[:, :], kf[:, :], ALU.add, [t_, kf], [t_])
                ts('dve', kf[:, :], t_[:, :], -PI, 2 * PI, ALU.is_lt, ALU.mult, [t_], [kf])
                tt('dve', t_[:, :], t_[:, :], kf[:, :], ALU.add, [t_, kf], [t_])
                act(t_[:, :], t_[:, :], AF.Sin, [t_], [t_])
                if which == 1:
                    ts('dve', t_[:, :], t_[:, :], cst[:, CI_SGN:CI_SGN + 1], None, ALU.mult, ALU.bypass, [t_, cst], [t_])
                kb.dma('sp', dst, dst[:, q4 * 1024:(q4 + 1) * 1024], t_, t_[:, :])
        kb.release(mk)

        for l in range(n_layers):
            if l > 0:
                kb.new_epoch()
            if only is None or 'P1' in only:
                mk = kb.mark()
                Win = kb.sbuf("Win", [128, 8 * DIN], BF16, dma='sw')
                st_ring = Ring(kb, "stg", 4, [128, 512], F32, dma=True)
                for kc in range(8):
                    kb.dma('pool', Win, Win[:, kc * DIN:(kc + 1) * DIN], w_in, w_in[l, kc * 128:(kc + 1) * 128, :])
                for name, c0, w in FM:
                    for ci in range((w + 127) // 128):
                        cc = c0 + ci * 128
                        n = min(128, c0 + w - cc)
                        row0 = FMOFF[name] + ci * 128
                        for tc in range(8):
                            h, tl = tc // 4, (tc % 4) * 512
                            ps = kb.psum()
                            for kc in range(8):
                                mm(ps, ps[0:n, :], Win[:, kc * DIN + cc:kc * DIN + cc + n],
                                   xT[h][:, kc * 2048 + tl:kc * 2048 + tl + 512], [Win, xT[h]],
                                   start=(kc == 0), stop=(kc == 7))
                            stg = st_ring.next()
                            evac(stg[0:n, :], ps[0:n, :], [ps], [stg])
                            kb.dma('sp', UF, UF[row0:row0 + n, tc * 512:(tc + 1) * 512], stg, stg[0:n, :])
                for tt_ in range(32):
                    h, tl = tt_ // 16, (tt_ % 16) * 128
                    for g in range(4):
                        g0 = g * 512
                        n = min(512, UTN - g0)
                        ps = kb.psum()
                        for kc in range(8):
                            mm(ps, ps[:, 0:n], xT[h][:, kc * 2048 + tl:kc * 2048 + tl + 128],
                               Win[:, kc * DIN + UT0 + g0:kc * DIN + UT0 + g0 + n], [Win, xT[h]],
                               start=(kc == 0), stop=(kc == 7))
                        stg = st_ring.next()
                        evac(stg[:, 0:n], ps[:, 0:n], [ps], [stg])
                        kb.dma('sp', UT, UT[tt_ * 128:(tt_ + 1) * 128, g0:g0 + n], stg, stg[:, 0:n])
                kb.release(mk)
            if stop == 'P1':
                break


            def genA():
                cpl = kb.sbuf("cplA", [128, NCOL], F32, dma=True)
                kb.dma('sp', cpl, cpl[:, :], colp, colp[l, :, :])
                omu, ow0, oa0, okk, oka, ork = [CP[n_][0] for n_ in ('rwkv_mu', 'rwkv_w0', 'rwkv_a0', 'rwkv_k_k', 'rwkv_k_a', 'rwkv_r_k')]
                gnb = kb.sbuf("gnb", [128, 512], F32, dma=True)
                kb.dma('sp', gnb, gnb[:, 0:256], rowp, row_bc_ap(l, 'rwkv_gn_g'))
                kb.dma('sp', gnb, gnb[:, 256:512], rowp, row_bc_ap(l, 'rwkv_gn_b'))
                WP = kb.sbuf("WP", [128, 768], BF16, dma='sw')
                memset('pool', WP, WP[:, :], 0.0)
                kb.dma('pool', WP, WP[0:32, 0:256], rw_wup, rw_wup[l, :, :])
                kb.dma('pool', WP, WP[32:64, 256:512], rw_aup, rw_aup[l, :, :])
                kb.dma('pool', WP, WP[64:128, 512:768], rw_gup, rw_gup[l, :, :])
                ST32 = [kb.sbuf("ST32_%d" % p_, [128, 64], F32) for p_ in range(2)]
                STb = [kb.sbuf("STb_%d" % p_, [128, 64], BF16) for p_ in range(2)]
                for p_ in range(2):
                    memset('dve', ST32[p_], ST32[p_][:, :], 0.0)
                    memset('dve', STb[p_], STb[p_][:, :], 0.0)
                BKM_r = Ring(kb, "BKM", 2, [128, 1024], BF16)
                for b_ in BKM_r.bufs:
                    memset('pool', b_, b_[:, :], 0.0)
                raw_r = Ring(kb, "rawA", 2, [128, 7 * 129], F32, dma=True)
                us_r = Ring(kb, "us", 2, [128, 896], F32)
                dd_r = Ring(kb, "dd", 1, [128, 896], F32)
                T6_r = Ring(kb, "T6", 2, [128, 128], BF16)
                f_r = Ring(kb, "fA", 2, [128, 4096], F32)
                bA_r = Ring(kb, "bA", 2, [128, 3072], BF16)
                mats_r = Ring(kb, "mats", 2, [128, 2560], BF16)
                Mt_r = Ring(kb, "Mt", 2, [128, 7 * 512], BF16)
                X_r = Ring(kb, "XA", 2, [128, 256], F32)
                Xb_r = Ring(kb, "XbA", 3, [128, 256], BF16)
                ep_r = Ring(kb, "epA", 2, [128, 1024], F32)
                sm_r = Ring(kb, "smA", 2, [128, 32], F32)
                ya_r = Ring(kb, "ya", 2, [128, 256], F32, dma=True)
                oA = FMOFF['A']
                ONES = cst[:, CI_ONES:CI_ONES + 128]
                BLK = cst[:, CI_BLK64:CI_BLK64 + 128]
                mTRI = cstb[:, CI_TRI:CI_TRI + 128]
                mTRISL = cstb[:, CI_TRISL:CI_TRISL + 128]
                mTRIS = cstb[:, CI_TRIS:CI_TRIS + 128]
                HM2b = cstb[:, CI_HM2:CI_HM2 + 2]
                DK = 0.6065306597126334

                def v3(ap, a):
                    return ap.rearrange("p (a t) -> p a t", a=a)
                for c in range(32):
                    t0 = c * 128
                    raw = raw_r.next()
                    r3 = v3(raw[:, :], 7)
                    if c == 0:
                        memset('dve', raw, raw[:, :], 0.0)
                        kb.dma('sp', raw, r3[:, :, 1:129], UF, UF[oA:oA + 896, 0:128].rearrange("(c p) t -> p c t", p=128))
                    else:
                        kb.dma('sp', raw, r3[:, :, :], UF, UF[oA:oA + 896, t0 - 1:t0 + 128].rearrange("(c p) t -> p c t", p=128))
                    yield
                    dd = dd_r.next()
                    us = us_r.next()
                    tt('dve', v3(dd[:, :], 7), r3[:, :, 0:128], r3[:, :, 1:129], ALU.subtract, [raw], [dd])
                    tt('pool', v3(dd[:, :], 7), v3(dd[:, :], 7), bc3(cpl[:, omu:omu + 7], [128, 7, 128], 2), ALU.mult, [dd, cpl], [dd])
                    tt('dve', v3(us[:, :], 7), v3(dd[:, :], 7), r3[:, :, 1:129], ALU.add, [dd, raw], [us])
                    R_ = us[:, 0:256]
                    K_ = us[:, 256:512]
                    V_ = us[:, 512:768]
                    yield
                    T6 = T6_r.next()
                    act(T6[0:32, :], us[0:32, 768:896], AF.Tanh, [us], [T6])
                    cp('pool', T6[32:64, :], us[32:64, 768:896], [us], [T6])
                    act(T6[64:128, :], us[64:128, 768:896], AF.Sigmoid, [us], [T6])
                    yield
                    psz = kb.psum()
                    for p_ in range(2):
                        mm(psz, psz[:, p_ * 128:(p_ + 1) * 128], WP[:, p_ * 128:(p_ + 1) * 128], T6[:, :], [WP, T6])
                        mm(psz, psz[:, 256 + p_ * 128:256 + (p_ + 1) * 128], WP[:, 256 + p_ * 128:256 + (p_ + 1) * 128], T6[:, :], [WP, T6])
                    f = f_r.next()
                    F_ = lambda i_: f[:, i_ * 256:(i_ + 1) * 256]
                    SG, AI, KKt, SQ, CS, E1, E1m, E2, E3, BV, KP, TMP = [F_(i_) for i_ in range(12)]
                    for p_ in range(2):
                        act(SG[:, p_ * 128:(p_ + 1) * 128], psz[:, p_ * 128:(p_ + 1) * 128], AF.Sigmoid, [psz, cpl], [f],
                            bias=cpl[:, ow0 + p_:ow0 + p_ + 1])
                        act(AI[:, p_ * 128:(p_ + 1) * 128], psz[:, 256 + p_ * 128:256 + (p_ + 1) * 128], AF.Sigmoid, [psz, cpl], [f],
                            bias=cpl[:, oa0 + p_:oa0 + p_ + 1])
                    yield
                    tt('dve', v3(KKt, 2), v3(K_, 2), bc3(cpl[:, okk:okk + 2], [128, 2, 128], 2), ALU.mult, [us, cpl], [f])
                    act(SQ, KKt, AF.Square, [f], [f])
                    psn = kb.psum()
                    mm(psn, psn[:, 0:256], BLK, SQ, [cst, f])
                    act(SQ, psn[:, 0:256], AF.Sqrt, [psn], [f])
                    ts('dve', SQ, SQ, 1e-12, None, ALU.max, ALU.bypass, [f], [f])
                    kb.op('dve', lambda e, SQ=SQ: e.reciprocal(out=SQ, in_=SQ), [f], [f])
                    tt('dve', KKt, KKt, SQ, ALU.mult, [f], [f])
                    yield
                    tt('pool', BV, KKt, AI, ALU.mult, [f], [f])
                    ts('dve', TMP, AI, -1.0, None, ALU.add, ALU.bypass, [f], [f])
                    tt('dve', v3(TMP, 2), v3(TMP, 2), bc3(cpl[:, oka:oka + 2], [128, 2, 128], 2), ALU.mult, [f, cpl], [f])
                    stt('dve', KP, TMP, 1.0, K_, ALU.add, ALU.mult, [f, us], [f])
                    yield
                    for p_ in range(2):
                        kb.op('dve', lambda e, CS=CS, SG=SG, p_=p_: e.tensor_tensor_scan(
                            out=CS[:, p_ * 128:(p_ + 1) * 128], data0=ONES, data1=SG[:, p_ * 128:(p_ + 1) * 128], initial=0.0,
                            op0=ALU.mult, op1=ALU.add), [f, cst], [f])
                    sm = sm_r.next()
                    ts('dve', sm[:, 0:2], v3(CS, 2)[:, :, 127], -DK, None, ALU.mult, ALU.bypass, [f], [sm])
                    act(E1, CS, AF.Exp, [f], [f], scale=-DK)
                    act(E2, CS, AF.Exp, [f], [f], scale=DK)
                    tt('pool', TMP, CS, SG, ALU.subtract, [f], [f])
                    act(E1m, TMP, AF.Exp, [f], [f], scale=-DK)
                    for p_ in range(2):
                        act(E3[:, p_ * 128:(p_ + 1) * 128], CS[:, p_ * 128:(p_ + 1) * 128], AF.Exp, [f, sm], [f], scale=DK,
                            bias=sm[:, p_:p_ + 1])
                    yield
                    bA = bA_r.next()
                    At, Am, Bt, Kt, Rm, RKP, Vb = (bA[:, 0:256], bA[:, 256:768], bA[:, 768:1024], bA[:, 1024:1280],
                                                   bA[:, 1280:1792], bA[:, 1792:2048], bA[:, 2048:2304])
                    Rt = bA[:, 2304:2560]
                    stt('dve', At, KKt, -1.0, E1m, ALU.mult, ALU.mult, [f], [bA])
                    tt('pool', Bt, BV, E2, ALU.mult, [f], [bA])
                    tt('pool', Kt, KP, E2, ALU.mult, [f], [bA])
                    tt('dve', Rt, R_, E1, ALU.mult, [us, f], [bA])
                    for p_ in range(2):
                        tt('dve' if p_ else 'pool', v3(Am[:, p_ * 256:(p_ + 1) * 256], 2), bc3(At[:, p_ * 128:(p_ + 1) * 128], [128, 2, 128], 1),
                           bc3(HM2b, [128, 2, 128], 2), ALU.mult, [bA, cstb], [bA])
                        tt('pool' if p_ else 'dve', v3(Rm[:, p_ * 256:(p_ + 1) * 256], 2), bc3(Rt[:, p_ * 128:(p_ + 1) * 128], [128, 2, 128], 1),
                           bc3(HM2b, [128, 2, 128], 2), ALU.mult, [bA, cstb], [bA])
                    tt('pool', TMP, R_, KP, ALU.mult, [us, f], [f])
                    tt('dve', v3(RKP, 2), v3(TMP, 2), bc3(cpl[:, ork:ork + 2], [128, 2, 128], 2), ALU.mult, [f, cpl], [bA])
                    tt('pool', E1m, BV, E3, ALU.mult, [f], [f])
                    tt('pool', E2, KP, E3, ALU.mult, [f], [f])
                    pstr = kb.psum()
                    for p_ in range(2):
                        kb.op('pe', lambda e, pstr=pstr, p_=p_, E1m=E1m: e.transpose(
                            out=pstr[:, p_ * 128:(p_ + 1) * 128], in_=E1m[:, p_ * 128:(p_ + 1) * 128], identity=ident), [f, cst], [pstr])
                        kb.op('pe', lambda e, pstr=pstr, p_=p_, E2=E2: e.transpose(
                            out=pstr[:, 256 + p_ * 128:256 + (p_ + 1) * 128], in_=E2[:, p_ * 128:(p_ + 1) * 128], identity=ident), [f, cst], [pstr])
                    BKM = BKM_r.next()
                    for w_ in range(2):
                        src = v3(pstr[:, w_ * 256:(w_ + 1) * 256], 2)
                        dst = v3(BKM[:, w_ * 512:(w_ + 1) * 512], 2)
                        cp('act', dst[:, :, 0:64], src[:, :, 0:64], [pstr], [BKM])
                        cp('dve', dst[:, :, 192:256], src[:, :, 64:128], [pstr], [BKM])
                    pstv = kb.psum()
                    for p_ in range(2):
                        kb.op('pe', lambda e, pstv=pstv, p_=p_, us=us: e.transpose(
                            out=pstv[:, p_ * 128:(p_ + 1) * 128], in_=us[:, 512 + p_ * 128:512 + (p_ + 1) * 128], identity=ident), [us, cst], [pstv])
                    ep = ep_r.next()
                    V32 = ep[:, 0:256]
                    cp('act', V32, pstv[:, 0:256], [pstv], [ep])
                    cp('dve', Vb, V32, [ep], [bA])
                    yield
                    mats = mats_r.next()
                    Akt, Rbt, Rkt = mats[:, 0:512], mats[:, 512:1024], mats[:, 1024:1536]
                    Nn = [mats[:, 1536:2048], mats[:, 2048:2560]]
                    Mt = Mt_r.next()
                    Mtk = lambda k_: Mt[:, k_ * 512:(k_ + 1) * 512]

                    def quad(lhs_of, rhs_of, reads, dst, dbuf, mask, eng):
                        ps = kb.psum()
                        for h in range(4):
                            mm(ps, ps[:, h * 128:(h + 1) * 128], lhs_of(h), rhs_of(h), reads)
                        if mask is None:
                            cp(eng, dst, ps[:, :], [ps], [dbuf])
                        else:
                            tt('dve', v3(dst, 4), v3(ps[:, :], 4), bc3(mask, [128, 4, 128], 1), ALU.mult, [ps, cstb], [dbuf])
                    Bt_p = lambda h: Bt[:, (h // 2) * 128:(h // 2 + 1) * 128]
                    Kt_p = lambda h: Kt[:, (h // 2) * 128:(h // 2 + 1) * 128]
                    Am_h = lambda h: Am[:, h * 128:(h + 1) * 128]
                    Rm_h = lambda h: Rm[:, h * 128:(h + 1) * 128]
                    M0 = Mtk(0)
                    quad(Bt_p, Am_h, [bA], M0, Mt, mTRISL, None)
                    quad(Am_h, Bt_p, [bA], Nn[0], mats, mTRIS, None)
                    quad(Kt_p, Am_h, [bA], Akt, mats, mTRISL, None)
                    quad(Bt_p, Rm_h, [bA], Rbt, mats, mTRI, None)
                    quad(Kt_p, Rm_h, [bA], Rkt, mats, mTRI, None)
                    yield
                    for k_ in range(6):
                        Mk, Nk = Mtk(k_), Nn[k_ % 2]
                        Mn, Nx = Mtk(k_ + 1), Nn[(k_ + 1) % 2]
                        quad(lambda h, Nk=Nk: Nk[:, h * 128:(h + 1) * 128], lambda h, Mk=Mk: Mk[:, h * 128:(h + 1) * 128], [mats, Mt], Mn, Mt, None, 'act')
                        if k_ < 5:
                            quad(lambda h, Mk=Mk: Mk[:, h * 128:(h + 1) * 128], lambda h, Nk=Nk: Nk[:, h * 128:(h + 1) * 128], [mats, Mt], Nx, mats, None, 'dve')
                        yield
                    yield
                    psx = kb.psum()
                    for h in range(4):
                        mm(psx, psx[:, h * 64:(h + 1) * 64], Am_h(h), STb[h // 2][:, :], [bA, STb[h // 2]], start=True, stop=False)
                        mm(psx, psx[:, h * 64:(h + 1) * 64], Akt[:, h * 128:(h + 1) * 128], Vb[:, h * 64:(h + 1) * 64], [mats, bA], start=False, stop=True)
                    X = X_r.next()
                    Xb = Xb_r.next()
                    cp('dve', X[:, :], psx[:, 0:256], [psx], [X])
                    cp('act', Xb[:, :], X[:, :], [X], [Xb])
                    for k_ in range(7):
                        psx = kb.psum()
                        Mk = Mtk(k_)
                        for h in range(4):
                            mm(psx, psx[:, h * 64:(h + 1) * 64], Mk[:, h * 128:(h + 1) * 128], Xb[:, h * 64:(h + 1) * 64], [Mt, Xb])
                        tt('dve', X[:, :], X[:, :], psx[:, 0:256], ALU.add, [X, psx], [X])
                        Xb = Xb_r.next()
                        cp('act', Xb[:, :], X[:, :], [X], [Xb])
                        yield
                    Ub = Xb
                    yield
                    psy = kb.psum()
                    for h in range(4):
                        o_ = psy[:, h * 64:(h + 1) * 64]
                        mm(psy, o_, Rm_h(h), STb[h // 2][:, :], [bA, STb[h // 2]], start=True, stop=False)
                        mm(psy, o_, Rbt[:, h * 128:(h + 1) * 128], Ub[:, h * 64:(h + 1) * 64], [mats, Ub], start=False, stop=False)
                        mm(psy, o_, Rkt[:, h * 128:(h + 1) * 128], Vb[:, h * 64:(h + 1) * 64], [mats, bA], start=False, stop=True)
                    yield
                    pss_ = kb.psum()
                    for p_ in range(2):
                        o_ = pss_[:, p_ * 64:(p_ + 1) * 64]
                        for hh in range(2):
                            h = p_ * 2 + hh
                            mm(pss_, o_, BKM[:, h * 128:(h + 1) * 128], Ub[:, h * 64:(h + 1) * 64], [BKM, Ub], start=(hh == 0), stop=False)
                            mm(pss_, o_, BKM[:, 512 + h * 128:512 + (h + 1) * 128], Vb[:, h * 64:(h + 1) * 64], [BKM, bA], start=False, stop=(hh == 1))
                    for p_ in range(2):
                        stt('dve', ST32[p_][:, :], ST32[p_][:, :], E1[:, p_ * 128 + 127:p_ * 128 + 128], pss_[:, p_ * 64:(p_ + 1) * 64],
                            ALU.mult, ALU.add, [ST32[p_], f, pss_], [ST32[p_]])
                        cp('act', STb[p_][:, :], ST32[p_][:, :], [ST32[p_]], [STb[p_]])
                    yield
                    psg = kb.psum()
                    mm(psg, psg[:, 0:256], T6[:, :], WP[:, 512:768], [T6, WP])
                    for p_ in range(2):
                        mm(psg, psg[:, 256 + 2 * p_:256 + 2 * p_ + 2], RKP[:, p_ * 128:(p_ + 1) * 128], HM2b, [bA, cstb])
                    HV, SQe, GG = ep[:, 256:512], ep[:, 512:768], ep[:, 768:1024]
                    cp('act', GG, psg[:, 0:256], [psg], [ep])
                    cp('dve', sm[:, 4:8], psg[:, 256:260], [psg], [sm])
                    kb.op('dve', lambda e, sm=sm, psy=psy: e.tensor_reduce(out=sm[:, 8:12], in_=v3(psy[:, 0:256], 4), axis=AX.X, op=ALU.add), [psy], [sm])
                    ts('dve', sm[:, 8:12], sm[:, 8:12], 1.0 / 64, None, ALU.mult, ALU.bypass, [sm], [sm])
                    tt('dve', v3(HV, 4), v3(psy[:, 0:256], 4), bc3(sm[:, 8:12], [128, 4, 64], 2), ALU.subtract, [psy, sm], [ep])
                    act(SQe, HV, AF.Square, [ep], [ep])
                    kb.op('dve', lambda e, sm=sm, SQe=SQe: e.tensor_reduce(out=sm[:, 12:16], in_=v3(SQe, 4), axis=AX.X, op=ALU.add), [ep], [sm])
                    rsqrt(sm, sm[:, 12:16], sm[:, 12:16], 1.0 / 64, 64e-5)
                    tt('dve', v3(HV, 4), v3(HV, 4), bc3(sm[:, 12:16], [128, 4, 64], 2), ALU.mult, [ep, sm], [ep])
                    tt('pool', HV, HV, gnb[:, 0:256], ALU.mult, [ep, gnb], [ep])
                    tt('pool', HV, HV, gnb[:, 256:512], ALU.add, [ep, gnb], [ep])
                    tt('dve', v3(SQe, 4), v3(V32, 4), bc3(sm[:, 4:8], [128, 4, 64], 2), ALU.mult, [ep, sm], [ep])
                    tt('pool', HV, HV, SQe, ALU.add, [ep], [ep])
                    ya = ya_r.next()
                    tt('dve', ya[:, :], HV, GG, ALU.mult, [ep], [ya])
                    kb.dma('sp', Y, Y[t0:t0 + 128, 0:256], ya, ya[:, :])

            def genB():
                AUP = kb.sbuf("AUP", [17, 128], BF16, dma='sw')
                kb.dma('pool', AUP, AUP[0:16, :], gla_aup, gla_aup[l, :, :])
                kb.dma('pool', AUP, AUP[16:17, :], gla_ab, gla_ab[l:l + 1, :])
                ngb = kb.sbuf("ngb", [128, 256], F32, dma=True)
                kb.dma('sp', ngb, ngb[:, :], rowp, row_bc_ap(l, 'gla_norm_g'))
                S32 = kb.sbuf("S32", [128, 64], F32)
                Sb = kb.sbuf("Sb", [128, 64], BF16)
                memset('dve', S32, S32[:, :], 0.0)
                memset('dve', Sb, Sb[:, :], 0.0)
                adT_r = Ring(kb, "adT", 2, [17, 128], BF16, dma='sw')
                for b_ in adT_r.bufs:
                    memset('dve', b_, b_[:, :], 1.0)
                qk_r = Ring(kb, "qk32", 2, [128, 256], F32, dma=True)
                tok_r = Ring(kb, "tok", 2, [128, 656], F32, dma=True)
                vb_r = Ring(kb, "vb", 2, [128, 256], BF16)
                w1 = Ring(kb, "w1", 2, [128, 128], F32)
                nl_r = Ring(kb, "nl", 2, [128, 128], F32)
                E_r = Ring(kb, "E", 2, [128, 384], F32)
                qe_r = Ring(kb, "qe", 2, [128, 128], BF16)
                ke_r = Ring(kb, "ke", 2, [128, 128], BF16)
                kd_r = Ring(kb, "kd", 2, [128, 128], BF16)
                Qbd_r = Ring(kb, "Qbd", 2, [128, 512], BF16)
                KDbd_r = Ring(kb, "KDbd", 2, [128, 512], BF16)
                sT_r = Ring(kb, "sT", 2, [128, 512], BF16)
                sq_r = Ring(kb, "sq", 2, [128, 256], F32)
                ss_r = Ring(kb, "ss", 2, [128, 8], F32)
                on_r = Ring(kb, "on", 2, [128, 256], F32)
                sg_r = Ring(kb, "sg", 2, [128, 256], F32)
                yb_r = Ring(kb, "yb", 2, [128, 256], F32, dma=True)
                TRI = cst[:, CI_TRI:CI_TRI + 128]
                TRIS = cst[:, CI_TRIS:CI_TRIS + 128]
                oq, ok_, oad = FMOFF['Bq'], FMOFF['Bk'], FMOFF['Bad']
                for c in range(32):
                    t0 = c * 128
                    adT = adT_r.next()
                    kb.dma('pool', adT, adT[0:16, :], UF, UF[oad:oad + 16, t0:t0 + 128])
                    qk = qk_r.next()
                    kb.dma('sp', qk, qk[:, 0:128], UF, UF[oq:oq + 128, t0:t0 + 128])
                    kb.dma('sp', qk, qk[:, 128:256], UF, UF[ok_:ok_ + 128, t0:t0 + 128])
                    tok = tok_r.next()
                    kb.dma('sp', tok, tok[:, :], UT, UT[t0:t0 + 128, 0:656])
                    vb = vb_r.next()
                    cp('pool', vb[:, :], tok[:, UT_BV:UT_BV + 256], [tok], [vb])
                    yield
                    psz = kb.psum()
                    mm(psz, psz[:, 0:128], adT[0:17, :], AUP[0:17, :], [adT, AUP])
                    ez = w1.next()
                    act(ez[:, :], psz[:, 0:128], AF.Exp, [psz], [ez], scale=-1.0)
                    nl = nl_r.next()
                    act(nl[:, :], ez[:, :], AF.Ln, [ez], [nl], bias=1.0)
                    yield
                    psb = kb.psum()
                    mm(psb, psb[:, 0:128], nl[:, :], TRI, [nl, cst])
                    mm(psb, psb[:, 128:256], TRIS, nl[:, :], [nl, cst])
                    E = E_r.next()
                    act(E[:, 0:128], psb[:, 0:128], AF.Exp, [psb], [E], scale=-1.0 / 16)
                    act(E[:, 128:256], psb[:, 0:128], AF.Exp, [psb], [E], scale=1.0 / 16)
                    act(E[:, 256:384], psb[:, 128:256], AF.Exp, [psb], [E], scale=-1.0 / 16)
                    qe = qe_r.next()
                    stt('dve', qe[:, :], qk[:, 0:128], 32.0 ** -0.5, E[:, 0:128], ALU.mult, ALU.mult, [qk, E], [qe])
                    ke = ke_r.next()
                    tt('pool', ke[:, :], qk[:, 128:256], E[:, 128:256], ALU.mult, [qk, E], [ke])
                    kd = kd_r.next()
                    tt('pool', kd[:, :], tok[:, UT_BK:UT_BK + 128], E[:, 256:384], ALU.mult, [tok, E], [kd])
                    Qbd = Qbd_r.next()
                    tt('dve', Qbd[:, :].rearrange("p (h i) -> p h i", h=4), bc3(qe[:, :], [128, 4, 128], 1),
                       bc3(cstb[:, CI_HM:CI_HM + 4], [128, 4, 128], 2), ALU.mult, [qe, cstb], [Qbd])
                    KDbd = KDbd_r.next()
                    tt('pool', KDbd[:, :].rearrange("p (h i) -> p h i", h=4), bc3(kd[:, :], [128, 4, 128], 1),
                       cstb[:, CI_HMROW:CI_HMROW + 512].rearrange("p (h i) -> p h i", h=4), ALU.mult, [kd, cstb], [KDbd])
                    yield
                    pss = kb.psum()
                    mm(pss, pss[:, 0:512], ke[:, :], Qbd[:, :], [ke, Qbd])
                    sT = sT_r.next()
                    tt('dve', sT[:, :].rearrange("p (h i) -> p h i", h=4), pss[:, :].rearrange("p (h i) -> p h i", h=4),
                       bc3(cstb[:, CI_TRI:CI_TRI + 128], [128, 4, 128], 1), ALU.mult, [pss, cstb], [sT])
                    yield
                    pso = kb.psum()
                    for h in range(4):
                        mm(pso, pso[:, h * 64:(h + 1) * 64], Qbd[:, h * 128:(h + 1) * 128], Sb[:, :], [Qbd, Sb],
                           start=True, stop=False)
                        mm(pso, pso[:, h * 64:(h + 1) * 64], sT[:, h * 128:(h + 1) * 128], vb[:, h * 64:(h + 1) * 64],
                           [sT, vb], start=False, stop=True)
                    yield
                    psu = kb.psum()
                    for h in range(4):
                        mm(psu, psu[:, 0:64], KDbd[:, h * 128:(h + 1) * 128], vb[:, h * 64:(h + 1) * 64], [KDbd, vb],
                           start=(h == 0), stop=(h == 3))
                    stt('dve', S32[:, :], S32[:, :], E[:, 127:128], psu[:, 0:64], ALU.mult, ALU.add, [S32, E, psu], [S32])
                    cp('act', Sb[:, :], S32[:, :], [S32], [Sb])
                    yield
                    sq = sq_r.next()
                    act(sq[:, :], pso[:, 0:256], AF.Square, [pso], [sq])
                    ss = ss_r.next()
                    kb.op('dve', lambda e, ss=ss, sq=sq: e.tensor_reduce(
                        out=ss[:, 0:4], in_=sq[:, :].rearrange("p (h v) -> p h v", h=4), axis=AX.X, op=ALU.add), [sq], [ss])
                    rsqrt(ss, ss[:, 4:8], ss[:, 0:4], 1.0 / 64, 1e-6)
                    on = on_r.next()
                    tt('dve', on[:, :].rearrange("p (h v) -> p h v", h=4), pso[:, 0:256].rearrange("p (h v) -> p h v", h=4),
                       bc3(ss[:, 4:8], [128, 4, 64], 2), ALU.mult, [pso, ss], [on])
                    sg = sg_r.next()
                    act(sg[:, :], tok[:, UT_BG:UT_BG + 256], AF.Silu, [tok], [sg])
                    tt('pool', sg[:, :], sg[:, :], ngb[:, :], ALU.mult, [sg, ngb], [sg])
                    yb = yb_r.next()
                    tt('pool', yb[:, :], on[:, :], sg[:, :], ALU.mult, [on, sg], [yb])
                    kb.dma('sp', Y, Y[t0:t0 + 128, 256:512], yb, yb[:, :])


            def genC():
                cpl = kb.sbuf("cplC", [128, NCOL], F32, dma=True)
                kb.dma('sp', cpl, cpl[:, :], colp, colp[l, :, :])
                ngc = kb.sbuf("ngc", [128, 256], F32, dma=True)
                kb.dma('sp', ngc, ngc[:, :], rowp, row_bc_ap(l, 'mlstm_norm_g'))
                ifb = kb.sbuf("ifb", [128, 8], F32, dma=True)
                kb.dma('sp', ifb, ifb[:, 0:4], rowp, row_bc_ap(l, 'mlstm_i_b'))
                kb.dma('sp', ifb, ifb[:, 4:8], rowp, row_bc_ap(l, 'mlstm_f_b'))
                M32 = [kb.sbuf("M32_%d" % p_, [128, 65], F32) for p_ in range(2)]
                Mb = [kb.sbuf("Mb_%d" % p_, [128, 65], BF16) for p_ in range(2)]
                for p_ in range(2):
                    memset('dve', M32[p_], M32[p_][:, :], 0.0)
                    memset('dve', Mb[p_], Mb[p_][:, :], 0.0)
                raw_r = Ring(kb, "raw", 2, [128, 4 * 131], F32, dma=True)
                tokc_r = Ring(kb, "tokc", 2, [128, 520], F32, dma=True)
                vaug_r = Ring(kb, "vaug", 2, [128, 260], BF16)
                KD2_r = Ring(kb, "KD2", 2, [128, 512], BF16)
                for b_ in vaug_r.bufs:
                    memset('dve', b_, b_[:, :], 1.0)
                for b_ in KD2_r.bufs:
                    memset('dve', b_, b_[:, :], 0.0)
                acc_r = Ring(kb, "cacc", 2, [128, 512], F32)
                qkc_r = Ring(kb, "qkc", 2, [128, 512], F32)
                g_r = Ring(kb, "gts", 2, [128, 32], F32)
                X_r = Ring(kb, "gX", 2, [128, 512], F32)
                EFG_r = Ring(kb, "EFG", 2, [128, 512], F32)
                qkt_r = Ring(kb, "qkt", 2, [128, 512], BF16)
                kdf_r = Ring(kb, "kdf", 2, [128, 256], BF16)
                sTc_r = Ring(kb, "sTc", 2, [128, 512], BF16)
                Q2_r = Ring(kb, "Q2", 2, [128, 512], BF16)
                hv_r = Ring(kb, "hv", 2, [128, 256], F32)
                sqc_r = Ring(kb, "sqc", 2, [128, 256], F32)
                so_r = Ring(kb, "so", 2, [128, 256], F32)
                yc_r = Ring(kb, "yc", 2, [128, 256], F32, dma=True)
                TRI = cst[:, CI_TRI:CI_TRI + 128]
                TRIS = cst[:, CI_TRIS:CI_TRIS + 128]
                oqk = FMOFF['Cqk']
                cw0, cb0 = CP['conv_w'][0], CP['conv_b'][0]
                for c in range(32):
                    t0 = c * 128
                    raw = raw_r.next()
                    r3 = raw[:, :].rearrange("p (c t) -> p c t", c=4)
                    if c == 0:
                        memset('dve', raw, raw[:, :], 0.0)
                        kb.dma('sp', raw, r3[:, :, 3:131], UF, UF[oqk:oqk + 512, 0:128].rearrange("(c p) t -> p c t", p=128))
                    else:
                        kb.dma('sp', raw, r3[:, :, :], UF, UF[oqk:oqk + 512, t0 - 3:t0 + 128].rearrange("(c p) t -> p c t", p=128))
                    tokc = tokc_r.next()
                    kb.dma('sp', tokc, tokc[:, :], UT, UT[t0:t0 + 128, UT_CV:UT_CV + 520])
                    vaug = vaug_r.next()
                    cp('pool', vaug[:, :].rearrange("p (h v) -> p h v", h=4)[:, :, 0:64],
                       tokc[:, 0:256].rearrange("p (h v) -> p h v", h=4), [tokc], [vaug])
                    yield
                    acc = acc_r.next()
                    for ci in range(4):
                        eng = 'dve'
                        a_ = acc[:, ci * 128:(ci + 1) * 128]
                        ts(eng, a_, r3[:, ci, 0:128], cpl[:, cw0 + ci:cw0 + ci + 1], cpl[:, cb0 + ci:cb0 + ci + 1],
                           ALU.mult, ALU.add, [raw, cpl], [acc])
                        for j in range(1, 4):
                            stt(eng, a_, r3[:, ci, j:j + 128], cpl[:, cw0 + j * 4 + ci:cw0 + j * 4 + ci + 1], a_,
                                ALU.mult, ALU.add, [raw, cpl, acc], [acc])
                    qkc = qkc_r.next()
                    act(qkc[:, :], acc[:, :], AF.Silu, [acc], [qkc])
                    yield
                    g = g_r.next()
                    tt('dve', g[:, 0:8], tokc[:, 512:520], ifb[:, 0:8], ALU.add, [tokc, ifb], [g])
                    act(g[:, 4:8], g[:, 4:8], AF.Exp, [g], [g], scale=-1.0)
                    act(g[:, 4:8], g[:, 4:8], AF.Ln, [g], [g], bias=1.0)
                    X = X_r.next()
                    cp('dve', X[:, 0:256].rearrange("p (h v) -> p h v", h=4), bc3(g[:, 4:8], [128, 4, 64], 2), [g], [X])
                    psF = kb.psum()
                    for p_ in range(2):
                        mm(psF, psF[:, p_ * 128:(p_ + 1) * 128], X[:, p_ * 128:(p_ + 1) * 128], TRI, [X, cst])
                    mm(psF, psF[:, 256:260], TRI, g[:, 4:8], [g, cst])
                    mm(psF, psF[:, 260:264], TRIS, g[:, 4:8], [g, cst])
                    tt('dve', g[:, 8:12], g[:, 0:4], psF[:, 256:260], ALU.add, [g, psF], [g])
                    tt('dve', g[:, 16:20], g[:, 0:4], psF[:, 260:264], ALU.subtract, [g, psF], [g])
                    act(g[:, 12:16], g[:, 16:20], AF.Exp, [g], [g])
                    cp('dve', X[:, 256:512].rearrange("p (h v) -> p h v", h=4), bc3(g[:, 8:12], [128, 4, 64], 2), [g], [X])
                    psG = kb.psum()
                    for p_ in range(2):
                        mm(psG, psG[:, p_ * 128:(p_ + 1) * 128], X[:, 256 + p_ * 128:256 + (p_ + 1) * 128], ident, [X, cst])
                    EFG = EFG_r.next()
                    act(EFG[:, 0:256], psF[:, 0:256], AF.Exp, [psF], [EFG], scale=-1.0)
                    act(EFG[:, 256:512], psG[:, 0:256], AF.Exp, [psG], [EFG])
                    yield
                    qkt = qkt_r.next()
                    tt('dve', qkt[:, 0:256], qkc[:, 0:256], EFG[:, 0:256], ALU.mult, [qkc, EFG], [qkt])
                    stt('pool', qkt[:, 256:512], qkc[:, 256:512], 0.125, EFG[:, 256:512], ALU.mult, ALU.mult, [qkc, EFG], [qkt])
                    yield
                    psT = kb.psum()
                    for p_ in range(2):
                        kb.op('pe', lambda e, psT=psT, p_=p_, qkc=qkc: e.transpose(
                            out=psT[:, p_ * 128:(p_ + 1) * 128], in_=qkc[:, 256 + p_ * 128:256 + (p_ + 1) * 128],
                            identity=ident), [qkc, cst], [psT])
                    kdf = kdf_r.next()
                    stt('dve', kdf[:, :].rearrange("p (h v) -> p h v", h=4), psT[:, 0:256].rearrange("p (h v) -> p h v", h=4),
                        0.125, bc3(g[:, 12:16], [128, 4, 64], 2), ALU.mult, ALU.mult, [psT, g], [kdf])
                    KD2 = KD2_r.next()
                    cp('pool', KD2[:, :].rearrange("p (a b) -> p a b", a=2)[:, :, 0:64],
                       kdf[:, :].rearrange("p (a b) -> p a b", a=2)[:, :, 0:64], [kdf], [KD2])
                    cp('pool', KD2[:, :].rearrange("p (a b) -> p a b", a=2)[:, :, 192:256],
                       kdf[:, :].rearrange("p (a b) -> p a b", a=2)[:, :, 64:128], [kdf], [KD2])
                    yield
                    Q2 = Q2_r.next()
                    for p_ in range(2):
                        tt('pool' if p_ else 'dve', Q2[:, p_ * 256:(p_ + 1) * 256].rearrange("p (a i) -> p a i", a=2),
                           bc3(qkt[:, p_ * 128:(p_ + 1) * 128], [128, 2, 128], 1),
                           bc3(cstb[:, CI_HM2:CI_HM2 + 2], [128, 2, 128], 2), ALU.mult, [qkt, cstb], [Q2])
                    pss = kb.psum()
                    for p_ in range(2):
                        mm(pss, pss[:, p_ * 256:(p_ + 1) * 256], qkt[:, 256 + p_ * 128:256 + (p_ + 1) * 128],
                           Q2[:, p_ * 256:(p_ + 1) * 256], [qkt, Q2])
                    sT = sTc_r.next()
                    tt('dve', sT[:, :].rearrange("p (h i) -> p h i", h=4), pss[:, :].rearrange("p (h i) -> p h i", h=4),
                       bc3(cstb[:, CI_TRI:CI_TRI + 128], [128, 4, 128], 1), ALU.mult, [pss, cstb], [sT])
                    yield
                    pso = kb.psum()
                    for h in range(4):
                        p_, hh = h // 2, h % 2
                        mm(pso, pso[:, h * 65:(h + 1) * 65], Q2[:, h * 128:(h + 1) * 128],
                           Mb[p_][:, :], [Q2, Mb[p_]], start=True, stop=False)
                        mm(pso, pso[:, h * 65:(h + 1) * 65], sT[:, h * 128:(h + 1) * 128], vaug[:, h * 65:(h + 1) * 65],
                           [sT, vaug], start=False, stop=True)
                    yield
                    psM = kb.psum()
                    for p_ in range(2):
                        for hh in range(2):
                            h = p_ * 2 + hh
                            mm(psM, psM[:, p_ * 128:p_ * 128 + 65], KD2[:, h * 128:(h + 1) * 128], vaug[:, h * 65:(h + 1) * 65],
                               [KD2, vaug], start=(hh == 0), stop=(hh == 1))
                    for p_ in range(2):
                        stt('dve', M32[p_][:, :], M32[p_][:, :], EFG[:, p_ * 128 + 127:p_ * 128 + 128], psM[:, p_ * 128:p_ * 128 + 65],
                            ALU.mult, ALU.add, [M32[p_], EFG, psM], [M32[p_]])
                        cp('act', Mb[p_][:, :], M32[p_][:, :], [M32[p_]], [Mb[p_]])
                    yield
                    po3 = pso[:, 0:260].rearrange("p (h v) -> p h v", h=4)
                    act(g[:, 20:24], po3[:, :, 64], AF.Abs, [pso], [g])
                    ts('dve', g[:, 20:24], g[:, 20:24], 1.0, None, ALU.max, ALU.bypass, [g], [g])
                    kb.op('dve', lambda e, g=g: e.reciprocal(out=g[:, 20:24], in_=g[:, 20:24]), [g], [g])
                    hv = hv_r.next()
                    hv3 = hv[:, :].rearrange("p (h v) -> p h v", h=4)
                    tt('dve', hv3, po3[:, :, 0:64], bc3(g[:, 20:24], [128, 4, 64], 2), ALU.mult, [pso, g], [hv])
                    kb.op('dve', lambda e, g=g, hv3=hv3: e.tensor_reduce(out=g[:, 24:28], in_=hv3, axis=AX.X, op=ALU.add), [hv], [g])
                    ts('dve', g[:, 24:28], g[:, 24:28], 1.0 / 64, None, ALU.mult, ALU.bypass, [g], [g])
                    tt('dve', hv3, hv3, bc3(g[:, 24:28], [128, 4, 64], 2), ALU.subtract, [hv, g], [hv])
                    sq = sqc_r.next()
                    act(sq[:, :], hv[:, :], AF.Square, [hv], [sq])
                    kb.op('dve', lambda e, g=g, sq=sq: e.tensor_reduce(
                        out=g[:, 28:32], in_=sq[:, :].rearrange("p (h v) -> p h v", h=4), axis=AX.X, op=ALU.add), [sq], [g])
                    rsqrt(g, g[:, 28:32], g[:, 28:32], 1.0 / 64, 1e-5)
                    so = so_r.next()
                    act(so[:, :], tokc[:, 256:512], AF.Sigmoid, [tokc], [so])
                    tt('pool', so[:, :], so[:, :], ngc[:, :], ALU.mult, [so, ngc], [so])
                    tt('dve', hv3, hv3, bc3(g[:, 28:32], [128, 4, 64], 2), ALU.mult, [hv, g], [hv])
                    yc = yc_r.next()
                    tt('pool', yc[:, :], hv[:, :], so[:, :], ALU.mult, [hv, so], [yc])
                    kb.dma('sp', Y, Y[t0:t0 + 128, 512:768], yc, yc[:, :])


            if only is None or any(t_ in only for t_ in 'ABCD'):
                for b_ in xT:
                    kb.retire(b_)
                top_ptr = kb.aptr
                kb.aptr = xT_base
            if only is None or any(t_ in only for t_ in 'ABC'):
                mk = kb.mark()
                gens = [(g_(), {'banks': bk_, 'i': 0}) for n_, g_, bk_ in
                        (('A', genA, [0, 1, 2, 3]), ('B', genB, [4, 5]), ('C', genC, [6, 7])) if only is None or n_ in only]
                while gens:
                    for ge_ in list(gens):
                        kb.pctx = ge_[1]
                        try:
                            next(ge_[0])
                        except StopIteration:
                            gens.remove(ge_)
                kb.pctx = None
                kb.release(mk)

            if only is None or 'D' in only:
                mk = kb.mark()
                kb.NROT = 6
                cpl = kb.sbuf("cplD", [128, NCOL], F32, dma=True)
                kb.dma('sp', cpl, cpl[:, :], colp, colp[l, :, :])
                oqn, okvn = CP['mla_q_norm_g'][0], CP['mla_kv_norm_g'][0]
                SC = 96.0 ** -0.5
                wq32 = kb.sbuf("wq32", [128, 768], F32, dma=True)
                kb.dma('sp', wq32, wq32[:, :].rearrange("p (k c) -> p k c", k=2), w_uq,
                       w_uq[l, :, :].rearrange("(k p) c -> p k c", p=128))
                Wq = kb.sbuf("Wq", [128, 768], BF16)
                Wqs = kb.sbuf("Wqs", [128, 768], BF16)
                for kc in range(2):
                    ts('dve', Wq[:, kc * 384:(kc + 1) * 384], wq32[:, kc * 384:(kc + 1) * 384], cpl[:, oqn + kc:oqn + kc + 1], SC,
                       ALU.mult, ALU.mult, [wq32, cpl], [Wq])
                cp('pool', Wqs[:, :], Wq[:, :], [Wq], [Wqs])
                w6 = Wq[:, :].rearrange("p (g c) -> p g c", c=96)
                ws6 = Wqs[:, :].rearrange("p (g c) -> p g c", c=96)
                cp('pool', ws6[:, :, 64:80], w6[:, :, 80:96], [Wq], [Wqs])
                cp('pool', ws6[:, :, 80:96], w6[:, :, 64:80], [Wq], [Wqs])
                wkv32 = kb.sbuf("wkv32", [128, 512], F32, dma=True)
                kb.dma('sp', wkv32, wkv32[:, :], w_ukv, w_ukv[l, :, :])
                Wkv = kb.sbuf("Wkv", [128, 512], BF16)
                ts('dve', Wkv[:, :], wkv32[:, :], cpl[:, okvn:okvn + 1], None, ALU.mult, ALU.bypass, [wkv32, cpl], [Wkv])
                QT = [kb.sbuf("QT%d" % h, [128, S], BF16) for h in range(4)]
                KT = [kb.sbuf("KT%d" % h, [128, S], BF16) for h in range(4)]
                Vaug = kb.sbuf("Vaug", [128, 32 * 260], BF16)
                memset('pool', Vaug, Vaug[:, :], 1.0)
                TW = 256
                mk2 = kb.mark()
                cq_r = Ring(kb, "cq32", 2, [128, 2 * TW], F32, dma=True)
                ckv_r = Ring(kb, "ckv32", 2, [128, TW], F32, dma=True)
                kr_r = Ring(kb, "kr", 2, [128, 2 * TW], F32, dma=True)
                cs2_r = Ring(kb, "cs2", 2, [128, 2 * TW], F32, dma=True)
                cqb_r = Ring(kb, "cqb", 2, [128, 2 * TW], BF16)
                ckvb_r = Ring(kb, "ckvb", 2, [128, TW], BF16)
                sqq_r = Ring(kb, "sqq", 2, [128, 2 * TW], F32)
                sqkv_r = Ring(kb, "sqkv", 2, [128, TW], F32)
                rq_r = Ring(kb, "rq", 2, [128, TW], F32)
                rkv_r = Ring(kb, "rkv", 2, [128, TW], F32)
                krot_r = Ring(kb, "krot", 2, [128, 2 * TW], F32)
                t12_r = Ring(kb, "t12", 2, [128, 2 * TW], F32)
                rc_r = Ring(kb, "rc", 4, [128, 8], F32)
                ocq, ockv, okr = FMOFF['Dcq'], FMOFF['Dckv'], FMOFF['Dkr']
                ONES = cst[:, CI_ONES:CI_ONES + 128]
                for tc in range(S // TW):
                    t0 = tc * TW
                    cq = cq_r.next()
                    kb.dma('sp', cq, cq[:, :].rearrange("p (k t) -> p k t", k=2), UF,
                           UF[ocq:ocq + 256, t0:t0 + TW].rearrange("(k p) t -> p k t", p=128))
                    ckv = ckv_r.next()
                    kb.dma('sp', ckv, ckv[:, :], UF, UF[ockv:ockv + 128, t0:t0 + TW])
                    kr = kr_r.next()
                    kb.dma('sp', kr, kr[64:96, 0:TW], UF, UF[okr:okr + 32, t0:t0 + TW])
                    kb.dma('sp', kr, kr[64:80, TW:2 * TW], UF, UF[okr + 16:okr + 32, t0:t0 + TW])
                    kb.dma('sp', kr, kr[80:96, TW:2 * TW], UF, UF[okr:okr + 16, t0:t0 + TW])
                    cs2 = cs2_r.next()
                    kb.dma('sp', cs2, cs2[64:96, 0:TW], COS2, COS2[64:96, t0:t0 + TW])
                    kb.dma('sp', cs2, cs2[64:96, TW:2 * TW], SIN2, SIN2[64:96, t0:t0 + TW])
                    cqb = cqb_r.next()
                    cp('pool', cqb[:, :], cq[:, :], [cq], [cqb])
                    ckvb = ckvb_r.next()
                    cp('pool', ckvb[:, :], ckv[:, :], [ckv], [ckvb])
                    sqq = sqq_r.next()
                    act(sqq[:, :], cq[:, :], AF.Square, [cq], [sqq])
                    sqkv = sqkv_r.next()
                    act(sqkv[:, :], ckv[:, :], AF.Square, [ckv], [sqkv])
                    psr = kb.psum()
                    for kc in range(2):
                        mm(psr, psr[0:96, 0:TW], ONES[:, 0:96], sqq[:, kc * TW:(kc + 1) * TW], [cst, sqq], start=(kc == 0), stop=(kc == 1))
                    rq = rq_r.next()
                    act(rq[0:96, 0:TW], psr[0:96, 0:TW], AF.Sqrt, [psr], [rq], bias=1e-6, scale=1.0 / 256)
                    kb.op('dve', lambda e, rq=rq: e.reciprocal(out=rq[0:96, 0:TW], in_=rq[0:96, 0:TW]), [rq], [rq])
                    psr2 = kb.psum()
                    mm(psr2, psr2[0:64, 0:TW], ONES[:, 0:64], sqkv[:, :], [cst, sqkv])
                    rkv = rkv_r.next()
                    act(rkv[0:64, 0:TW], psr2[0:64, 0:TW], AF.Sqrt, [psr2], [rkv], bias=1e-6, scale=1.0 / 128)
                    kb.op('dve', lambda e, rkv=rkv: e.reciprocal(out=rkv[0:64, 0:TW], in_=rkv[0:64, 0:TW]), [rkv], [rkv])
                    krot = krot_r.next()
                    tt('dve', krot[64:96, 0:2 * TW], kr[64:96, 0:2 * TW], cs2[64:96, 0:2 * TW], ALU.mult, [kr, cs2], [krot])
                    tt('dve', krot[64:96, 0:TW], krot[64:96, 0:TW], krot[64:96, TW:2 * TW], ALU.add, [krot], [krot])
                    for h in range(4):
                        psq = kb.psum()
                        psqs = kb.psum()
                        for kc in range(2):
                            g_ = kc * 4 + h
                            mm(psq, psq[0:96, 0:TW], w6[:, g_, :], cqb[:, kc * TW:(kc + 1) * TW], [Wq, cqb], start=(kc == 0), stop=(kc == 1))
                        for kc in range(2):
                            g_ = kc * 4 + h
                            mm(psqs, psqs[0:96, 0:TW], ws6[:, g_, :], cqb[:, kc * TW:(kc + 1) * TW], [Wqs, cqb], start=(kc == 0), stop=(kc == 1))
                        tt('dve', QT[h][0:64, t0:t0 + TW], psq[0:64, 0:TW], rq[0:64, 0:TW], ALU.mult, [psq, rq], [QT[h]])
                        t12 = t12_r.next()
                        tt('dve', t12[64:96, 0:TW], psq[64:96, 0:TW], cs2[64:96, 0:TW], ALU.mult, [psq, cs2], [t12])
                        tt('dve', t12[64:96, TW:2 * TW], psqs[64:96, 0:TW], cs2[64:96, TW:2 * TW], ALU.mult, [psqs, cs2], [t12])
                        tt('pool', t12[64:96, 0:TW], t12[64:96, 0:TW], t12[64:96, TW:2 * TW], ALU.add, [t12], [t12])
                        tt('pool', QT[h][64:96, t0:t0 + TW], t12[64:96, 0:TW], rq[64:96, 0:TW], ALU.mult, [t12, rq], [QT[h]])
                        psk = kb.psum()
                        mm(psk, psk[0:64, 0:TW], Wkv[:, h * 128:h * 128 + 64], ckvb[:, :], [Wkv, ckvb])
                        tt('dve', KT[h][0:64, t0:t0 + TW], psk[0:64, 0:TW], rkv[0:64, 0:TW], ALU.mult, [psk, rkv], [KT[h]])
                        cp('act', KT[h][64:96, t0:t0 + TW], krot[64:96, 0:TW], [krot], [KT[h]])
                    for j in range(TW // 128):
                        tt_ = tc * (TW // 128) + j
                        psv = kb.psum()
                        mm(psv, psv[:, :], ckvb[:, j * 128:(j + 1) * 128], Wkv[:, :], [ckvb, Wkv])
                        psc = kb.psum()
                        mm(psc, psc[:, 0:1], sqkv[:, j * 128:(j + 1) * 128], ONES[:, 0:1], [sqkv, cst])
                        rc = rc_r.next()
                        act(rc[:, 0:1], psc[:, 0:1], AF.Sqrt, [psc], [rc], bias=1e-6, scale=1.0 / 128)
                        kb.op('dve', lambda e, rc=rc: e.reciprocal(out=rc[:, 0:1], in_=rc[:, 0:1]), [rc], [rc])
                        ts('dve', Vaug[:, tt_ * 260:(tt_ + 1) * 260].rearrange("p (h c) -> p h c", h=4)[:, :, 0:64],
                           psv[:, :].rearrange("p (h c) -> p h c", h=4)[:, :, 64:128], rc[:, 0:1], None, ALU.mult, ALU.bypass,
                           [psv, rc], [Vaug])
                kb.release(mk2)
                mx_r = Ring(kb, "mx", 6, [128, 16], F32)
                nrow_r = Ring(kb, "nrow", 6, [1, 128], BF16)
                PT_r = Ring(kb, "PT", 4, [128, 512], BF16)
                yd_r = Ring(kb, "yd", 2, [128, 256], F32, dma=True)
                onesb = cstb[0:1, CI_ONES:CI_ONES + 128]
                TRIb = cstb[:, CI_TRI:CI_TRI + 128]
                its = [(i, h) for i in range(32) for h in range(4)]
                st_ = {}

                def part1(n_):
                    i, h = its[n_]
                    q_ap = QT[h][0:96, i * 128:(i + 1) * 128]
                    nk = (i + 1) * 128
                    mx = mx_r.next()
                    ngr = (i + 4) // 4
                    for g_ in range(ngr):
                        k0 = g_ * 512
                        n = min(512, nk - k0)
                        ps = kb.psum()
                        mm(ps, ps[:, 0:n], q_ap, KT[h][0:96, k0:k0 + n], [QT[h], KT[h]])
                        kb.op('dve', lambda e, mx=mx, ps=ps, n=n, g_=g_: e.reduce_max(
                            out=mx[:, g_:g_ + 1], in_=ps[:, 0:n], axis=AX.X), [ps], [mx])
                    kb.op('dve', lambda e, mx=mx, ngr=ngr: e.reduce_max(out=mx[:, 8:9], in_=mx[:, 0:ngr], axis=AX.X), [mx], [mx])
                    ts('dve', mx[:, 9:10], mx[:, 8:9], -1.0, None, ALU.mult, ALU.bypass, [mx], [mx])
                    pst = kb.psum()
                    kb.op('pe', lambda e, pst=pst, mx=mx: e.transpose(out=pst[0:1, 0:128], in_=mx[:, 9:10], identity=ident), [mx, cst], [pst])
                    nrow = nrow_r.next()
                    cp('act', nrow[0:1, :], pst[0:1, 0:128], [pst], [nrow])
                    st_[n_] = (mx, nrow, ngr, q_ap)

                def part2(n_):
                    i, h = its[n_]
                    mx, nrow, ngr, q_ap = st_.pop(n_)
                    if h == 0:
                        st_['yd'] = yd_r.next()
                    yd = st_['yd']
                    po = kb.psum_banks[6 + (n_ % 2)]
                    for g_ in range(ngr):
                        kts = list(range(g_ * 4, min(g_ * 4 + 4, i + 1)))
                        ps = kb.psum()
                        for a_, kt in enumerate(kts):
                            mm(ps, ps[:, a_ * 128:(a_ + 1) * 128], KT[h][0:96, kt * 128:(kt + 1) * 128], q_ap, [KT[h], QT[h]],
                               start=True, stop=False)
                            mm(ps, ps[:, a_ * 128:(a_ + 1) * 128], onesb, nrow[0:1, :], [cstb, nrow], start=False, stop=True)
                        PT = PT_r.next()
                        nn = len(kts) * 128
                        act(PT[:, 0:nn], ps[:, 0:nn], AF.Exp, [ps], [PT])
                        if kts[-1] == i:
                            a_ = len(kts) - 1
                            tt('pool', PT[:, a_ * 128:(a_ + 1) * 128], PT[:, a_ * 128:(a_ + 1) * 128], TRIb, ALU.mult, [PT, cstb], [PT])
                        for a_, kt in enumerate(kts):
                            mm(po, po[:, 0:65], PT[:, a_ * 128:(a_ + 1) * 128], Vaug[:, kt * 260 + h * 65:kt * 260 + (h + 1) * 65],
                               [PT, Vaug], start=(kt == 0), stop=(kt == i))
                    kb.op('dve', lambda e, mx=mx, po=po: e.reciprocal(out=mx[:, 10:11], in_=po[:, 64:65]), [po], [mx])
                    ts('dve', yd[:, h * 64:(h + 1) * 64], po[:, 0:64], mx[:, 10:11], None, ALU.mult, ALU.bypass, [po, mx], [yd])
                    if h == 3:
                        kb.dma('sp', Y, Y[i * 128:(i + 1) * 128, 768:1024], yd, yd[:, :])

                part1(0)
                part1(1)
                for n_ in range(len(its)):
                    if n_ + 2 < len(its):
                        part1(n_ + 2)
                    part2(n_)
                kb.NROT = 8
                kb.release(mk)

            if only is None or any(t_ in only for t_ in 'ABCD'):
                kb.aptr = xT_base
                for h_ in range(2):
                    xT[h_] = kb.sbuf("xT%d_l%d" % (h_, l), [128, 8 * 2048], BF16)
                assert kb.aptr == top_ptr, (kb.aptr, top_ptr)
            if stop == 'MIX':
                break

            Xcur = x_in if l == 0 else X2
            if only is None or 'WO' in only:
                mk = kb.mark()
                Wo = kb.sbuf("Wo", [128, 8 * 1024], BF16, dma='sw')
                kb.dma('pool', Wo, Wo[:, :].rearrange("p (k d) -> p k d", k=8), w_out, w_out[l, :, :].rearrange("(k p) d -> p k d", p=128))
                gb1 = kb.sbuf("gb1", [128, 2048], F32, dma=True)
                kb.dma('sp', gb1, gb1[:, 0:1024], rowp, row_bc_ap(l, 'ln1_g'))
                kb.dma('sp', gb1, gb1[:, 1024:2048], rowp, row_bc_ap(l, 'ln1_b'))
                Wr = kb.sbuf("Wr", [128, 8 * 36], F32, dma=True)
                wr3 = Wr[:, :].rearrange("p (k e) -> p k e", k=8)
                kb.dma('sp', Wr, wr3[:, :, 0:4], w_rg, w_rg[l, :, :].rearrange("(k p) e -> p k e", p=128))
                kb.dma('sp', Wr, wr3[:, :, 4:36], w_re, w_re[l, :, :].rearrange("(k p) e -> p k e", p=128))
                rb = kb.sbuf("rb", [128, 36], F32, dma=True)
                kb.dma('sp', rb, rb[:, 0:4], rowp, row_bc_ap(l, 'moe_b_rg'))
                kb.dma('sp', rb, rb[:, 4:36], rowp, row_bc_ap(l, 'moe_b_re'))
                yt_r = Ring(kb, "yt", 3, [128, 1024], F32, dma=True)
                xr_r = Ring(kb, "xr", 3, [128, 1024], F32, dma=True)
                yTb_r = Ring(kb, "yTb", 3, [128, 1024], BF16)
                r_r = Ring(kb, "r1", 3, [128, 1024], F32, dma=True)
                x32_r = Ring(kb, "x32", 3, [128, 1024], F32)
                wk_r = Ring(kb, "wk1", 3, [128, 16], F32)
                rt_r = Ring(kb, "rt", 3, [128, 160], F32, dma=True)
                def wo_tile(tt_):
                    yt = yt_r.next()
                    kb.dma('sp', yt, yt[:, :], Y, Y[tt_ * 128:(tt_ + 1) * 128, :])
                    xr = xr_r.next()
                    kb.dma('sp', xr, xr[:, :], Xcur, Xcur[tt_ * 128:(tt_ + 1) * 128, :])
                    yTb = yTb_r.next()
                    for g in range(2):
                        ps = kb.psum()
                        for j in range(4):
                            kc = g * 4 + j
                            kb.op('pe', lambda e, ps=ps, j=j, kc=kc, yt=yt: e.transpose(
                                out=ps[:, j * 128:(j + 1) * 128], in_=yt[:, kc * 128:(kc + 1) * 128], identity=ident), [yt, cst], [ps])
                        evac(yTb[:, g * 512:(g + 1) * 512], ps[:, :], [ps], [yTb])
                    yield
                    r = r_r.next()
                    for dh in range(2):
                        ps = kb.psum()
                        for kc in range(8):
                            mm(ps, ps[:, :], yTb[:, kc * 128:(kc + 1) * 128], Wo[:, kc * 1024 + dh * 512:kc * 1024 + (dh + 1) * 512],
                               [yTb, Wo], start=(kc == 0), stop=(kc == 7))
                        stt('dve', r[:, dh * 512:(dh + 1) * 512], xr[:, dh * 512:(dh + 1) * 512], ALPHA, ps[:, :], ALU.mult, ALU.add, [xr, ps], [r])
                    yield
                    wk = wk_r.next()
                    layer_norm_tile(r, gb1, 0, 1024, 1e-5, wk)
                    kb.dma('sp', X1, X1[tt_ * 128:(tt_ + 1) * 128, :], r, r[:, :])
                    yield
                    x32 = x32_r.next()
                    transpose_tile(r, tt_, x32)
                    yield
                    psr = kb.psum()
                    for kc in range(8):
                        mm(psr, psr[:, 0:36], x32[:, kc * 128:(kc + 1) * 128], wr3[:, kc, :], [x32, Wr], start=(kc == 0), stop=(kc == 7))
                    rt = rt_r.next()
                    tt('dve', rt[:, 0:36], psr[:, 0:36], rb[:, :], ALU.add, [psr, rb], [rt])
                    kb.op('dve', lambda e, rt=rt: e.reduce_max(out=rt[:, 116:117], in_=rt[:, 0:4], axis=AX.X), [rt], [rt])
                    ts('dve', rt[:, 40:44], rt[:, 0:4], rt[:, 116:117], None, ALU.is_ge, ALU.bypass, [rt], [rt])
                    ts('dve', rt[:, 117:118], rt[:, 116:117], -1.0, None, ALU.mult, ALU.bypass, [rt], [rt])
                    act(rt[:, 36:40], rt[:, 0:4], AF.Exp, [rt], [rt], bias=rt[:, 117:118])
                    kb.op('dve', lambda e, rt=rt: e.reduce_sum(out=rt[:, 118:119], in_=rt[:, 36:40], axis=AX.X), [rt], [rt])
                    cp('dve', rt[:, 44:76].rearrange("p (g e) -> p g e", g=4), bc3(rt[:, 40:44], [128, 4, 8], 2), [rt], [rt])
                    ts('dve', rt[:, 76:108], rt[:, 44:76], -1.0, 1e30, ALU.add, ALU.mult, [rt], [rt])
                    tt('dve', rt[:, 128:160], rt[:, 4:36], rt[:, 44:76], ALU.mult, [rt], [rt])
                    tt('dve', rt[:, 128:160], rt[:, 128:160], rt[:, 76:108], ALU.add, [rt], [rt])
                    kb.op('dve', lambda e, rt=rt: e.max(out=rt[:, 108:116], in_=rt[:, 128:160]), [rt], [rt])
                    yield
                    ts('dve', rt[:, 44:76], rt[:, 128:160], rt[:, 109:110], None, ALU.is_ge, ALU.bypass, [rt], [rt])
                    ts('dve', rt[:, 119:120], rt[:, 108:109], -1.0, None, ALU.mult, ALU.bypass, [rt], [rt])
                    act(rt[:, 76:108], rt[:, 128:160], AF.Exp, [rt], [rt], bias=rt[:, 119:120])
                    act(rt[:, 120:121], rt[:, 109:110], AF.Exp, [rt], [rt], bias=rt[:, 119:120])
                    ts('dve', rt[:, 120:121], rt[:, 120:121], 1.0, None, ALU.add, ALU.bypass, [rt], [rt])
                    tt('dve', rt[:, 120:121], rt[:, 120:121], rt[:, 118:119], ALU.mult, [rt], [rt])
                    kb.op('dve', lambda e, rt=rt: e.reciprocal(out=rt[:, 121:122], in_=rt[:, 120:121]), [rt], [rt])
                    tt('dve', rt[:, 76:108], rt[:, 76:108], rt[:, 44:76], ALU.mult, [rt], [rt])
                    ts('dve', Gall[:, tt_ * 32:(tt_ + 1) * 32], rt[:, 76:108], rt[:, 121:122], None, ALU.mult, ALU.bypass, [rt], [Gall])
                    if stop == 'WO':
                        cp('dve', rt[:, 0:32], Gall[:, tt_ * 32:(tt_ + 1) * 32], [Gall], [rt])
                        kb.dma('sp', GD, GD[tt_ * 128:(tt_ + 1) * 128, :], rt, rt[:, 0:32])

                def wo_stream(s_):
                    for tt_ in range(s_, 32, 3):
                        yield from wo_tile(tt_)
                gens = [(wo_stream(s_), {'banks': bk_, 'i': 0}) for s_, bk_ in ((0, [0, 1, 2]), (1, [3, 4, 5]), (2, [6, 7]))]
                while gens:
                    for ge_ in list(gens):
                        kb.pctx = ge_[1]
                        try:
                            next(ge_[0])
                        except StopIteration:
                            gens.remove(ge_)
                kb.pctx = None
                kb.release(mk)
            if stop == 'WO':
                break

            last = (l == n_layers - 1)
            if only is None or 'FF' in only:
                mk = kb.mark()
                gb2 = kb.sbuf("gb2", [128, 3072], F32, dma=True)
                kb.dma('sp', gb2, gb2[:, 0:1024], rowp, row_bc_ap(l, 'ln2_g'))
                kb.dma('sp', gb2, gb2[:, 1024:2048], rowp, row_bc_ap(l, 'ln2_b'))
                kb.dma('sp', gb2, gb2[:, 2048:3072], rowp, row_bc_ap(l, 'ple_b_gate'))
                acc = kb.sbuf("acc", [128, 8 * 1024], F32)
                Wg_r = Ring(kb, "Wg", 2, [128, 8 * 512], BF16, dma='sw')
                Wu_r = Ring(kb, "Wu", 2, [128, 8 * 512], BF16, dma='sw')
                Wd_r = Ring(kb, "Wd", 2, [128, 4 * 1024], BF16, dma='sw')
                hT_r = Ring(kb, "hT", 2, [128, 4 * 512], BF16)
                sgm_r = Ring(kb, "sgm", 3, [128, 512], F32)
                x1_r = Ring(kb, "x1t", 2, [128, 1024], F32, dma=True)
                pt_r = Ring(kb, "pt", 2, [128, 256], F32, dma=True)
                pTb_r = Ring(kb, "pTb", 2, [128, 256], BF16)
                r2_r = Ring(kb, "r2", 2, [128, 1024], F32, dma=True)
                wk_r = Ring(kb, "wk2", 2, [128, 16], F32)
                for qt in range(4):
                    hf, qo = qt // 2, (qt % 2) * 1024
                    memset('pool', acc, acc[:, :], 0.0)
                    for e_ in range(32):
                        Wg, Wu, Wd = Wg_r.next(), Wu_r.next(), Wd_r.next()
                        kb.dma('pool', Wg, Wg[:, :].rearrange("p (k f) -> p k f", k=8), w_gate,
                               w_gate[l, e_, :, :].rearrange("(k p) f -> p k f", p=128))
                        kb.dma('pool', Wu, Wu[:, :].rearrange("p (k f) -> p k f", k=8), w_up,
                               w_up[l, e_, :, :].rearrange("(k p) f -> p k f", p=128))
                        kb.dma('pool', Wd, Wd[:, :].rearrange("p (k d) -> p k d", k=4), w_down,
                               w_down[l, e_, :, :].rearrange("(k p) d -> p k d", p=128))
                        for tcq in range(2):
                            to = qo + tcq * 512
                            hT = hT_r.next()
                            for fc in range(4):
                                psg = kb.psum()
                                for kc in range(8):
                                    mm(psg, psg[:, :], Wg[:, kc * 512 + fc * 128:kc * 512 + (fc + 1) * 128],
                                       xT[hf][:, kc * 2048 + to:kc * 2048 + to + 512], [Wg, xT[hf]], start=(kc == 0), stop=(kc == 7))
                                psu = kb.psum()
                                for kc in range(8):
                                    mm(psu, psu[:, :], Wu[:, kc * 512 + fc * 128:kc * 512 + (fc + 1) * 128],
                                       xT[hf][:, kc * 2048 + to:kc * 2048 + to + 512], [Wu, xT[hf]], start=(kc == 0), stop=(kc == 7))
                                sgm = sgm_r.next()
                                act(sgm[:, :], psg[:, :], AF.Silu, [psg], [sgm])
                                tt('dve', hT[:, fc * 512:(fc + 1) * 512], sgm[:, :], psu[:, :], ALU.mult, [sgm, psu], [hT])
                            for tl in range(4):
                                ti = tcq * 4 + tl
                                tt_ = qt * 8 + ti
                                for dh in range(2):
                                    psd = kb.psum()
                                    for fc in range(4):
                                        mm(psd, psd[:, :], hT[:, fc * 512 + tl * 128:fc * 512 + (tl + 1) * 128],
                                           Wd[:, fc * 1024 + dh * 512:fc * 1024 + (dh + 1) * 512], [hT, Wd], start=(fc == 0), stop=(fc == 3))
                                    a_ = acc[:, ti * 1024 + dh * 512:ti * 1024 + (dh + 1) * 512]
                                    stt('dve', a_, psd[:, :], Gall[:, tt_ * 32 + e_:tt_ * 32 + e_ + 1], a_, ALU.mult, ALU.add, [psd, Gall, acc], [acc])
                    Wpg, Wpg2, Wp = Wg_r.next(), Wu_r.next(), Wd_r.next()
                    kb.dma('pool', Wpg, Wpg[:, :].rearrange("p (k d) -> p k d", k=4), ple_wg,
                           ple_wg[l, 0:512, :].rearrange("(k p) d -> p k d", p=128))
                    kb.dma('pool', Wpg2, Wpg2[:, :].rearrange("p (k d) -> p k d", k=4), ple_wg,
                           ple_wg[l, 512:1024, :].rearrange("(k p) d -> p k d", p=128))
                    kb.dma('pool', Wp, Wp[:, 0:2048].rearrange("p (k d) -> p k d", k=2), ple_w,
                           ple_w[l, :, :].rearrange("(k p) d -> p k d", p=128))
                    def ff_tile(ti, qt=qt, hf=hf, qo=qo, Wpg=Wpg, Wpg2=Wpg2, Wp=Wp):
                        tt_ = qt * 8 + ti
                        tl = qo + ti * 128
                        x1t = x1_r.next()
                        kb.dma('sp', x1t, x1t[:, :], X1, X1[tt_ * 128:(tt_ + 1) * 128, :])
                        pt = pt_r.next()
                        kb.dma('sp', pt, pt[:, :], p_in, p_in[l, tt_ * 128:(tt_ + 1) * 128, :])
                        pst = kb.psum()
                        for j in range(2):
                            kb.op('pe', lambda e, pst=pst, j=j, pt=pt: e.transpose(
                                out=pst[:, j * 128:(j + 1) * 128], in_=pt[:, j * 128:(j + 1) * 128], identity=ident), [pt, cst], [pst])
                        pTb = pTb_r.next()
                        cp('act', pTb[:, :], pst[:, 0:256], [pst], [pTb])
                        yield
                        r2 = r2_r.next()
                        for dh in range(2):
                            psq = kb.psum()
                            for kc in range(8):
                                W_ = Wpg if kc < 4 else Wpg2
                                k4 = kc % 4
                                mm(psq, psq[:, :], xT[hf][:, kc * 2048 + tl:kc * 2048 + tl + 128],
                                   W_[:, k4 * 1024 + dh * 512:k4 * 1024 + (dh + 1) * 512], [xT[hf], W_], start=(kc == 0), stop=(kc == 7))
                            psp = kb.psum()
                            for kc in range(2):
                                mm(psp, psp[:, :], pTb[:, kc * 128:(kc + 1) * 128], Wp[:, kc * 1024 + dh * 512:kc * 1024 + (dh + 1) * 512],
                                   [pTb, Wp], start=(kc == 0), stop=(kc == 1))
                            sgm = sgm_r.next()
                            tt('dve', sgm[:, :], psq[:, :], gb2[:, 2048 + dh * 512:2048 + (dh + 1) * 512], ALU.add, [psq, gb2], [sgm])
                            act(sgm[:, :], sgm[:, :], AF.Sigmoid, [sgm], [sgm])
                            tt('dve', sgm[:, :], sgm[:, :], psp[:, :], ALU.mult, [sgm, psp], [sgm])
                            a_ = acc[:, ti * 1024 + dh * 512:ti * 1024 + (dh + 1) * 512]
                            stt('dve', r2[:, dh * 512:(dh + 1) * 512], x1t[:, dh * 512:(dh + 1) * 512], ALPHA, a_, ALU.mult, ALU.add, [x1t, acc], [r2])
                            tt('pool', r2[:, dh * 512:(dh + 1) * 512], r2[:, dh * 512:(dh + 1) * 512], sgm[:, :], ALU.add, [r2, sgm], [r2])
                        yield
                        wk = wk_r.next()
                        layer_norm_tile(r2, gb2, 0, 1024, 1e-5, wk)
                        dst = OUT if last else X2
                        kb.dma('sp', dst, dst[tt_ * 128:(tt_ + 1) * 128, :], r2, r2[:, :])
                        yield
                        if not last:
                            transpose_tile(r2, tt_)

                    def ff_stream(s_):
                        for ti in range(s_, 8, 2):
                            yield from ff_tile(ti)
                    gens = [(ff_stream(s_), {'banks': bk_, 'i': 0}) for s_, bk_ in ((0, [0, 1, 2, 3]), (1, [4, 5, 6, 7]))]
                    while gens:
                        for ge_ in list(gens):
                            kb.pctx = ge_[1]
                            try:
                                next(ge_[0])
                            except StopIteration:
                                gens.remove(ge_)
                    kb.pctx = None
                kb.release(mk)
            if stop == 'FF':
                break

        outs = {'P1': [UF, UT], 'MIX': [Y], 'WO': [X1, GD]}.get(stop, [OUT])
        kb.wait_all_writes('sp', outs)
        stats = kb.emit()
        print("instr stats", stats)
    return nc


_NC_CACHE = {}


def kernel(**inputs):
    inp = {k: np.asarray(v) for k, v in inputs.items()}
    n = 8
    if 'nc' not in _NC_CACHE:
        _NC_CACHE['nc'] = build(n_layers=L)
    nc = _NC_CACHE['nc']
    rowp, colp = pack_params(inp)
    shared = {'consts': make_consts(), 'rowp': rowp, 'colp': colp}
    for nm in IN_NAMES:
        shared[nm] = np.ascontiguousarray(inp[nm], dtype=np.float32)
    in_maps = [core_inputs(inp, b, shared) for b in range(n)]
    res = run_bass_kernel_spmd(nc, in_maps, core_ids=list(range(n)))
    out = np.stack([np.asarray(res.results[b]['out'], dtype=np.float32) for b in range(n)], axis=0)
    return out
```
